# Optimizing a Trainium2 kernel written in Bass

```python
import jax, jax.numpy as jnp
from jax import lax
import numpy as np

D_MODEL = 1024
BATCH = 4
SEQ = 8192
DEPTH = 1

PLE_DIM = 256
EPS = 1e-6
ROPE_THETA = 10000.0
D_FF = 2816

NSA_HEADS = 8
NSA_KV_HEADS = 2
HEAD_DIM = 64
HPG = NSA_HEADS // NSA_KV_HEADS
NSA_WIDTH = NSA_HEADS * HEAD_DIM
KV_WIDTH = NSA_KV_HEADS * HEAD_DIM
CMP_LEN = 32
CMP_STRIDE = 16
CMP_HIDDEN = 256
SEL_LEN = 64
N_SEL = 16
N_LOCAL = 2
WINDOW = 512
Q_BLOCK = 128
NEG = -1e30
FORCED_SCORE = 1e4
INVALID_SCORE = -1e4

CONV_WIDTH = 512
CONV_K = 3

SPLIT_SIZES = (NSA_WIDTH, KV_WIDTH, KV_WIDTH, KV_WIDTH, KV_WIDTH, KV_WIDTH, KV_WIDTH,
               3 * NSA_HEADS, CONV_WIDTH, CONV_WIDTH, CONV_WIDTH, D_MODEL, D_MODEL)
IN_COLS = 4888

kernel_name = "hybrid_nsa_shortconv_macaron_block"


def rms_norm(x, g):
    xf = x.astype(jnp.float32)
    y = xf * lax.rsqrt(jnp.mean(xf * xf, axis=-1, keepdims=True) + EPS)
    return (y * g.astype(jnp.float32)).astype(x.dtype)


def swiglu(x, w_gate, w_up, w_down):
    return (jax.nn.silu(x @ w_gate) * (x @ w_up)) @ w_down


def rope(x, positions):
    half = HEAD_DIM // 2
    inv_freq = ROPE_THETA ** (-jnp.arange(half, dtype=jnp.float32) / half)
    ang = positions.astype(jnp.float32)[..., None] * inv_freq
    cos = jnp.cos(ang)[:, :, None, :]
    sin = jnp.sin(ang)[:, :, None, :]
    x1 = x[..., :half].astype(jnp.float32)
    x2 = x[..., half:].astype(jnp.float32)
    out = jnp.concatenate([x1 * cos - x2 * sin, x2 * cos + x1 * sin], axis=-1)
    return out.astype(x.dtype)


def compress(kv, pos_emb, w1, b1, w2):
    B, S, G, dh = kv.shape
    nc = (S - CMP_LEN) // CMP_STRIDE + 1
    idx = jnp.arange(nc)[:, None] * CMP_STRIDE + jnp.arange(CMP_LEN)[None, :]
    blocks = kv[:, idx] + pos_emb[None, None, :, None, :]
    flat = blocks.transpose(0, 1, 3, 2, 4).reshape(B, nc, G, CMP_LEN * dh)
    return jax.nn.gelu(flat @ w1 + b1) @ w2


def nsa_attention(q, kc, vc, ks, vs, kw, vw, gates):
    B, S = q.shape[0], q.shape[1]
    G = NSA_KV_HEADS
    nc = kc.shape[1]
    ns = S // SEL_LEN
    n_sel = min(N_SEL, ns)
    nqb = S // Q_BLOCK
    scale = HEAD_DIM ** -0.5
    f32 = jnp.float32

    ks_blk = ks.reshape(B, ns, SEL_LEN, G, HEAD_DIM).transpose(0, 3, 1, 2, 4)
    vs_blk = vs.reshape(B, ns, SEL_LEN, G, HEAD_DIM).transpose(0, 3, 1, 2, 4)
    pad = ((0, 0), (WINDOW, 0), (0, 0), (0, 0))
    kw_pad = jnp.pad(kw, pad)
    vw_pad = jnp.pad(vw, pad)

    c_start = jnp.arange(nc) * CMP_STRIDE
    c_last = c_start + CMP_LEN - 1
    s_start = jnp.arange(ns) * SEL_LEN
    overlap = jnp.clip(jnp.minimum(c_start[:, None] + CMP_LEN, s_start[None, :] + SEL_LEN)
                       - jnp.maximum(c_start[:, None], s_start[None, :]), 0, None).astype(f32) / CMP_LEN
    s_idx = jnp.arange(ns)
    sel_off = jnp.arange(SEL_LEN)
    win_off = jnp.arange(Q_BLOCK + WINDOW)

    def block(args):
        qb, gb, qs = args
        tq = qs + jnp.arange(Q_BLOCK)

        s_c = jnp.einsum('bqghd,bcgd->bghqc', qb, kc, preferred_element_type=f32) * scale
        c_mask = c_last[None, :] <= tq[:, None]
        p_c = jax.nn.softmax(jnp.where(c_mask, s_c, NEG), axis=-1)
        p_c = p_c * jnp.any(c_mask, axis=-1)[:, None].astype(f32)
        o_c = jnp.einsum('bghqc,bcgd->bqghd', p_c.astype(vc.dtype), vc)

        imp = jnp.einsum('bghqc,cs->bgqs', p_c, overlap)
        blk_q = tq // SEL_LEN
        valid = s_idx[None, :] <= blk_q[:, None]
        forced = (s_idx[None, :] == 0) | (valid & (s_idx[None, :] > blk_q[:, None] - N_LOCAL))
        score = jnp.where(forced, FORCED_SCORE, jnp.where(valid, imp, INVALID_SCORE))
        _, sel = lax.top_k(score, n_sel)

        gather = jax.vmap(jax.vmap(lambda kb, ix: kb[ix]))
        gk = gather(ks_blk, sel)
        gv = gather(vs_blk, sel)
        tok = sel[..., None] * SEL_LEN + sel_off
        t_mask = (tok <= tq[None, None, :, None, None])[:, :, None]
        s_s = jnp.einsum('bqghd,bgqnld->bghqnl', qb, gk, preferred_element_type=f32) * scale
        s_s = jnp.where(t_mask, s_s, NEG)
        sh = s_s.shape
        p_s = jax.nn.softmax(s_s.reshape(sh[:4] + (sh[4] * sh[5],)), axis=-1).reshape(sh)
        o_s = jnp.einsum('bghqnl,bgqnld->bqghd', p_s.astype(gv.dtype), gv)

        kwb = lax.dynamic_slice_in_dim(kw_pad, qs, Q_BLOCK + WINDOW, axis=1)
        vwb = lax.dynamic_slice_in_dim(vw_pad, qs, Q_BLOCK + WINDOW, axis=1)
        kpos = qs - WINDOW + win_off
        w_mask = ((kpos[None, :] <= tq[:, None]) & (kpos[None, :] > tq[:, None] - WINDOW)
                  & (kpos[None, :] >= 0))
        s_w = jnp.einsum('bqghd,bkgd->bghqk', qb, kwb, preferred_element_type=f32) * scale
        p_w = jax.nn.softmax(jnp.where(w_mask, s_w, NEG), axis=-1)
        o_w = jnp.einsum('bghqk,bkgd->bqghd', p_w.astype(vwb.dtype), vwb)

        return gb[..., 0:1] * o_c + gb[..., 1:2] * o_s + gb[..., 2:3] * o_w

    def to_blocks(t):
        return jnp.moveaxis(t.reshape((B, nqb, Q_BLOCK) + t.shape[2:]), 1, 0)

    out = lax.map(block, (to_blocks(q), to_blocks(gates), jnp.arange(nqb) * Q_BLOCK))
    return jnp.moveaxis(out, 0, 1).reshape(B, S, NSA_WIDTH)


def short_conv(xb, xc, xin, w_conv, b_conv):
    u = xc * xin
    y = lax.conv_general_dilated(u, w_conv[:, None, :].astype(u.dtype), window_strides=(1,),
                                 padding=[(CONV_K - 1, 0)],
                                 dimension_numbers=('NWC', 'WIO', 'NWC'),
                                 feature_group_count=CONV_WIDTH)
    return xb * (y + b_conv)


def token_mixing(u, positions, w_in, cmp_k_pos, cmp_k_w1, cmp_k_b1, cmp_k_w2,
                 cmp_v_pos, cmp_v_w1, cmp_v_b1, cmp_v_w2, conv_w, conv_b,
                 w_proj_nsa, w_proj_conv, w_out):
    B, S, _ = u.shape
    G = NSA_KV_HEADS
    points = []
    acc = 0
    for sz in SPLIT_SIZES[:-1]:
        acc += sz
        points.append(acc)
    z = u @ w_in
    (q, kc_raw, vc_raw, ks, vs, kw, vw, g_nsa, cb, cc, cx, g_a, g_b) = jnp.split(z, points, axis=-1)

    q = rope(q.reshape(B, S, NSA_HEADS, HEAD_DIM), positions).reshape(B, S, G, HPG, HEAD_DIM)
    kc = compress(rope(kc_raw.reshape(B, S, G, HEAD_DIM), positions), cmp_k_pos, cmp_k_w1, cmp_k_b1, cmp_k_w2)
    vc = compress(vc_raw.reshape(B, S, G, HEAD_DIM), cmp_v_pos, cmp_v_w1, cmp_v_b1, cmp_v_w2)
    ks = rope(ks.reshape(B, S, G, HEAD_DIM), positions)
    vs = vs.reshape(B, S, G, HEAD_DIM)
    kw = rope(kw.reshape(B, S, G, HEAD_DIM), positions)
    vw = vw.reshape(B, S, G, HEAD_DIM)
    gates = jax.nn.sigmoid(g_nsa.reshape(B, S, G, HPG, 3))

    o_nsa = nsa_attention(q, kc, vc, ks, vs, kw, vw, gates)
    o_conv = short_conv(cb, cc, cx, conv_w, conv_b)
    merged = (jax.nn.sigmoid(g_a) * (o_nsa @ w_proj_nsa)
              + jax.nn.sigmoid(g_b) * (o_conv @ w_proj_conv))
    return merged @ w_out


def setup_inputs(seed: int = 0) -> dict:
    key = jax.random.key(seed)
    ks = iter(jax.random.split(key, 40))
    f32 = jnp.float32

    def nrm(shape, scale):
        return jax.random.normal(next(ks), shape, f32) * scale

    def gain(shape):
        return 1.0 + 0.1 * jax.random.normal(next(ks), shape, f32)

    L = DEPTH
    x = jax.random.normal(next(ks), (BATCH, SEQ, D_MODEL), f32)
    p = jax.random.normal(next(ks), (DEPTH, BATCH, SEQ, PLE_DIM), f32)
    offsets = jax.random.randint(next(ks), (BATCH, 1), 0, 4096, dtype=jnp.int32)
    positions = offsets + jnp.arange(SEQ, dtype=jnp.int32)[None, :]
    return {
        "x": x,
        "p": p,
        "positions": positions,
        "ffn1_norm": gain((L, D_MODEL)),
        "ffn1_w_gate": nrm((L, D_MODEL, D_FF), D_MODEL ** -0.5),
        "ffn1_w_up": nrm((L, D_MODEL, D_FF), D_MODEL ** -0.5),
        "ffn1_w_down": nrm((L, D_FF, D_MODEL), D_FF ** -0.5),
        "mix_norm": gain((L, D_MODEL)),
        "w_in": nrm((L, D_MODEL, IN_COLS), D_MODEL ** -0.5),
        "cmp_k_pos": nrm((L, CMP_LEN, HEAD_DIM), 0.1),
        "cmp_k_w1": nrm((L, CMP_LEN * HEAD_DIM, CMP_HIDDEN), (CMP_LEN * HEAD_DIM) ** -0.5),
        "cmp_k_b1": nrm((L, CMP_HIDDEN), 0.01),
        "cmp_k_w2": nrm((L, CMP_HIDDEN, HEAD_DIM), CMP_HIDDEN ** -0.5),
        "cmp_v_pos": nrm((L, CMP_LEN, HEAD_DIM), 0.1),
        "cmp_v_w1": nrm((L, CMP_LEN * HEAD_DIM, CMP_HIDDEN), (CMP_LEN * HEAD_DIM) ** -0.5),
        "cmp_v_b1": nrm((L, CMP_HIDDEN), 0.01),
        "cmp_v_w2": nrm((L, CMP_HIDDEN, HEAD_DIM), CMP_HIDDEN ** -0.5),
        "conv_w": nrm((L, CONV_K, CONV_WIDTH), 0.5),
        "conv_b": nrm((L, CONV_WIDTH), 0.01),
        "w_proj_nsa": nrm((L, NSA_WIDTH, D_MODEL), NSA_WIDTH ** -0.5),
        "w_proj_conv": nrm((L, CONV_WIDTH, D_MODEL), CONV_WIDTH ** -0.5),
        "w_out": nrm((L, D_MODEL, D_MODEL), D_MODEL ** -0.5),
        "ffn2_norm": gain((L, D_MODEL)),
        "ffn2_w_gate": nrm((L, D_MODEL, D_FF), D_MODEL ** -0.5),
        "ffn2_w_up": nrm((L, D_MODEL, D_FF), D_MODEL ** -0.5),
        "ffn2_w_down": nrm((L, D_FF, D_MODEL), D_FF ** -0.5),
        "ple_norm": gain((L, D_MODEL)),
        "ple_w_gate": nrm((L, D_MODEL, D_MODEL), D_MODEL ** -0.5),
        "ple_w_proj": nrm((L, PLE_DIM, D_MODEL), PLE_DIM ** -0.5),
        "final_norm": gain((D_MODEL,)),
    }


def reference(x, p, positions, ffn1_norm, ffn1_w_gate, ffn1_w_up, ffn1_w_down,
              mix_norm, w_in, cmp_k_pos, cmp_k_w1, cmp_k_b1, cmp_k_w2,
              cmp_v_pos, cmp_v_w1, cmp_v_b1, cmp_v_w2, conv_w, conv_b,
              w_proj_nsa, w_proj_conv, w_out, ffn2_norm, ffn2_w_gate, ffn2_w_up, ffn2_w_down,
              ple_norm, ple_w_gate, ple_w_proj, final_norm):
    h = x
    for i in range(DEPTH):
        h = h + 0.5 * swiglu(rms_norm(h, ffn1_norm[i]), ffn1_w_gate[i], ffn1_w_up[i], ffn1_w_down[i])
        h = h + token_mixing(rms_norm(h, mix_norm[i]), positions, w_in[i],
                             cmp_k_pos[i], cmp_k_w1[i], cmp_k_b1[i], cmp_k_w2[i],
                             cmp_v_pos[i], cmp_v_w1[i], cmp_v_b1[i], cmp_v_w2[i],
                             conv_w[i], conv_b[i], w_proj_nsa[i], w_proj_conv[i], w_out[i])
        h = h + 0.5 * swiglu(rms_norm(h, ffn2_norm[i]), ffn2_w_gate[i], ffn2_w_up[i], ffn2_w_down[i])
        h = h + jax.nn.sigmoid(rms_norm(h, ple_norm[i]) @ ple_w_gate[i]) * (p[i] @ ple_w_proj[i])
    return rms_norm(h, final_norm)
```

```python
import math
from contextlib import ExitStack

import numpy as np
import concourse.bass as bass
import concourse.mybir as mybir
from concourse.bass_utils import run_bass_kernel_spmd

F32 = mybir.dt.float32
BF16 = mybir.dt.bfloat16
I32 = mybir.dt.int32
AF = mybir.ActivationFunctionType
ALU = mybir.AluOpType

D = 1024
DFF = 2816
NJ = DFF // 128
NEG = -30000.0
EPS = 1e-6
SEM_EPOCH = 60000


class Sched:
    def __init__(self, nc, ndma=40):
        self.nc = nc
        self.E = {"pe": nc.tensor, "act": nc.scalar, "dve": nc.vector, "pool": nc.gpsimd, "sp": nc.sync}
        self.sems = {k: [nc.alloc_semaphore(name="s_%s0" % k)] for k in self.E}
        self.cnt = {k: 0 for k in self.E}
        self.dsem = [nc.alloc_semaphore(name="d%d" % i) for i in range(ndma)]
        self.dtarget = [0] * ndma
        self.dnext = 0
        self.seen = {k: {} for k in self.E}
        self.lastw = {}
        self.readers = {}
        self.nwait = 0

    def _cur_sem(self, e):
        ep = self.cnt[e] // SEM_EPOCH
        while len(self.sems[e]) <= ep:
            self.sems[e].append(self.nc.alloc_semaphore(name="s_%s%d" % (e, len(self.sems[e]))))
        return ep

    def _wait(self, e, tok):
        kind, s, v = tok
        if kind == "e":
            if s == e and e == "pe":
                return
            ep = (v - 1) // SEM_EPOCH
            sid = (s, ep)
            semh = self.sems[s][ep]
            val = v - ep * SEM_EPOCH
            for (s2, ep2), _ in list(self.seen[e].items()):
                if s2 == s and ep2 > ep:
                    return
        else:
            sid = ("d", s)
            semh = self.dsem[s]
            val = v
        if self.seen[e].get(sid, 0) >= val:
            return
        self.seen[e][sid] = val
        self.E[e].wait_ge(semh, val)
        self.nwait += 1

    def deps(self, e, reads, writes):
        for k in reads:
            w = self.lastw.get(k)
            if w is not None:
                self._wait(e, w)
        for k in writes:
            w = self.lastw.get(k)
            if w is not None:
                self._wait(e, w)
            rd = self.readers.get(k)
            if rd:
                for sid, r in rd.items():
                    if r[0] == "e" and r[1] == e:
                        continue
                    self._wait(e, r)

    def commit(self, tok, reads, writes):
        for k in writes:
            self.lastw[k] = tok
            self.readers[k] = {}
        sid = (tok[0], tok[1])
        for k in reads:
            self.readers.setdefault(k, {})[sid] = tok

    def op(self, e, fn, r=(), w=()):
        self.deps(e, r, w)
        ins = fn(self.E[e])
        ep = self._cur_sem(e)
        self.cnt[e] += 1
        ins.then_inc(self.sems[e][ep], 1)
        self.commit(("e", e, self.cnt[e]), r, w)

    def dma(self, q, out, in_, r=(), w=(), **kw):
        s = self.dnext
        self.dnext = (self.dnext + 1) % len(self.dsem)
        if self.dtarget[s] > 0:
            self._wait(q, ("d", s, self.dtarget[s]))
        self.deps(q, r, w)
        ins = self.E[q].dma_start(out=out, in_=in_, **kw)
        self.dtarget[s] += 16
        ins.then_inc(self.dsem[s], 16)
        self.commit(("d", s, self.dtarget[s]), r, w)

    def barrier(self):
        toks = [("e", k, self.cnt[k]) for k in self.E if self.cnt[k] > 0]
        toks += [("d", s, t) for s, t in enumerate(self.dtarget) if t > 0]
        for e in self.E:
            for t in toks:
                self._wait(e, t)

    def finish(self):
        toks = [("e", k, self.cnt[k]) for k in self.E if self.cnt[k] > 0]
        toks += [("d", s, t) for s, t in enumerate(self.dtarget) if t > 0]
        for t in toks:
            self._wait("sp", t)


WIN_CHUNKS = 45


def perm_win(w_in):
    off = np.cumsum([0, 512, 128, 128, 128, 128, 128, 128, 24, 512, 512, 512, 1024, 1024])
    q, kc, vc, ks, vs, kw, vw, gn, cb, cc, cx, ga, gb = [w_in[:, off[i]:off[i + 1]] for i in range(13)]

    def swap(m):
        n = m.shape[1] // 64
        mm = m.reshape(m.shape[0], n, 2, 32)
        return mm[:, :, ::-1, :].reshape(m.shape[0], n * 64)

    def qperm(m):
        mm = m.reshape(m.shape[0], 8, 64)
        order = [0, 4, 1, 5, 2, 6, 3, 7]
        return mm[:, order, :].reshape(m.shape[0], 512)

    cols = [qperm(q), qperm(swap(q)), kc, swap(kc), ks, swap(ks), kw, swap(kw), vc,
            vs, vw, cb, cc, cx, ga, gb]
    big = np.concatenate(cols, axis=1)
    assert big.shape[1] == WIN_CHUNKS * 128
    return np.ascontiguousarray(big), np.ascontiguousarray(gn)


C_Q, C_QS, C_KC, C_KCS, C_KS, C_KSS, C_KW, C_KWS, C_VC, C_V, C_CB, C_CC, C_CX, C_GA, C_GB = (
    0, 4, 8, 9, 10, 11, 12, 13, 14, 15, 17, 21, 25, 29, 37)


def pk(v):
    v = np.asarray(v, np.float32)
    return np.ascontiguousarray(v.reshape(-1, 128).T)


def host_consts(S, hf):
    nsel = S // 64
    ncb = (S - 32) // 16 + 1
    nct = (ncb + 127) // 128
    nown = S // 256
    p = np.arange(128)[:, None].astype(np.float64)
    j = np.arange(128)[None, :].astype(np.float64)
    c = {}
    c["c_dp"] = (p - j - 128 * hf).astype(np.float32)
    c["c_tp"] = (16 * p + 31 - j - 128 * hf).astype(np.float32)
    c["c_gidx"] = np.broadcast_to(np.arange(128, dtype=np.float32)[None, :], (128, 128)).copy()
    g0 = np.zeros((128, 128), np.float32)
    g0[:, 0] = 1e4
    c["c_g0"] = g0
    gm = np.zeros((128, S), np.float32)
    xs = np.arange(S)
    gm[xs // 64, xs] = 1.0
    c["c_gm"] = gm
    cs = np.arange(nct * 128)[:, None] * 16
    ss = np.arange(128)[None, :] * 64
    ov = np.clip(np.minimum(cs + 32, ss + 64) - np.maximum(cs, ss), 0, None).astype(np.float32) / 32.0
    ov[ncb:, :] = 0.0
    ov[:, nsel:] = 0.0
    c["c_ov"] = np.ascontiguousarray(ov.reshape(nct, 128, 128).transpose(1, 0, 2))
    i = np.arange(nown)[None, :]
    blkq = 2 * (2 * i + hf) + (np.arange(128)[:, None] >= 64)
    c["c_blkq"] = blkq.astype(np.float32)
    half = 32
    inv_freq = (10000.0 ** (-np.arange(half, dtype=np.float32) / half)).astype(np.float32)
    pp = np.arange(128)
    sc = np.zeros((128, 4), np.float32)
    sc[:, 0] = inv_freq[pp % 32] / (2.0 * np.pi)
    sc[:, 1] = np.where((pp % 64) < 32, -1.0, 1.0) * 2.0 * np.pi
    sc[:, 2] = 1.0 - hf
    sc[:, 3] = float(hf)
    c["c_sc"] = sc
    return c


def build_program(S, stage=9, sub=9):
    NB = S // 128
    NOWN = NB // 2
    SO = NOWN * 128
    NT1 = S // 512
    NT2 = NOWN // 4
    NCB = (S - 32) // 16 + 1
    NCT = (NCB + 127) // 128
    NCP = NCT * 128

    nc = bass.Bass("TRN2", target_bir_lowering=False)

    def din(name, shape, dt=F32):
        return nc.dram_tensor(name, list(shape), dt, kind="ExternalInput").ap()

    def dscr(name, shape, dt):
        return nc.dram_tensor(name, list(shape), dt, kind="Internal").ap()

    xT = din("xT", [8, 128, S])
    posg = din("posg", [1, S], I32)
    pT = din("pT", [2, 128, SO])
    poso = din("poso", [1, SO], I32)
    outT = nc.dram_tensor("outT", [8, 128, SO], F32, kind="ExternalOutput").ap()
    w_f1g, w_f1u, w_f1d = din("w_f1g", [D, DFF]), din("w_f1u", [D, DFF]), din("w_f1d", [DFF, D])
    w_f2g, w_f2u, w_f2d = din("w_f2g", [D, DFF]), din("w_f2u", [D, DFF]), din("w_f2d", [DFF, D])
    w_in = din("w_in", [D, WIN_CHUNKS * 128])
    w_gn = din("w_gn", [D, 24])
    w_pa, w_pb = din("w_pa", [512, D]), din("w_pb", [512, D])
    w_o, w_pg, w_pp = din("w_o", [D, D]), din("w_pg", [D, D]), din("w_pp", [256, D])
    gains = din("gains", [128, 5, 8])
    convp = din("convp", [128, 4, 4])
    cw1 = [din("ck_w1", [128, 32 * 256]), din("cv_w1", [128, 32 * 256])]
    cw2 = [din("ck_w2", [128, 2, 64]), din("cv_w2", [128, 2, 64])]
    cb1 = [din("ck_b1", [128, 2]), din("cv_b1", [128, 2])]
    cpos = [din("ck_pos", [128, 32]), din("cv_pos", [128, 32])]
    c_dp, c_tp, c_gidx, c_g0 = din("c_dp", [128, 128]), din("c_tp", [128, 128]), din("c_gidx", [128, 128]), din("c_g0", [128, 128])
    c_gm = din("c_gm", [128, S])
    c_ov = din("c_ov", [128, NCT, 128])
    c_blkq = din("c_blkq", [128, NOWN])
    c_sc = din("c_sc", [128, 4])

    s_f1g, s_f1u = dscr("s_f1g", [NJ, 128, 1024], BF16), dscr("s_f1u", [NJ, 128, 1024], BF16)
    s_f2g, s_f2u = dscr("s_f2g", [NJ, 128, 1024], BF16), dscr("s_f2u", [NJ, 128, 1024], BF16)
    s_f1d, s_f2d = dscr("s_f1d", [8, 128, DFF], BF16), dscr("s_f2d", [8, 128, DFF], BF16)
    s_in = dscr("s_in", [WIN_CHUNKS, 128, 1024], BF16)
    s_v = dscr("s_v", [128, 8 * 256], BF16)
    s_gn = dscr("s_gn", [128, 8 * 24], BF16)
    s_pa, s_pb = dscr("s_pa", [8, 128, 512], BF16), dscr("s_pb", [8, 128, 512], BF16)
    s_o, s_pg = dscr("s_o", [8, 128, 1024], BF16), dscr("s_pg", [8, 128, 1024], BF16)
    s_pp = dscr("s_pp", [8, 128, 256], BF16)
    h1s = dscr("h1s", [8, 128, S], F32)

    es = ExitStack()
    with es:
        mem = {"cur": 0, "max": 0}

        class _Acct:
            def __init__(self, n):
                self.n = n

            def __enter__(self):
                mem["cur"] += self.n
                mem["max"] = max(mem["max"], mem["cur"])

            def __exit__(self, *a):
                mem["cur"] -= self.n

        def sb(name, shape, dt, stack=es):
            n = int(np.prod(shape[1:])) * (4 if dt in (F32, I32) else 2)
            stack.enter_context(_Acct(n))
            return stack.enter_context(nc.sbuf_tensor(name, list(shape), dt))

        Sx = Sched(nc)
        op, dma = Sx.op, Sx.dma

        PSA = es.enter_context(nc.psum_tensor("psA", [128, 8 * 512], F32))

        def PS(b):
            return PSA[:, b * 512:(b + 1) * 512]

        rot = {"d": 0, "st": 0}

        def psd():
            rot["d"] = (rot["d"] + 1) % 4
            return rot["d"]

        def pst():
            rot["st"] = (rot["st"] + 1) % 3
            return rot["st"]

        ident = sb("ident", [128, 128], F32)
        identb = sb("identb", [128, 128], BF16)
        onesb = sb("onesb", [128, 128], BF16)
        zerob = sb("zerob", [128, 128], BF16)
        epsc = sb("epsc", [128, 1], F32)
        gn_t = sb("gn_t", [128, 5, 8], F32)
        gq_t = sb("gq_t", [128, 8], F32)
        cvp = sb("cvp", [128, 4, 4], F32)
        sc_t = sb("sc_t", [128, 4], F32)
        Dp = sb("Dp", [128, 128], F32)
        Tp = sb("Tp", [128, 128], F32)
        Gidx = sb("Gidx", [128, 128], F32)
        G0 = sb("G0", [128, 128], F32)
        blkq = sb("blkq", [128, NOWN], F32)
        blkq2 = sb("blkq2", [128, NOWN], F32)
        Gm = sb("Gm", [128, S], BF16)
        BAb = sb("BAb", [128, 4, 128], BF16)
        BBb = sb("BBb", [128, 4, 128], BF16)
        W0b = sb("W0b", [128, 4, 128], BF16)
        W1b = sb("W1b", [128, 4, 128], BF16)
        KsT = sb("KsT", [128, S], BF16)
        Vs = sb("Vs", [128, NB, 192], BF16)
        HALOALL = sb("HALOALL", [128, 8, NB, 2], F32)
        KCT = sb("KCT", [128, NCP], BF16)
        CR = sb("CR", [128, NCT, 2, 194], BF16)
        H0_ = sb("H0", [128, 8, 512], F32)
        H = [H0_, H0_]
        UN = sb("UN", [128, 8, 512], BF16)
        BIG = sb("BIG", [128, NJ, 512], BF16)
        RS = sb("RS", [128, 512], F32)
        SG = [sb("SG0", [128, 512], F32), sb("SG1", [128, 512], F32)]
        WR = [sb("WR%d" % i, [128, 1024], BF16) for i in range(5)]
        CS = sb("CS", [128, 512], F32)
        SN = sb("SN", [128, 512], F32)
        TA = sb("TA", [128, 512], F32)
        TB = sb("TB", [128, 512], F32)
        TI = sb("TI", [128, 512], I32)
        PI = TI
        LNV = TB

        HT = H0_
        HK = "H"
        HKEYS = [(HK, d) for d in range(8)]
        s_kw = dscr("s_kw", [128, S], BF16)
        s_vw = dscr("s_vw", [128, NB, 192], BF16)

        wr_i = {"r": 0}

        def wslot(src_ap, ncols, skey):
            i = wr_i["r"]
            wr_i["r"] = (i + 1) % len(WR)
            dma("sp", WR[i][:, 0:ncols], src_ap, r=[skey], w=[("WR", i)])
            return WR[i], ("WR", i)

        def vaug(base_ap, col, ones_col):
            pstride = base_ap.ap[0][0]
            return bass.AP(base_ap.tensor, base_ap.offset + col, [[pstride, 128], [ones_col - col, 2], [1, 64]])

        op("pool", lambda e: e.iota(ident[:, :], pattern=[[1, 128]], base=0, channel_multiplier=-1,
                                    allow_small_or_imprecise_dtypes=True), w=["ident"])
        op("dve", lambda e: e.tensor_scalar(out=ident[:, :], in0=ident[:, :], scalar1=0.0, scalar2=None,
                                            op0=ALU.is_equal), r=["ident"], w=["ident"])
        op("dve", lambda e: e.tensor_copy(out=identb[:, :], in_=ident[:, :]), r=["ident"], w=["identb"])
        op("dve", lambda e: e.memset(onesb[:, :], 1.0), w=["onesb"])
        op("dve", lambda e: e.memset(zerob[:, :], 0.0), w=["zerob"])
        op("dve", lambda e: e.memset(epsc[:, :], EPS), w=["epsc"])
        dma("sp", gn_t[:, :, :], gains[:, :, :], w=["gn"])
        dma("sp", cvp[:, :, :], convp[:, :, :], w=["cvp"])
        dma("sp", sc_t[:, :], c_sc[:, :], w=["sc"])
        dma("sp", Dp[:, :], c_dp[:, :], w=["Dp"])
        dma("sp", Tp[:, :], c_tp[:, :], w=["Tp"])
        dma("sp", Gidx[:, :], c_gidx[:, :], w=["Gidx"])
        dma("sp", G0[:, :], c_g0[:, :], w=["G0"])
        dma("sp", blkq[:, :], c_blkq[:, :], w=["blkq"])
        op("dve", lambda e: e.tensor_scalar(out=blkq2[:, :], in0=blkq[:, :], scalar1=-2.0, scalar2=None, op0=ALU.add),
           r=["blkq"], w=["blkq2"])
        op("dve", lambda e: e.tensor_scalar(out=gq_t[:, :], in0=gn_t[:, 1, :], scalar1=0.125, scalar2=None, op0=ALU.mult),
           r=["gn"], w=["gq"])
        for (dst, o0, thr, nm) in ((BAb, ALU.is_gt, 0.0, "BA"), (BBb, ALU.is_gt, -128.0, "BB"),
                                   (W0b, ALU.is_le, 0.0, "W0"), (W1b, ALU.is_le, -128.0, "W1")):
            for h in range(4):
                op("dve", lambda e, dst=dst, o0=o0, thr=thr, h=h: e.tensor_scalar(
                    out=dst[:, h, :], in0=Dp[:, :], scalar1=thr, scalar2=NEG, op0=o0, op1=ALU.mult),
                   r=["Dp"], w=[nm])
        hflat = HT[:, :, :].rearrange("p a b -> p (a b)")
        for c0 in range(0, S, 4096):
            n = min(4096, S - c0)
            dma("sp", hflat[:, 0:n], c_gm[:, c0:c0 + n], w=HKEYS)
            op("dve", lambda e, c0=c0, n=n: e.tensor_copy(out=Gm[:, c0:c0 + n], in_=hflat[:, 0:n]), r=HKEYS, w=["Gm"])
        for ct in range(NCT):
            dma("sp", hflat[:, 0:128], c_ov[:, ct, :], w=HKEYS)
            for g in range(2):
                op("dve", lambda e, ct=ct, g=g: e.tensor_copy(out=CR[:, ct, g, 0:128], in_=hflat[:, 0:128]), r=HKEYS, w=["CR"])
                op("dve", lambda e, ct=ct, g=g: e.memset(CR[:, ct, g, 192:194], 1.0), w=["CR"])
        op("pool", lambda e: e.memset(Vs[:, :, 64:128], 1.0), w=["Vs1"])
        op("pool", lambda e: e.memset(KCT[:, :], 0.0), w=["KCT"])

        with ExitStack() as es0:
            STF = [sb("STF%d" % i, [128, DFF], F32, es0) for i in range(2)]
            STB = [sb("STB%d" % i, [128, DFF], BF16, es0) for i in range(2)]
            cvi = {"i": 0}

            def conv(src, dst, ncols, nk, gain=None, scale=1.0, dkey=None):
                i = cvi["i"] % 2
                eng = "dve" if (cvi["i"] % 3) != 2 else "pool"
                cvi["i"] += 1
                inner = ncols // nk
                sf = STF[i][:, 0:ncols].rearrange("p (k n) -> p k n", k=nk)
                sbb = STB[i][:, 0:ncols].rearrange("p (k n) -> p k n", k=nk)
                dma("sp", sf, src, w=[("STF", i)])
                if gain is not None:
                    op(eng, lambda e: e.tensor_tensor(out=sbb, in0=sf, in1=gain.unsqueeze(2).to_broadcast([128, nk, inner]),
                                                      op=ALU.mult), r=[("STF", i), "gn", "gq"], w=[("STB", i)])
                elif scale != 1.0:
                    op(eng, lambda e: e.tensor_scalar(out=sbb, in0=sf, scalar1=scale, scalar2=None, op0=ALU.mult),
                       r=[("STF", i)], w=[("STB", i)])
                else:
                    op(eng, lambda e: e.tensor_copy(out=sbb, in_=sf), r=[("STF", i)], w=[("STB", i)])
                dma("pool", dst, STB[i][:, 0:ncols], r=[("STB", i)], w=[dkey])

            def kview(wsrc, c0, c1):
                return wsrc.rearrange("(k p) n -> p k n", p=128)[:, :, c0:c1]

            def conv_ffn(wg, wu, wd, sg, su, sd, gi, tag):
                for j in range(NJ):
                    conv(kview(wg, j * 128, (j + 1) * 128), sg[j, :, :], 1024, 8, gain=gn_t[:, gi, :], dkey=(tag + "g", j))
                    conv(kview(wu, j * 128, (j + 1) * 128), su[j, :, :], 1024, 8, gain=gn_t[:, gi, :], dkey=(tag + "u", j))
                for d in range(8):
                    conv(wd.rearrange("(j p) n -> p j n", p=128)[:, :, d * 128:(d + 1) * 128], sd[d, :, :], DFF, NJ,
                         scale=0.5, dkey=(tag + "d", d))

            conv_ffn(w_f1g, w_f1u, w_f1d, s_f1g, s_f1u, s_f1d, 0, "f1")
            for c in range(WIN_CHUNKS):
                if c in (C_V, C_V + 1):
                    continue
                conv(kview(w_in, c * 128, (c + 1) * 128), s_in[c, :, :], 1024, 8,
                     gain=(gq_t[:, :] if c < 8 else gn_t[:, 1, :]), dkey=("in", c))
            conv(kview(w_in, C_V * 128, (C_V + 2) * 128), s_v[:, :], 2048, 8, gain=gn_t[:, 1, :], dkey="s_v")
            conv(kview(w_gn, 0, 24), s_gn[:, :], 192, 8, gain=gn_t[:, 1, :], dkey="s_gn")
            for d in range(8):
                conv(w_pa.rearrange("(k p) n -> p k n", p=128)[:, :, d * 128:(d + 1) * 128], s_pa[d, :, :], 512, 4, dkey=("pa", d))
                conv(w_pb.rearrange("(k p) n -> p k n", p=128)[:, :, d * 128:(d + 1) * 128], s_pb[d, :, :], 512, 4, dkey=("pb", d))
                conv(kview(w_o, d * 128, (d + 1) * 128), s_o[d, :, :], 1024, 8, dkey=("o", d))
                conv(kview(w_pg, d * 128, (d + 1) * 128), s_pg[d, :, :], 1024, 8, gain=gn_t[:, 3, :], dkey=("pg", d))
                conv(w_pp.rearrange("(k p) n -> p k n", p=128)[:, :, d * 128:(d + 1) * 128], s_pp[d, :, :], 256, 2, dkey=("pp", d))
            conv_ffn(w_f2g, w_f2u, w_f2d, s_f2g, s_f2u, s_f2d, 2, "f2")
            Sx.barrier()

        def rms_norm(out_bf, okey, apply=True):
            op("pool", lambda e: e.tensor_tensor(out=out_bf[:, 0:4, :], in0=HT[:, 0:4, :], in1=HT[:, 0:4, :], op=ALU.mult),
               r=HKEYS, w=[(okey, 0)])
            op("dve", lambda e: e.tensor_tensor(out=out_bf[:, 4:8, :], in0=HT[:, 4:8, :], in1=HT[:, 4:8, :], op=ALU.mult),
               r=HKEYS, w=[(okey, 1)])
            b = psd()
            for k in range(8):
                op("pe", lambda e, k=k: e.matmul(PS(b), lhsT=onesb[:, :], rhs=out_bf[:, k, :], start=(k == 0), stop=(k == 7)),
                   r=[(okey, k // 4), "onesb"], w=[("PS", b)])
            op("act", lambda e: e.activation(out=LNV[:, :], in_=PS(b), func=AF.Ln, bias=epsc[:, :], scale=1.0 / D),
               r=[("PS", b), "epsc"], w=["TB"])
            op("act", lambda e: e.activation(out=RS[:, :], in_=LNV[:, :], func=AF.Exp, scale=-0.5), r=["TB"], w=["RS"])
            if not apply:
                return
            rb = RS[:, :].unsqueeze(1).to_broadcast([128, 4, 512])
            op("dve", lambda e: e.tensor_tensor(out=out_bf[:, 0:4, :], in0=HT[:, 0:4, :], in1=rb, op=ALU.mult),
               r=HKEYS + ["RS"], w=[(okey, 0)])
            op("pool", lambda e: e.tensor_tensor(out=out_bf[:, 4:8, :], in0=HT[:, 4:8, :], in1=rb, op=ALU.mult),
               r=HKEYS + ["RS"], w=[(okey, 1)])

        UNK = [("UN", 0), ("UN", 1)]

        def fm_proj(src_dram, skey, xin, xkeys, nk, ncols, consume, ntok=512):
            wt, wk = wslot(src_dram, ncols, skey)
            b = psd()
            wv = wt[:, 0:ncols].rearrange("p (k n) -> p k n", k=nk)
            for k in range(nk):
                op("pe", lambda e, k=k: e.matmul(PS(b)[:, 0:ntok], lhsT=wv[:, k, :], rhs=xin[:, k, :], start=(k == 0), stop=(k == nk - 1)),
                   r=[wk] + xkeys, w=[("PS", b)])
            consume(b)

        def ffn(sg, su, sd, tag):
            rms_norm(UN, "UN")
            for j in range(NJ):
                wg, wgk = wslot(sg[j, :, :], 1024, (tag + "g", j))
                wu, wuk = wslot(su[j, :, :], 1024, (tag + "u", j))
                bg, bu = psd(), psd()
                wgv = wg[:, :].rearrange("p (k n) -> p k n", k=8)
                wuv = wu[:, :].rearrange("p (k n) -> p k n", k=8)
                for k in range(8):
                    op("pe", lambda e, k=k: e.matmul(PS(bg), lhsT=wgv[:, k, :], rhs=UN[:, k, :], start=(k == 0), stop=(k == 7)),
                       r=[wgk] + UNK, w=[("PS", bg)])
                for k in range(8):
                    op("pe", lambda e, k=k: e.matmul(PS(bu), lhsT=wuv[:, k, :], rhs=UN[:, k, :], start=(k == 0), stop=(k == 7)),
                       r=[wuk] + UNK, w=[("PS", bu)])
                sgt = SG[j % 2]
                op("act", lambda e: e.activation(out=sgt[:, :], in_=PS(bg), func=AF.Silu), r=[("PS", bg)], w=[("SG", j % 2)])
                op("dve", lambda e: e.tensor_tensor(out=BIG[:, j, :], in0=sgt[:, :], in1=PS(bu), op=ALU.mult),
                   r=[("SG", j % 2), ("PS", bu)], w=[("BIG", j)])
            for d in range(8):
                b = psd()
                for (j0, j1) in ((0, 8), (8, 16), (16, NJ)):
                    wd, wdk = wslot(sd[d, :, j0 * 128:j1 * 128], (j1 - j0) * 128, (tag + "d", d))
                    wdv = wd[:, 0:(j1 - j0) * 128].rearrange("p (j n) -> p j n", j=j1 - j0)
                    for j in range(j0, j1):
                        op("pe", lambda e, j=j, wdv=wdv, j0=j0: e.matmul(PS(b), lhsT=wdv[:, j - j0, :], rhs=BIG[:, j, :],
                                                                        start=(j == 0), stop=(j == NJ - 1)),
                           r=[wdk, ("BIG", j)], w=[("PS", b)])
                op("dve", lambda e, d=d: e.tensor_tensor(out=HT[:, d, :], in0=HT[:, d, :], in1=PS(b), op=ALU.add),
                   r=[(HK, d), ("PS", b)], w=[(HK, d)])

        SIN_SCALE = 2.0 * math.pi * 0.999999

        def rope_tables(pos_ap, ntok):
            dma("sp", PI[:, 0:ntok], pos_ap.partition_broadcast(128), w=["TI"])
            op("dve", lambda e: e.tensor_copy(out=TA[:, 0:ntok], in_=PI[:, 0:ntok]), r=["TI"], w=["TA"])
            op("dve", lambda e: e.tensor_scalar(out=TA[:, 0:ntok], in0=TA[:, 0:ntok], scalar1=sc_t[:, 0:1], scalar2=None,
                                                op0=ALU.mult), r=["TA", "sc"], w=["TA"])
            for (dst, shift, scale_ap, nm) in ((SN, 0.0, sc_t[:, 1:2], "SN"), (CS, 0.25, SIN_SCALE, "CS")):
                eng = "dve"
                op(eng, lambda e, shift=shift: e.tensor_scalar(out=TB[:, 0:ntok], in0=TA[:, 0:ntok], scalar1=shift, scalar2=None,
                                                               op0=ALU.add), r=["TA"], w=["TB"])
                op(eng, lambda e: e.tensor_copy(out=TI[:, 0:ntok], in_=TB[:, 0:ntok]), r=["TB"], w=["TI"])
                op(eng, lambda e, dst=dst: e.tensor_copy(out=dst[:, 0:ntok], in_=TI[:, 0:ntok]), r=["TI"], w=[nm])
                op(eng, lambda e, dst=dst: e.tensor_tensor(out=TB[:, 0:ntok], in0=TB[:, 0:ntok], in1=dst[:, 0:ntok], op=ALU.subtract),
                   r=["TB", nm], w=["TB"])
                op(eng, lambda e, dst=dst: e.tensor_scalar(out=dst[:, 0:ntok], in0=TB[:, 0:ntok], scalar1=0.5, scalar2=None,
                                                           op0=ALU.is_gt), r=["TB"], w=[nm])
                op(eng, lambda e, dst=dst: e.tensor_tensor(out=TB[:, 0:ntok], in0=TB[:, 0:ntok], in1=dst[:, 0:ntok], op=ALU.subtract),
                   r=["TB", nm], w=["TB"])
                op(eng, lambda e, dst=dst: e.tensor_scalar(out=dst[:, 0:ntok], in0=TB[:, 0:ntok], scalar1=-0.5, scalar2=None,
                                                           op0=ALU.is_lt), r=["TB"], w=[nm])
                op(eng, lambda e, dst=dst: e.tensor_tensor(out=TB[:, 0:ntok], in0=TB[:, 0:ntok], in1=dst[:, 0:ntok], op=ALU.add),
                   r=["TB", nm], w=["TB"])
                op("act", lambda e, dst=dst, scale_ap=scale_ap: e.activation(out=dst[:, 0:ntok], in_=TB[:, 0:ntok], func=AF.Sin,
                                                                             scale=scale_ap), r=["TB", "sc"], w=[nm])

        def rope_apply(bp, bs, out_ap, okeys, ntok):
            op("dve", lambda e: e.tensor_tensor(out=TA[:, 0:ntok], in0=PS(bp)[:, 0:ntok], in1=CS[:, 0:ntok], op=ALU.mult),
               r=[("PS", bp), "CS"], w=["TA"])
            op("dve", lambda e: e.tensor_tensor(out=TB[:, 0:ntok], in0=PS(bs)[:, 0:ntok], in1=SN[:, 0:ntok], op=ALU.mult),
               r=[("PS", bs), "SN"], w=["TB"])
            op("pool", lambda e: e.tensor_tensor(out=out_ap, in0=TA[:, 0:ntok], in1=TB[:, 0:ntok], op=ALU.add),
               r=["TA", "TB"], w=okeys)

        with ExitStack() as es1:
            KcT = sb("KcT", [128, S], BF16, es1)
            VcT = sb("VcT", [128, S], BF16, es1)
            KWst = sb("KWst", [128, 512], BF16, es1)
            VWst = sb("VWst", [128, 4, 192], BF16, es1)
            op("pool", lambda e: e.memset(VWst[:, :, 64:128], 1.0), w=["VWst"])
            for t in range(NT1 if stage >= 1 else 0):
                tok = slice(t * 512, (t + 1) * 512)
                dma("sp", HT[:, :, :], xT[:, :, tok].rearrange("c p t -> p c t"), w=HKEYS)
                ffn(s_f1g, s_f1u, s_f1d, "f1")
                if sub < 2:
                    continue
                dma("pool", h1s[:, :, tok].rearrange("c p t -> p c t"), HT[:, :, :], r=HKEYS, w=[("h1s", t)])
                op("pool", lambda e, t=t: e.tensor_copy(out=HALOALL[:, :, t * 4:(t + 1) * 4, :],
                                                        in_=HT[:, :, :].rearrange("p c (b t) -> p c b t", t=128)[:, :, :, 126:128]),
                   r=HKEYS, w=["HALOALL"])
                if sub < 3:
                    continue
                rms_norm(UN, "UN")
                rope_tables(posg[:, tok], 512)
                if sub < 4:
                    continue
                for (cp, cs_, dst_ap, keys) in ((C_KC, C_KCS, KcT[:, tok], [("KcT", t)]), (C_KS, C_KSS, KsT[:, tok], [("KsT", t)]),
                                                (C_KW, C_KWS, KWst[:, :], ["KWst"])):
                    banks = []
                    for c in (cp, cs_):
                        fm_proj(s_in[c, :, :], ("in", c), UN, UNK, 8, 1024, lambda b: banks.append(b))
                    rope_apply(banks[0], banks[1], dst_ap, keys, 512)
                dma("pool", s_kw[:, tok], KWst[:, :], r=["KWst"], w=[("s_kw", t)])
                if sub < 5:
                    continue
                fm_proj(s_in[C_VC, :, :], ("in", C_VC), UN, UNK, 8, 1024,
                        lambda b: op("act", lambda e: e.copy(out=VcT[:, tok], in_=PS(b)), r=[("PS", b)], w=[("VcT", t)]))
                if sub < 6:
                    continue
                wvt0, wvk0 = wslot(s_v[:, 0:1024], 1024, "s_v")
                wvt1, wvk1 = wslot(s_v[:, 1024:2048], 1024, "s_v")
                for bl in range(4):
                    gb_ = t * 4 + bl
                    b = psd()
                    for k in range(8):
                        wt_, wk_ = (wvt0, wvk0) if k < 4 else (wvt1, wvk1)
                        kk = k % 4
                        op("pe", lambda e, k=k, kk=kk, wt_=wt_, bl=bl, b=b: e.matmul(
                            PS(b)[:, 0:256], lhsT=UN[:, k, bl * 128:(bl + 1) * 128], rhs=wt_[:, kk * 256:(kk + 1) * 256],
                            start=(k == 0), stop=(k == 7)), r=[wk_] + UNK, w=[("PS", b)])
                    op("act", lambda e, gb_=gb_, b=b: e.copy(out=Vs[:, gb_, 0:64], in_=PS(b)[:, 0:64]), r=[("PS", b)], w=[("Vs", gb_)])
                    op("act", lambda e, gb_=gb_, b=b: e.copy(out=Vs[:, gb_, 128:192], in_=PS(b)[:, 64:128]), r=[("PS", b)], w=[("Vs", gb_)])
                    if sub < 7:
                        continue
                    op("act", lambda e, bl=bl, b=b: e.copy(out=VWst[:, bl, 0:64], in_=PS(b)[:, 128:192]), r=[("PS", b)], w=["VWst"])
                    op("act", lambda e, bl=bl, b=b: e.copy(out=VWst[:, bl, 128:192], in_=PS(b)[:, 192:256]), r=[("PS", b)], w=["VWst"])
                if sub < 7:
                    continue
                dma("pool", s_vw[:, t * 4:(t + 1) * 4, :], VWst[:, :, :], r=["VWst"], w=[("s_vw", t)])

            with ExitStack() as es15:
                W1b_ = sb("W1b_", [128, 32 * 256], BF16, es15)
                W2f = sb("W2f", [128, 2, 64], F32, es15)
                W2p = sb("W2p", [128, 2, 128], BF16, es15)
                B1 = sb("B1", [128, 2], F32, es15)
                B1e = sb("B1e", [128, 2], F32, es15)
                POSf = sb("POSf", [128, 32], F32, es15)
                POSb = sb("POSb", [128, 32], BF16, es15)
                HID = sb("HID", [128, 2, NCP], BF16, es15)
                HF, HG = TA, CS
                allK = [("KcT", t) for t in range(NT1)]
                allV = [("VcT", t) for t in range(NT1)]
                op("dve", lambda e: e.memset(HID[:, :, :], 0.0), w=["HID"])
                for kind in range(2 if stage >= 2 else 0):
                    srcT, skeys = (KcT, allK) if kind == 0 else (VcT, allV)
                    for q2 in range(2):
                        dma("sp", hflat[:, 0:4096], cw1[kind][:, q2 * 4096:(q2 + 1) * 4096], w=HKEYS)
                        op("dve", lambda e, q2=q2: e.tensor_copy(out=W1b_[:, q2 * 4096:(q2 + 1) * 4096], in_=hflat[:, 0:4096]),
                           r=HKEYS, w=["W1b"])
                    dma("sp", W2f[:, :, :], cw2[kind][:, :, :], w=["W2f"])
                    dma("sp", B1[:, :], cb1[kind][:, :], w=["B1"])
                    dma("sp", POSf[:, :], cpos[kind][:, :], w=["POSf"])
                    op("dve", lambda e: e.tensor_copy(out=W2p[:, :, 0:64], in_=W2f[:, :, :]), r=["W2f"], w=["W2p"])
                    op("dve", lambda e: e.tensor_copy(out=W2p[:, :, 64:128], in_=W2f[:, :, :]), r=["W2f"], w=["W2p"])
                    op("dve", lambda e: e.tensor_copy(out=POSb[:, :], in_=POSf[:, :]), r=["POSf"], w=["POSb"])
                    w1v = W1b_[:, :].rearrange("p (l n) -> p l n", l=32)
                    for hh in range(2):
                        b = psd()
                        for l in range(32):
                            op("pe", lambda e, l=l, hh=hh, b=b: e.matmul(PS(b)[:, 0:1], lhsT=w1v[0:64, l, hh * 128:(hh + 1) * 128],
                                                                       rhs=POSb[0:64, l:l + 1], start=(l == 0), stop=(l == 31)),
                               r=["W1b", "POSb"], w=[("PS", b)])
                        op("dve", lambda e, hh=hh, b=b: e.tensor_tensor(out=B1e[:, hh:hh + 1], in0=PS(b)[:, 0:1], in1=B1[:, hh:hh + 1],
                                                                      op=ALU.add), r=[("PS", b), "B1"], w=["B1e"])
                    for g in range(2):
                        gp = slice(64 * g, 64 * g + 64)
                        for hh in range(2):
                            b = psd()
                            for l in range(32):
                                rhs = srcT[gp, l:l + 16 * (NCB - 1) + 1:16]
                                op("pe", lambda e, l=l, hh=hh, b=b, rhs=rhs, gp=gp: e.matmul(
                                    PS(b)[:, 0:NCB], lhsT=w1v[gp, l, hh * 128:(hh + 1) * 128], rhs=rhs,
                                    start=(l == 0), stop=(l == 31)), r=["W1b"] + skeys, w=[("PS", b)])
                            op("act", lambda e, hh=hh, b=b: e.activation(out=HF[:, 0:NCB], in_=PS(b)[:, 0:NCB], func=AF.Identity,
                                                                        bias=B1e[:, hh:hh + 1], scale=1.0), r=[("PS", b), "B1e"], w=["TA"])
                            op("dve", lambda e: e.tensor_tensor(out=HG[:, 0:NCB], in0=HF[:, 0:NCB], in1=HF[:, 0:NCB], op=ALU.mult),
                               r=["TA"], w=["CS"])
                            op("dve", lambda e: e.tensor_scalar(out=HG[:, 0:NCB], in0=HG[:, 0:NCB], scalar1=0.044715, scalar2=1.0,
                                                                op0=ALU.mult, op1=ALU.add), r=["CS"], w=["CS"])
                            op("dve", lambda e: e.tensor_tensor(out=HG[:, 0:NCB], in0=HG[:, 0:NCB], in1=HF[:, 0:NCB], op=ALU.mult),
                               r=["CS", "TA"], w=["CS"])
                            op("act", lambda e: e.activation(out=HG[:, 0:NCB], in_=HG[:, 0:NCB], func=AF.Sigmoid, scale=1.5957691216),
                               r=["CS"], w=["CS"])
                            op("dve", lambda e, hh=hh: e.tensor_tensor(out=HID[:, hh, 0:NCB], in0=HG[:, 0:NCB], in1=HF[:, 0:NCB], op=ALU.mult),
                               r=["CS", "TA"], w=["HID"])
                        if kind == 0:
                            b = psd()
                            for hh in range(2):
                                op("pe", lambda e, hh=hh, b=b: e.matmul(PS(b)[:, 0:NCB], lhsT=W2p[:, hh, :], rhs=HID[:, hh, 0:NCB],
                                                                       start=(hh == 0), stop=(hh == 1)), r=["W2p", "HID"], w=[("PS", b)])
                            op("act", lambda e, b=b, gp=gp: e.copy(out=KCT[gp, 0:NCB], in_=PS(b)[gp, 0:NCB]), r=[("PS", b)], w=["KCT"])
                        else:
                            for ct in range(NCT):
                                b = psd()
                                for hh in range(2):
                                    op("pe", lambda e, hh=hh, b=b, ct=ct: e.matmul(PS(b)[:, 0:64], lhsT=HID[:, hh, ct * 128:(ct + 1) * 128],
                                                                                 rhs=W2p[:, hh, 0:64], start=(hh == 0), stop=(hh == 1)),
                                       r=["W2p", "HID"], w=[("PS", b)])
                                op("act", lambda e, b=b, ct=ct, g=g: e.copy(out=CR[:, ct, g, 128:192], in_=PS(b)[:, 0:64]),
                                   r=[("PS", b)], w=["CR"])
                Sx.barrier()
            Sx.barrier()

        NH = 2 * NOWN
        with ExitStack() as es2:
            HCU = sb("HCU", [128, 4, NH], F32, es2)
            def halo_pass(esh):
                HALO = sb("HALO", [128, 8, NOWN, 2], F32, esh)
                HUN = sb("HUN", [128, 8, NH], BF16, esh)
                HSQ = sb("HSQ", [128, 8, NH], BF16, esh)
                HTMP = sb("HTMP", [128, NH], F32, esh)
                op("dve", lambda e: e.memset(HALO[:, :, :, :], 0.0), w=["HALO"])
                if NOWN > 1:
                    op("dve", lambda e: e.tensor_scalar(out=HALO[:, :, 1:NOWN, :], in0=HALOALL[:, :, 1:NB - 1:2, :], scalar1=sc_t[:, 2:3],
                                                        scalar2=None, op0=ALU.mult), r=["HALOALL", "sc", "HALO"], w=["HALO"])
                op("dve", lambda e: e.scalar_tensor_tensor(out=HALO[:, :, :, :], in0=HALOALL[:, :, 0:NB:2, :], scalar=sc_t[:, 3:4],
                                                           in1=HALO[:, :, :, :], op0=ALU.mult, op1=ALU.add),
                   r=["HALOALL", "HALO", "sc"], w=["HALO"])
                hal = HALO[:, :, :, :].rearrange("p c n t -> p c (n t)")
                op("dve", lambda e: e.tensor_tensor(out=HSQ[:, :, :], in0=hal, in1=hal, op=ALU.mult), r=["HALO"], w=["HSQ"])
                b = psd()
                for k in range(8):
                    op("pe", lambda e, k=k, b=b: e.matmul(PS(b)[:, 0:NH], lhsT=onesb[:, :], rhs=HSQ[:, k, :], start=(k == 0), stop=(k == 7)),
                       r=["HSQ", "onesb"], w=[("PS", b)])
                op("act", lambda e, b=b: e.activation(out=HTMP[:, :], in_=PS(b)[:, 0:NH], func=AF.Ln, bias=epsc[:, :], scale=1.0 / D),
                   r=[("PS", b), "epsc"], w=["HTMP"])
                op("act", lambda e: e.activation(out=HTMP[:, :], in_=HTMP[:, :], func=AF.Exp, scale=-0.5), r=["HTMP"], w=["HTMP"])
                op("dve", lambda e: e.tensor_tensor(out=HUN[:, :, :], in0=hal, in1=HTMP[:, :].unsqueeze(1).to_broadcast([128, 8, NH]),
                                                    op=ALU.mult), r=["HALO", "HTMP"], w=["HUN"])
                for c4 in range(4):
                    banks = []
                    for cbase in (C_CC, C_CX):
                        fm_proj(s_in[cbase + c4, :, :], ("in", cbase + c4), HUN, ["HUN"], 8, 1024, lambda b: banks.append(b), ntok=NH)
                    op("act", lambda e, b0=banks[0]: e.copy(out=HTMP[:, :], in_=PS(b0)[:, 0:NH]), r=[("PS", banks[0])], w=["HTMP"])
                    op("dve", lambda e, c4=c4, b1=banks[1]: e.tensor_tensor(out=HCU[:, c4, :], in0=HTMP[:, :], in1=PS(b1)[:, 0:NH], op=ALU.mult),
                       r=["HTMP", ("PS", banks[1])], w=["HCU"])
                Sx.barrier()

            if stage >= 3:
                with ExitStack() as esh_:
                    halo_pass(esh_)

            CAND = sb("CAND", [128, 8, 128], F32, es2)
            QT = sb("QT", [128, 4, 512], BF16, es2)
            SIGN = sb("SIGN", [128, 4, 24], F32, es2)
            CBt = sb("CBt", [128, 512], BF16, es2)
            CU = sb("CU", [128, 4, 130], F32, es2)
            CY = sb("CY", [128, 512], F32, es2)
            ONT = sb("ONT", [128, 4, 512], BF16, es2)
            ONSA = sb("ONSA", [128, 512], F32, es2)
            PTb = [sb("PTb%d" % i, [128, 512], BF16, es2) for i in range(3)]
            BiasT = [sb("BiasT%d" % i, [128, 4, 128], BF16, es2) for i in range(2)]
            CBI = [sb("CBI%d" % i, [128, 4, 128], BF16, es2) for i in range(2)]
            OTs = sb("OTs", [128, 512], F32, es2)
            AFm = sb("AFm", [128, 128], F32, es2)
            FFm = sb("FFm", [128, 128], F32, es2)
            IMP = sb("IMP", [128, 128], F32, es2)
            SC1 = sb("SC1", [128, 128], F32, es2)
            SC2 = sb("SC2", [128, 128], F32, es2)
            BROW = sb("BROW", [128, 128], F32, es2)
            M8a = sb("M8a", [128, 8], F32, es2)
            M8b = sb("M8b", [128, 8], F32, es2)
            R4 = sb("R4", [128, 4], F32, es2)
            W4 = sb("W4", [128, 4], F32, es2)
            PTlb = sb("PTlb", [128, 2, 512], BF16, es2)
            KWt = [sb("KWt%d" % i, [128, 6 * 128], BF16, es2) for i in range(2)]
            VWt = [sb("VWt%d" % i, [128, 6, 192], BF16, es2) for i in range(2)]
            pt_i = {"i": 0, "cb": 0}
            h1v = h1s.rearrange("c p (n t) -> p c n t", t=128)
            h1keys = [("h1s", t) for t in range(NT1)]
            kwkeys = [("s_kw", t) for t in range(NT1)]
            vwkeys = [("s_vw", t) for t in range(NT1)]

            def attn_tiles(i, bl, g, kind, qrhs, qkeys, win):
                gp = slice(64 * g, 64 * g + 64)
                sgkey = ("SIGN", bl)
                if kind == 0:
                    kts = list(range(0, 2 * i + 2))
                    ob, gcol = 4, 1
                else:
                    kts = [k for k in range(2 * i - 4, 2 * i + 2) if k >= 0]
                    ob, gcol = 5, 2
                    wslot_i, kt0 = win
                for n_, kt in enumerate(kts):
                    b = pst()
                    extra = []
                    if kind == 0:
                        extra.append((Gm[:, kt * 128:(kt + 1) * 128], BiasT[g][:, :, :], ["Gm", ("BiasT", g)]))
                        klhs = KsT[gp, kt * 128:(kt + 1) * 128]
                        kkeys = [("KsT", kt // 4)]
                        vl = Vs[:, kt, 64 * g:64 * g + 128]
                        vkeys = [("Vs", kt), "Vs1"]
                    else:
                        kl = kt - kt0
                        klhs = KWt[wslot_i][gp, kl * 128:(kl + 1) * 128]
                        kkeys = [("KWt", wslot_i)]
                        vl = VWt[wslot_i][:, kl, 64 * g:64 * g + 128]
                        vkeys = [("VWt", wslot_i)]
                    rel = kt - 2 * i
                    if rel == 0:
                        extra.append((identb[:, :], BAb[:, :, :], ["identb", "BA"]))
                    elif rel == 1:
                        extra.append((identb[:, :], BBb[:, :, :], ["identb", "BB"]))
                    elif kind == 1 and rel == -4:
                        extra.append((identb[:, :], W0b[:, :, :], ["identb", "W0"]))
                    elif kind == 1 and rel == -3:
                        extra.append((identb[:, :], W1b[:, :, :], ["identb", "W1"]))
                    ne = len(extra)
                    op("pe", lambda e, b=b, klhs=klhs, ne=ne: e.matmul(PS(b), lhsT=klhs, rhs=qrhs, start=True, stop=(ne == 0)),
                       r=kkeys + qkeys, w=[("PS", b)])
                    for xi, (l_, r_, ks_) in enumerate(extra):
                        op("pe", lambda e, b=b, l_=l_, r_=r_, xi=xi, ne=ne: e.matmul(PS(b), lhsT=l_, rhs=r_, start=False, stop=(xi == ne - 1)),
                           r=ks_, w=[("PS", b)])
                    pi_ = pt_i["i"]
                    pt_i["i"] = (pi_ + 1) % 3
                    op("act", lambda e, b=b, pi_=pi_: e.activation(out=PTb[pi_][:, :], in_=PS(b), func=AF.Exp), r=[("PS", b)], w=[("PTb", pi_)])
                    nk_ = len(kts)
                    op("pe", lambda e, pi_=pi_, vl=vl, n_=n_, nk_=nk_: e.matmul(PS(ob), lhsT=vl, rhs=PTb[pi_][:, :],
                                                                              start=(n_ == 0), stop=(n_ == nk_ - 1)),
                       r=vkeys + [("PTb", pi_)], w=[("PS", ob)])
                op("act", lambda e: e.copy(out=OTs[:, :], in_=PS(ob)), r=[("PS", ob)], w=["OTs"])
                for h in range(4):
                    op("pe", lambda e, h=h: e.transpose(PS(3)[:, h * 128:(h + 1) * 128], OTs[:, h * 128:(h + 1) * 128], ident[:, :]),
                       r=["OTs", "ident"], w=[("PS", 3)])
                p3 = PS(3).rearrange("p (h n) -> p h n", h=4)
                oc, dc = 64 * g, 64 * (1 - g)
                op("dve", lambda e: e.tensor_scalar(out=R4[:, :], in0=p3[:, :, dc], scalar1=1e-30, scalar2=None, op0=ALU.max),
                   r=[("PS", 3)], w=["R4"])
                op("dve", lambda e: e.reciprocal(out=R4[:, :], in_=R4[:, :]), r=["R4"], w=["R4"])
                op("dve", lambda e: e.tensor_tensor(out=W4[:, :], in0=R4[:, :], in1=SIGN[:, bl, g * 12 + gcol:g * 12 + 12:3], op=ALU.mult),
                   r=["R4", sgkey], w=["W4"])
                for h in range(4):
                    hc_ = slice((g * 4 + h) * 64, (g * 4 + h) * 64 + 64)
                    op("dve", lambda e, h=h, hc_=hc_: e.scalar_tensor_tensor(out=ONSA[:, hc_], in0=p3[:, h, oc:oc + 64], scalar=W4[:, h:h + 1],
                                                                            in1=ONSA[:, hc_], op0=ALU.mult, op1=ALU.add),
                       r=[("PS", 3), "W4", "ONSA"], w=["ONSA"])

            for t in range(NT2 if stage >= 4 else 0):
                otok = slice(t * 512, (t + 1) * 512)
                for bl in range(4):
                    i = t * 4 + bl
                    hb_ = HT[:, :, bl * 128:(bl + 1) * 128]
                    dma("sp", hb_, h1v[:, :, 2 * i, :], r=h1keys, w=HKEYS)
                    dma("sp", CAND[:, :, :], h1v[:, :, 2 * i + 1, :], r=h1keys, w=["CAND"])
                    op("dve", lambda e, hb_=hb_: e.tensor_scalar(out=hb_, in0=hb_, scalar1=sc_t[:, 2:3], scalar2=None, op0=ALU.mult),
                       r=HKEYS + ["sc"], w=HKEYS)
                    op("dve", lambda e, hb_=hb_: e.scalar_tensor_tensor(out=hb_, in0=CAND[:, :, :], scalar=sc_t[:, 3:4], in1=hb_,
                                                                        op0=ALU.mult, op1=ALU.add), r=["CAND", "sc"] + HKEYS, w=HKEYS)
                rms_norm(UN, "UN")
                rope_tables(poso[:, otok], 512)
                for c in range(4):
                    banks = []
                    for cc_ in (C_Q + c, C_QS + c):
                        fm_proj(s_in[cc_, :, :], ("in", cc_), UN, UNK, 8, 1024, lambda b: banks.append(b))
                    rope_apply(banks[0], banks[1], QT[:, c, :], [("QT", c)], 512)
                wgt, wgk = wslot(s_gn[:, :], 192, "s_gn")
                for bl in range(4):
                    b = psd()
                    for k in range(8):
                        op("pe", lambda e, k=k, bl=bl, b=b: e.matmul(PS(b)[:, 0:24], lhsT=UN[:, k, bl * 128:(bl + 1) * 128],
                                                                    rhs=wgt[:, k * 24:(k + 1) * 24], start=(k == 0), stop=(k == 7)),
                           r=[wgk] + UNK, w=[("PS", b)])
                    op("act", lambda e, bl=bl, b=b: e.activation(out=SIGN[:, bl, :], in_=PS(b)[:, 0:24], func=AF.Sigmoid),
                       r=[("PS", b)], w=[("SIGN", bl)])
                for c4 in range(4):
                    fm_proj(s_in[C_CB + c4, :, :], ("in", C_CB + c4), UN, UNK, 8, 1024,
                            lambda b: op("act", lambda e: e.copy(out=CBt[:, :], in_=PS(b)), r=[("PS", b)], w=["CBt"]))
                    banks = []
                    fm_proj(s_in[C_CC + c4, :, :], ("in", C_CC + c4), UN, UNK, 8, 1024, lambda b: banks.append(b))
                    fm_proj(s_in[C_CX + c4, :, :], ("in", C_CX + c4), UN, UNK, 8, 1024, lambda b: banks.append(b))
                    op("act", lambda e, b0=banks[0]: e.copy(out=CY[:, :], in_=PS(b0)), r=[("PS", banks[0])], w=["CY"])
                    op("dve", lambda e, b1=banks[1]: e.tensor_tensor(
                        out=CU[:, :, 2:130], in0=CY[:, :].rearrange("p (b t) -> p b t", b=4),
                        in1=PS(b1).rearrange("p (b t) -> p b t", b=4), op=ALU.mult), r=["CY", ("PS", banks[1])], w=["CU"])
                    op("pool", lambda e, c4=c4, t=t: e.tensor_copy(out=CU[:, :, 0:2],
                                                                   in_=HCU[:, c4, t * 8:(t + 1) * 8].rearrange("p (b t) -> p b t", t=2)),
                       r=["HCU"], w=["CU"])
                    cy3 = CY[:, :].rearrange("p (b t) -> p b t", b=4)
                    op("dve", lambda e, c4=c4, cy3=cy3: e.tensor_scalar(out=cy3, in0=CU[:, :, 0:128], scalar1=cvp[:, c4, 0:1],
                                                                        scalar2=cvp[:, c4, 3:4], op0=ALU.mult, op1=ALU.add),
                       r=["CU", "cvp"], w=["CY"])
                    op("dve", lambda e, c4=c4, cy3=cy3: e.scalar_tensor_tensor(out=cy3, in0=CU[:, :, 1:129], scalar=cvp[:, c4, 1:2], in1=cy3,
                                                                               op0=ALU.mult, op1=ALU.add), r=["CU", "cvp", "CY"], w=["CY"])
                    op("dve", lambda e, c4=c4, cy3=cy3: e.scalar_tensor_tensor(out=cy3, in0=CU[:, :, 2:130], scalar=cvp[:, c4, 2:3], in1=cy3,
                                                                               op0=ALU.mult, op1=ALU.add), r=["CU", "cvp", "CY"], w=["CY"])
                    op("pool", lambda e, c4=c4: e.tensor_tensor(out=BIG[:, 16 + c4, :], in0=CY[:, :], in1=CBt[:, :], op=ALU.mult),
                       r=["CY", "CBt"], w=[("BIG", 16 + c4)])
                for which, cbase in ((0, C_GA), (1, C_GB)):
                    for d in range(8):
                        fm_proj(s_in[cbase + d, :, :], ("in", cbase + d), UN, UNK, 8, 1024,
                                lambda b, which=which, d=d: op("act", lambda e: e.activation(out=BIG[:, which * 8 + d, :], in_=PS(b), func=AF.Sigmoid),
                                                               r=[("PS", b)], w=[("BIG", which * 8 + d)]))
                for bl in range(4):
                    i = t * 4 + bl
                    sgkey = ("SIGN", bl)
                    kt0 = max(0, 2 * i - 4)
                    nwk = 2 * i + 2 - kt0
                    wsl = i % 2
                    dma("sp", KWt[wsl][:, 0:nwk * 128], s_kw[:, kt0 * 128:(2 * i + 2) * 128], r=kwkeys, w=[("KWt", wsl)])
                    dma("sp", VWt[wsl][:, 0:nwk, :], s_vw[:, kt0:2 * i + 2, :], r=vwkeys, w=[("VWt", wsl)])
                    op("dve", lambda e, i=i: e.tensor_scalar(out=AFm[:, :], in0=Gidx[:, :], scalar1=blkq[:, i:i + 1], scalar2=None, op0=ALU.is_le),
                       r=["Gidx", "blkq"], w=["AFm"])
                    op("dve", lambda e, i=i: e.tensor_scalar(out=FFm[:, :], in0=Gidx[:, :], scalar1=blkq2[:, i:i + 1], scalar2=None, op0=ALU.is_gt),
                       r=["Gidx", "blkq2"], w=["FFm"])
                    op("dve", lambda e: e.tensor_tensor(out=FFm[:, :], in0=FFm[:, :], in1=AFm[:, :], op=ALU.mult), r=["FFm", "AFm"], w=["FFm"])
                    op("dve", lambda e: e.scalar_tensor_tensor(out=FFm[:, :], in0=FFm[:, :], scalar=1e4, in1=G0[:, :], op0=ALU.mult, op1=ALU.add),
                       r=["FFm", "G0"], w=["FFm"])
                    n_ct = min(NCT, (16 * i + 14) // 128 + 1)
                    cbias = {}
                    for ct in range(n_ct):
                        thr = 128 * (2 * i - 16 * ct)
                        if thr < 2063:
                            ci = pt_i["cb"]
                            pt_i["cb"] = (ci + 1) % 2
                            op("pool", lambda e, ci=ci, thr=thr: e.tensor_scalar(
                                out=CBI[ci][:, :, :], in0=Tp[:, :].unsqueeze(1).to_broadcast([128, 4, 128]), scalar1=float(thr), scalar2=NEG,
                                op0=ALU.is_gt, op1=ALU.mult), r=["Tp"], w=[("CBI", ci)])
                            cbias[ct] = ci
                    assert len(cbias) <= 2
                    for g in range(2):
                        gp = slice(64 * g, 64 * g + 64)
                        qrhs = QT[gp, :, bl * 128:(bl + 1) * 128]
                        qkeys = [("QT", c) for c in range(4)]
                        for zb in (6, 7):
                            op("pe", lambda e, zb=zb: e.matmul(PS(zb), lhsT=zerob[:, :], rhs=BAb[:, :, :].rearrange("p a b -> p (a b)"),
                                                              start=True, stop=False), r=["zerob", "BA"], w=[("PS", 6)])
                        for ct in range(n_ct):
                            b = pst()
                            hb = ct in cbias
                            op("pe", lambda e, b=b, ct=ct, hb=hb: e.matmul(PS(b), lhsT=KCT[gp, ct * 128:(ct + 1) * 128], rhs=qrhs, start=True, stop=not hb),
                               r=["KCT"] + qkeys, w=[("PS", b)])
                            if hb:
                                ci = cbias[ct]
                                op("pe", lambda e, b=b, ci=ci: e.matmul(PS(b), lhsT=identb[:, :], rhs=CBI[ci][:, :, :], start=False, stop=True),
                                   r=["identb", ("CBI", ci)], w=[("PS", b)])
                            pi_ = pt_i["i"]
                            pt_i["i"] = (pi_ + 1) % 3
                            op("act", lambda e, b=b, pi_=pi_: e.activation(out=PTb[pi_][:, :], in_=PS(b), func=AF.Exp), r=[("PS", b)], w=[("PTb", pi_)])
                            for h in range(4):
                                op("pe", lambda e, h=h, ct=ct, pi_=pi_, g=g: e.matmul(
                                    PSA[:, 6 * 512 + h * 256:6 * 512 + h * 256 + 193], lhsT=PTb[pi_][:, h * 128:(h + 1) * 128],
                                    rhs=CR[:, ct, g, 0:193], start=False, stop=(ct == n_ct - 1)),
                                   r=[("PTb", pi_), "CR"], w=[("PS", 6)])
                        co = PSA[:, 6 * 512:8 * 512].rearrange("p (h n) -> p h n", h=4)
                        op("dve", lambda e: e.tensor_scalar(out=R4[:, :], in0=co[:, :, 192], scalar1=1e-30, scalar2=None, op0=ALU.max),
                           r=[("PS", 6)], w=["R4"])
                        op("dve", lambda e: e.reciprocal(out=R4[:, :], in_=R4[:, :]), r=["R4"], w=["R4"])
                        op("dve", lambda e: e.tensor_scalar(out=IMP[:, :], in0=co[:, 0, 0:128], scalar1=R4[:, 0:1], scalar2=None, op0=ALU.mult),
                           r=[("PS", 6), "R4"], w=["IMP"])
                        for h in range(1, 4):
                            op("dve", lambda e, h=h: e.scalar_tensor_tensor(out=IMP[:, :], in0=co[:, h, 0:128], scalar=R4[:, h:h + 1], in1=IMP[:, :],
                                                                            op0=ALU.mult, op1=ALU.add), r=[("PS", 6), "R4", "IMP"], w=["IMP"])
                        op("dve", lambda e, g=g: e.tensor_tensor(out=W4[:, :], in0=R4[:, :], in1=SIGN[:, bl, g * 12:g * 12 + 12:3], op=ALU.mult),
                           r=["R4", sgkey], w=["W4"])
                        for h in range(4):
                            hc_ = slice((g * 4 + h) * 64, (g * 4 + h) * 64 + 64)
                            op("dve", lambda e, h=h, hc_=hc_: e.tensor_scalar(out=ONSA[:, hc_], in0=co[:, h, 128:192], scalar1=W4[:, h:h + 1],
                                                                              scalar2=None, op0=ALU.mult), r=[("PS", 6), "W4"], w=["ONSA"])
                        op("dve", lambda e: e.scalar_tensor_tensor(out=SC1[:, :], in0=IMP[:, :], scalar=1.0, in1=AFm[:, :], op0=ALU.add, op1=ALU.mult),
                           r=["IMP", "AFm"], w=["SC1"])
                        op("dve", lambda e: e.tensor_tensor(out=SC1[:, :], in0=SC1[:, :], in1=FFm[:, :], op=ALU.add), r=["SC1", "FFm"], w=["SC1"])
                        op("dve", lambda e: e.max(out=M8a[:, :], in_=SC1[:, :]), r=["SC1"], w=["M8a"])
                        op("dve", lambda e: e.match_replace(out=SC2[:, :], in_to_replace=M8a[:, :], in_values=SC1[:, :], imm_value=-1.0),
                           r=["SC1", "M8a"], w=["SC2"])
                        op("dve", lambda e: e.max(out=M8b[:, :], in_=SC2[:, :]), r=["SC2"], w=["M8b"])
                        op("dve", lambda e: e.tensor_scalar(out=BROW[:, :], in0=SC1[:, :], scalar1=M8b[:, 7:8], scalar2=NEG, op0=ALU.is_lt, op1=ALU.mult),
                           r=["SC1", "M8b"], w=["BROW"])
                        op("pe", lambda e: e.transpose(PS(3)[:, 0:128], BROW[:, :], ident[:, :]), r=["BROW", "ident"], w=[("PS", 3)])
                        op("dve", lambda e, g=g: e.tensor_copy(out=BiasT[g][:, :, :], in_=PS(3)[:, 0:128].unsqueeze(1).to_broadcast([128, 4, 128])),
                           r=[("PS", 3)], w=[("BiasT", g)])
                        attn_tiles(i, bl, g, 0, qrhs, qkeys, None)
                        attn_tiles(i, bl, g, 1, qrhs, qkeys, (wsl, kt0))
                    for c in range(4):
                        op("pe", lambda e, c=c: e.transpose(PS(3)[:, c * 128:(c + 1) * 128], ONSA[:, c * 128:(c + 1) * 128], ident[:, :]),
                           r=["ONSA", "ident"], w=[("PS", 3)])
                    op("act", lambda e, bl=bl: e.copy(out=ONT[:, :, bl * 128:(bl + 1) * 128], in_=PS(3).rearrange("p (c n) -> p c n", c=4)),
                       r=[("PS", 3)], w=[("ONT", bl)])
                ontk = [("ONT", bl) for bl in range(4)]
                for d in range(8):
                    fm_proj(s_pa[d, :, :], ("pa", d), ONT, ontk, 4, 512,
                            lambda b, d=d: op("dve", lambda e: e.tensor_tensor(out=TA[:, :], in0=BIG[:, d, :], in1=PS(b), op=ALU.mult),
                                              r=[("BIG", d), ("PS", b)], w=["TA"]))
                    fm_proj(s_pb[d, :, :], ("pb", d), BIG[:, 16:20, :], [("BIG", 16 + c) for c in range(4)], 4, 512,
                            lambda b, d=d: op("dve", lambda e: e.tensor_tensor(out=TB[:, :], in0=BIG[:, 8 + d, :], in1=PS(b), op=ALU.mult),
                                              r=[("BIG", 8 + d), ("PS", b)], w=["TB"]))
                    op("pool", lambda e, d=d: e.tensor_tensor(out=BIG[:, d, :], in0=TA[:, :], in1=TB[:, :], op=ALU.add),
                       r=["TA", "TB"], w=[("BIG", d)])
                mk = [("BIG", d) for d in range(8)]
                for d in range(8):
                    fm_proj(s_o[d, :, :], ("o", d), BIG[:, 0:8, :], mk, 8, 1024,
                            lambda b, d=d: op("dve", lambda e: e.tensor_tensor(out=HT[:, d, :], in0=HT[:, d, :], in1=PS(b), op=ALU.add),
                                              r=[(HK, d), ("PS", b)], w=[(HK, d)]))
                ffn(s_f2g, s_f2u, s_f2d, "f2")
                rms_norm(UN, "UN")
                dma("sp", TA[:, :], pT[0, :, otok], w=["TA"])
                dma("sp", CS[:, :], pT[1, :, otok], w=["CS"])
                op("pool", lambda e: e.tensor_copy(out=PTlb[:, 0, :], in_=TA[:, :]), r=["TA"], w=["PTlb"])
                op("pool", lambda e: e.tensor_copy(out=PTlb[:, 1, :], in_=CS[:, :]), r=["CS"], w=["PTlb"])
                for d in range(8):
                    fm_proj(s_pg[d, :, :], ("pg", d), UN, UNK, 8, 1024,
                            lambda b: op("act", lambda e: e.activation(out=CY[:, :], in_=PS(b), func=AF.Sigmoid), r=[("PS", b)], w=["CY"]))
                    fm_proj(s_pp[d, :, :], ("pp", d), PTlb, ["PTlb"], 2, 256,
                            lambda b: op("dve", lambda e: e.tensor_tensor(out=CY[:, :], in0=CY[:, :], in1=PS(b), op=ALU.mult),
                                         r=["CY", ("PS", b)], w=["CY"]))
                    op("dve", lambda e, d=d: e.tensor_tensor(out=HT[:, d, :], in0=HT[:, d, :], in1=CY[:, :], op=ALU.add),
                       r=[(HK, d), "CY"], w=[(HK, d)])
                rms_norm(UN, "UN", apply=False)
                for d in range(8):
                    eng = "dve" if d % 2 == 0 else "pool"
                    op(eng, lambda e, d=d: e.tensor_tensor(out=HT[:, d, :], in0=HT[:, d, :], in1=RS[:, :], op=ALU.mult),
                       r=[(HK, d), "RS"], w=[(HK, d)])
                    op(eng, lambda e, d=d: e.tensor_scalar(out=HT[:, d, :], in0=HT[:, d, :], scalar1=gn_t[:, 4, d:d + 1], scalar2=None, op0=ALU.mult),
                       r=[(HK, d), "gn"], w=[(HK, d)])
                dma("pool", outT[:, :, otok].rearrange("c p t -> p c t"), HT[:, :, :], r=HKEYS, w=[("out", t)])
            Sx.finish()
    print("[kernel] sbuf max bytes/partition", mem["max"], "instr", {k: v for k, v in Sx.cnt.items()}, "waits", Sx.nwait)
    return nc


def make_in_maps(inputs, S, B):
    f = lambda a: np.ascontiguousarray(np.asarray(a), dtype=np.float32)
    x = f(inputs["x"])
    p = f(inputs["p"])[0]
    pos = np.asarray(inputs["positions"]).astype(np.int32)
    w_in_p, w_gn = perm_win(f(inputs["w_in"])[0])
    gains = np.stack([pk(inputs[k][0]) for k in ("ffn1_norm", "mix_norm", "ffn2_norm", "ple_norm")] + [pk(inputs["final_norm"])], axis=1)
    cw = f(inputs["conv_w"])[0]
    cbv = f(inputs["conv_b"])[0]
    convp = np.stack([pk(cw[0]), pk(cw[1]), pk(cw[2]), pk(cbv)], axis=2)
    shared = {
        "w_f1g": f(inputs["ffn1_w_gate"])[0], "w_f1u": f(inputs["ffn1_w_up"])[0], "w_f1d": f(inputs["ffn1_w_down"])[0],
        "w_f2g": f(inputs["ffn2_w_gate"])[0], "w_f2u": f(inputs["ffn2_w_up"])[0], "w_f2d": f(inputs["ffn2_w_down"])[0],
        "w_in": w_in_p, "w_gn": w_gn,
        "w_pa": f(inputs["w_proj_nsa"])[0], "w_pb": f(inputs["w_proj_conv"])[0],
        "w_o": f(inputs["w_out"])[0], "w_pg": f(inputs["ple_w_gate"])[0], "w_pp": f(inputs["ple_w_proj"])[0],
        "gains": np.ascontiguousarray(gains), "convp": np.ascontiguousarray(convp),
    }
    for nm, pre in (("ck", "cmp_k"), ("cv", "cmp_v")):
        w1 = f(inputs[pre + "_w1"])[0].reshape(32, 64, 256).transpose(1, 0, 2).reshape(64, 32 * 256)
        shared[nm + "_w1"] = np.ascontiguousarray(np.concatenate([w1, w1], axis=0))
        shared[nm + "_w2"] = np.ascontiguousarray(f(inputs[pre + "_w2"])[0].reshape(2, 128, 64).transpose(1, 0, 2))
        shared[nm + "_b1"] = pk(f(inputs[pre + "_b1"])[0])
        pe = f(inputs[pre + "_pos"])[0].T
        shared[nm + "_pos"] = np.ascontiguousarray(np.concatenate([pe, pe], axis=0))
    maps = []
    owns = []
    for b in range(B):
        xTb = np.ascontiguousarray(x[b].T.reshape(8, 128, S))
        for hf in range(2):
            own = (np.arange(S // 256)[:, None] * 256 + hf * 128 + np.arange(128)[None, :]).reshape(-1)
            owns.append((b, own))
            m = dict(shared)
            m["xT"] = xTb
            m["posg"] = np.ascontiguousarray(pos[b][None, :])
            m["pT"] = np.ascontiguousarray(p[b][own].T.reshape(2, 128, own.size))
            m["poso"] = np.ascontiguousarray(pos[b][own][None, :])
            m.update(host_consts(S, hf))
            maps.append(m)
    return maps, owns


_CACHE = {}


def run(inputs, S, B):
    if S not in _CACHE:
        _CACHE[S] = build_program(S)
    nc = _CACHE[S]
    maps, owns = make_in_maps(inputs, S, B)
    res = run_bass_kernel_spmd(nc, maps, core_ids=list(range(2 * B)))
    out = np.zeros((B, S, D), np.float32)
    for c, (b, own) in enumerate(owns):
        o = np.asarray(res.results[c]["outT"]).reshape(D, own.size)
        out[b, own, :] = o.T
    return out


def kernel(**inputs):
    return run(inputs, 8192, 4)
```

```python
import math
from contextlib import ExitStack

import numpy as np
import concourse.bass as bass
import concourse.mybir as mybir
from concourse.bass_utils import run_bass_kernel_spmd

F32 = mybir.dt.float32
BF16 = mybir.dt.bfloat16
I32 = mybir.dt.int32
AF = mybir.ActivationFunctionType
ALU = mybir.AluOpType

D = 1024
DFF = 2816
NJ = DFF // 128
NEG = -30000.0
EPS = 1e-6
SEM_EPOCH = 60000


class Sched:
    def __init__(self, nc, ndma=40):
        self.nc = nc
        self.E = {"pe": nc.tensor, "act": nc.scalar, "dve": nc.vector, "pool": nc.gpsimd, "sp": nc.sync}
        self.sems = {k: [nc.alloc_semaphore(name="s_%s0" % k)] for k in self.E}
        self.cnt = {k: 0 for k in self.E}
        self.dsem = [nc.alloc_semaphore(name="d%d" % i) for i in range(ndma)]
        self.dtarget = [0] * ndma
        self.dnext = 0
        self.seen = {k: {} for k in self.E}
        self.lastw = {}
        self.readers = {}
        self.nwait = 0

    def _cur_sem(self, e):
        ep = self.cnt[e] // SEM_EPOCH
        while len(self.sems[e]) <= ep:
            self.sems[e].append(self.nc.alloc_semaphore(name="s_%s%d" % (e, len(self.sems[e]))))
        return ep

    def _wait(self, e, tok):
        kind, s, v = tok
        if kind == "e":
            if s == e and e == "pe":
                return
            ep = (v - 1) // SEM_EPOCH
            sid = (s, ep)
            semh = self.sems[s][ep]
            val = v - ep * SEM_EPOCH
            for (s2, ep2), _ in list(self.seen[e].items()):
                if s2 == s and ep2 > ep:
                    return
        else:
            sid = ("d", s)
            semh = self.dsem[s]
            val = v
        if self.seen[e].get(sid, 0) >= val:
            return
        self.seen[e][sid] = val
        self.E[e].wait_ge(semh, val)
        self.nwait += 1

    def deps(self, e, reads, writes):
        for k in reads:
            w = self.lastw.get(k)
            if w is not None:
                self._wait(e, w)
        for k in writes:
            w = self.lastw.get(k)
            if w is not None:
                self._wait(e, w)
            rd = self.readers.get(k)
            if rd:
                for sid, r in rd.items():
                    if r[0] == "e" and r[1] == e:
                        continue
                    self._wait(e, r)

    def commit(self, tok, reads, writes):
        for k in writes:
            self.lastw[k] = tok
            self.readers[k] = {}
        sid = (tok[0], tok[1])
        for k in reads:
            self.readers.setdefault(k, {})[sid] = tok

    def op(self, e, fn, r=(), w=()):
        self.deps(e, r, w)
        ins = fn(self.E[e])
        ep = self._cur_sem(e)
        self.cnt[e] += 1
        ins.then_inc(self.sems[e][ep], 1)
        self.commit(("e", e, self.cnt[e]), r, w)

    def dma(self, q, out, in_, r=(), w=(), **kw):
        s = self.dnext
        self.dnext = (self.dnext + 1) % len(self.dsem)
        if self.dtarget[s] > 0:
            self._wait(q, ("d", s, self.dtarget[s]))
        self.deps(q, r, w)
        ins = self.E[q].dma_start(out=out, in_=in_, **kw)
        self.dtarget[s] += 16
        ins.then_inc(self.dsem[s], 16)
        self.commit(("d", s, self.dtarget[s]), r, w)

    def barrier(self):
        toks = [("e", k, self.cnt[k]) for k in self.E if self.cnt[k] > 0]
        toks += [("d", s, t) for s, t in enumerate(self.dtarget) if t > 0]
        for e in self.E:
            for t in toks:
                self._wait(e, t)

    def finish(self):
        toks = [("e", k, self.cnt[k]) for k in self.E if self.cnt[k] > 0]
        toks += [("d", s, t) for s, t in enumerate(self.dtarget) if t > 0]
        for t in toks:
            self._wait("sp", t)


WIN_CHUNKS = 45


def perm_win(w_in):
    off = np.cumsum([0, 512, 128, 128, 128, 128, 128, 128, 24, 512, 512, 512, 1024, 1024])
    q, kc, vc, ks, vs, kw, vw, gn, cb, cc, cx, ga, gb = [w_in[:, off[i]:off[i + 1]] for i in range(13)]

    def swap(m):
        n = m.shape[1] // 64
        mm = m.reshape(m.shape[0], n, 2, 32)
        return mm[:, :, ::-1, :].reshape(m.shape[0], n * 64)

    def qperm(m):
        mm = m.reshape(m.shape[0], 8, 64)
        order = [0, 4, 1, 5, 2, 6, 3, 7]
        return mm[:, order, :].reshape(m.shape[0], 512)

    cols = [qperm(q), qperm(swap(q)), kc, swap(kc), ks, swap(ks), kw, swap(kw), vc,
            vs, vw, cb, cc, cx, ga, gb]
    big = np.concatenate(cols, axis=1)
    assert big.shape[1] == WIN_CHUNKS * 128
    return np.ascontiguousarray(big), np.ascontiguousarray(gn)


C_Q, C_QS, C_KC, C_KCS, C_KS, C_KSS, C_KW, C_KWS, C_VC, C_V, C_CB, C_CC, C_CX, C_GA, C_GB = (
    0, 4, 8, 9, 10, 11, 12, 13, 14, 15, 17, 21, 25, 29, 37)


def pk(v):
    v = np.asarray(v, np.float32)
    return np.ascontiguousarray(v.reshape(-1, 128).T)


def host_consts(S, hf):
    nsel = S // 64
    ncb = (S - 32) // 16 + 1
    nct = (ncb + 127) // 128
    nown = S // 256
    p = np.arange(128)[:, None].astype(np.float64)
    j = np.arange(128)[None, :].astype(np.float64)
    c = {}
    c["c_dp"] = (p - j - 128 * hf).astype(np.float32)
    c["c_tp"] = (16 * p + 31 - j - 128 * hf).astype(np.float32)
    c["c_gidx"] = np.broadcast_to(np.arange(128, dtype=np.float32)[None, :], (128, 128)).copy()
    g0 = np.zeros((128, 128), np.float32)
    g0[:, 0] = 1e4
    c["c_g0"] = g0
    gm = np.zeros((128, S), np.float32)
    xs = np.arange(S)
    gm[xs // 64, xs] = 1.0
    c["c_gm"] = gm
    cs = np.arange(nct * 128)[:, None] * 16
    ss = np.arange(128)[None, :] * 64
    ov = np.clip(np.minimum(cs + 32, ss + 64) - np.maximum(cs, ss), 0, None).astype(np.float32) / 32.0
    ov[ncb:, :] = 0.0
    ov[:, nsel:] = 0.0
    c["c_ov"] = np.ascontiguousarray(ov.reshape(nct, 128, 128).transpose(1, 0, 2))
    i = np.arange(nown)[None, :]
    blkq = 2 * (2 * i + hf) + (np.arange(128)[:, None] >= 64)
    c["c_blkq"] = blkq.astype(np.float32)
    half = 32
    inv_freq = (10000.0 ** (-np.arange(half, dtype=np.float32) / half)).astype(np.float32)
    pp = np.arange(128)
    sc = np.zeros((128, 4), np.float32)
    sc[:, 0] = inv_freq[pp % 32] / (2.0 * np.pi)
    sc[:, 1] = np.where((pp % 64) < 32, -1.0, 1.0) * 2.0 * np.pi
    sc[:, 2] = 1.0 - hf
    sc[:, 3] = float(hf)
    c["c_sc"] = sc
    return c


def build_program(S, stage=9, sub=9):
    NB = S // 128
    NOWN = NB // 2
    SO = NOWN * 128
    NT1 = S // 512
    NT2 = NOWN // 4
    NCB = (S - 32) // 16 + 1
    NCT = (NCB + 127) // 128
    NCP = NCT * 128

    nc = bass.Bass("TRN2", target_bir_lowering=False)

    def din(name, shape, dt=F32):
        return nc.dram_tensor(name, list(shape), dt, kind="ExternalInput").ap()

    def dscr(name, shape, dt):
        return nc.dram_tensor(name, list(shape), dt, kind="Internal").ap()

    xT = din("xT", [8, 128, S])
    posg = din("posg", [1, S], I32)
    pT = din("pT", [2, 128, SO])
    poso = din("poso", [1, SO], I32)
    outT = nc.dram_tensor("outT", [8, 128, SO], F32, kind="ExternalOutput").ap()
    w_f1g, w_f1u, w_f1d = din("w_f1g", [D, DFF]), din("w_f1u", [D, DFF]), din("w_f1d", [DFF, D])
    w_f2g, w_f2u, w_f2d = din("w_f2g", [D, DFF]), din("w_f2u", [D, DFF]), din("w_f2d", [DFF, D])
    w_in = din("w_in", [D, WIN_CHUNKS * 128])
    w_gn = din("w_gn", [D, 24])
    w_pa, w_pb = din("w_pa", [512, D]), din("w_pb", [512, D])
    w_o, w_pg, w_pp = din("w_o", [D, D]), din("w_pg", [D, D]), din("w_pp", [256, D])
    gains = din("gains", [128, 5, 8])
    convp = din("convp", [128, 4, 4])
    cw1 = [din("ck_w1", [128, 32 * 256]), din("cv_w1", [128, 32 * 256])]
    cw2 = [din("ck_w2", [128, 2, 64]), din("cv_w2", [128, 2, 64])]
    cb1 = [din("ck_b1", [128, 2]), din("cv_b1", [128, 2])]
    cpos = [din("ck_pos", [128, 32]), din("cv_pos", [128, 32])]
    c_dp, c_tp, c_gidx, c_g0 = din("c_dp", [128, 128]), din("c_tp", [128, 128]), din("c_gidx", [128, 128]), din("c_g0", [128, 128])
    c_gm = din("c_gm", [128, S])
    c_ov = din("c_ov", [128, NCT, 128])
    c_blkq = din("c_blkq", [128, NOWN])
    c_sc = din("c_sc", [128, 4])

    s_f1g, s_f1u = dscr("s_f1g", [NJ, 128, 1024], BF16), dscr("s_f1u", [NJ, 128, 1024], BF16)
    s_f2g, s_f2u = dscr("s_f2g", [NJ, 128, 1024], BF16), dscr("s_f2u", [NJ, 128, 1024], BF16)
    s_f1d, s_f2d = dscr("s_f1d", [8, 128, DFF], BF16), dscr("s_f2d", [8, 128, DFF], BF16)
    s_in = dscr("s_in", [WIN_CHUNKS, 128, 1024], BF16)
    s_v = dscr("s_v", [128, 8 * 256], BF16)
    s_gn = dscr("s_gn", [128, 8 * 24], BF16)
    s_pa, s_pb = dscr("s_pa", [8, 128, 512], BF16), dscr("s_pb", [8, 128, 512], BF16)
    s_o, s_pg = dscr("s_o", [8, 128, 1024], BF16), dscr("s_pg", [8, 128, 1024], BF16)
    s_pp = dscr("s_pp", [8, 128, 256], BF16)
    h1s = dscr("h1s", [8, 128, S], F32)

    es = ExitStack()
    with es:
        mem = {"cur": 0, "max": 0}

        class _Acct:
            def __init__(self, n):
                self.n = n

            def __enter__(self):
                mem["cur"] += self.n
                mem["max"] = max(mem["max"], mem["cur"])

            def __exit__(self, *a):
                mem["cur"] -= self.n

        def sb(name, shape, dt, stack=es):
            n = int(np.prod(shape[1:])) * (4 if dt in (F32, I32) else 2)
            stack.enter_context(_Acct(n))
            return stack.enter_context(nc.sbuf_tensor(name, list(shape), dt))

        Sx = Sched(nc)
        op, dma = Sx.op, Sx.dma

        PSA = es.enter_context(nc.psum_tensor("psA", [128, 8 * 512], F32))

        def PS(b):
            return PSA[:, b * 512:(b + 1) * 512]

        rot = {"d": 0, "st": 0}

        def psd():
            rot["d"] = (rot["d"] + 1) % 4
            return rot["d"]

        def pst():
            rot["st"] = (rot["st"] + 1) % 3
            return rot["st"]

        ident = sb("ident", [128, 128], F32)
        identb = sb("identb", [128, 128], BF16)
        onesb = sb("onesb", [128, 128], BF16)
        zerob = sb("zerob", [128, 128], BF16)
        epsc = sb("epsc", [128, 1], F32)
        gn_t = sb("gn_t", [128, 5, 8], F32)
        gq_t = sb("gq_t", [128, 8], F32)
        cvp = sb("cvp", [128, 4, 4], F32)
        sc_t = sb("sc_t", [128, 4], F32)
        Dp = sb("Dp", [128, 128], F32)
        Tp = sb("Tp", [128, 128], F32)
        Gidx = sb("Gidx", [128, 128], F32)
        G0 = sb("G0", [128, 128], F32)
        blkq = sb("blkq", [128, NOWN], F32)
        blkq2 = sb("blkq2", [128, NOWN], F32)
        Gm = sb("Gm", [128, S], BF16)
        BAb = sb("BAb", [128, 4, 128], BF16)
        BBb = sb("BBb", [128, 4, 128], BF16)
        W0b = sb("W0b", [128, 4, 128], BF16)
        W1b = sb("W1b", [128, 4, 128], BF16)
        KsT = sb("KsT", [128, S], BF16)
        Vs = sb("Vs", [128, NB, 192], BF16)
        HALOALL = sb("HALOALL", [128, 8, NB, 2], F32)
        KCT = sb("KCT", [128, NCP], BF16)
        CR = sb("CR", [128, NCT, 2, 194], BF16)
        H0_ = sb("H0", [128, 8, 512], F32)
        H = [H0_, H0_]
        UN = sb("UN", [128, 8, 512], BF16)
        BIG = sb("BIG", [128, NJ, 512], BF16)
        RS = sb("RS", [128, 512], F32)
        SG = [sb("SG0", [128, 512], F32), sb("SG1", [128, 512], F32)]
        WR = [sb("WR%d" % i, [128, 1024], BF16) for i in range(5)]
        CS = sb("CS", [128, 512], F32)
        SN = sb("SN", [128, 512], F32)
        TA = sb("TA", [128, 512], F32)
        TB = sb("TB", [128, 512], F32)
        TI = sb("TI", [128, 512], I32)
        PI = TI
        LNV = TB

        HT = H0_
        HK = "H"
        HKEYS = [(HK, d) for d in range(8)]
        s_kw = dscr("s_kw", [128, S], BF16)
        s_vw = dscr("s_vw", [128, NB, 192], BF16)

        wr_i = {"r": 0}

        def wslot(src_ap, ncols, skey):
            i = wr_i["r"]
            wr_i["r"] = (i + 1) % len(WR)
            dma("sp", WR[i][:, 0:ncols], src_ap, r=[skey], w=[("WR", i)])
            return WR[i], ("WR", i)

        def vaug(base_ap, col, ones_col):
            pstride = base_ap.ap[0][0]
            return bass.AP(base_ap.tensor, base_ap.offset + col, [[pstride, 128], [ones_col - col, 2], [1, 64]])

        op("pool", lambda e: e.iota(ident[:, :], pattern=[[1, 128]], base=0, channel_multiplier=-1,
                                    allow_small_or_imprecise_dtypes=True), w=["ident"])
        op("dve", lambda e: e.tensor_scalar(out=ident[:, :], in0=ident[:, :], scalar1=0.0, scalar2=None,
                                            op0=ALU.is_equal), r=["ident"], w=["ident"])
        op("dve", lambda e: e.tensor_copy(out=identb[:, :], in_=ident[:, :]), r=["ident"], w=["identb"])
        op("dve", lambda e: e.memset(onesb[:, :], 1.0), w=["onesb"])
        op("dve", lambda e: e.memset(zerob[:, :], 0.0), w=["zerob"])
        op("dve", lambda e: e.memset(epsc[:, :], EPS), w=["epsc"])
        dma("sp", gn_t[:, :, :], gains[:, :, :], w=["gn"])
        dma("sp", cvp[:, :, :], convp[:, :, :], w=["cvp"])
        dma("sp", sc_t[:, :], c_sc[:, :], w=["sc"])
        dma("sp", Dp[:, :], c_dp[:, :], w=["Dp"])
        dma("sp", Tp[:, :], c_tp[:, :], w=["Tp"])
        dma("sp", Gidx[:, :], c_gidx[:, :], w=["Gidx"])
        dma("sp", G0[:, :], c_g0[:, :], w=["G0"])
        dma("sp", blkq[:, :], c_blkq[:, :], w=["blkq"])
        op("dve", lambda e: e.tensor_scalar(out=blkq2[:, :], in0=blkq[:, :], scalar1=-2.0, scalar2=None, op0=ALU.add),
           r=["blkq"], w=["blkq2"])
        op("dve", lambda e: e.tensor_scalar(out=gq_t[:, :], in0=gn_t[:, 1, :], scalar1=0.125, scalar2=None, op0=ALU.mult),
           r=["gn"], w=["gq"])
        for (dst, o0, thr, nm) in ((BAb, ALU.is_gt, 0.0, "BA"), (BBb, ALU.is_gt, -128.0, "BB"),
                                   (W0b, ALU.is_le, 0.0, "W0"), (W1b, ALU.is_le, -128.0, "W1")):
            for h in range(4):
                op("dve", lambda e, dst=dst, o0=o0, thr=thr, h=h: e.tensor_scalar(
                    out=dst[:, h, :], in0=Dp[:, :], scalar1=thr, scalar2=NEG, op0=o0, op1=ALU.mult),
                   r=["Dp"], w=[nm])
        hflat = HT[:, :, :].rearrange("p a b -> p (a b)")
        for c0 in range(0, S, 4096):
            n = min(4096, S - c0)
            dma("sp", hflat[:, 0:n], c_gm[:, c0:c0 + n], w=HKEYS)
            op("dve", lambda e, c0=c0, n=n: e.tensor_copy(out=Gm[:, c0:c0 + n], in_=hflat[:, 0:n]), r=HKEYS, w=["Gm"])
        for ct in range(NCT):
            dma("sp", hflat[:, 0:128], c_ov[:, ct, :], w=HKEYS)
            for g in range(2):
                op("dve", lambda e, ct=ct, g=g: e.tensor_copy(out=CR[:, ct, g, 0:128], in_=hflat[:, 0:128]), r=HKEYS, w=["CR"])
                op("dve", lambda e, ct=ct, g=g: e.memset(CR[:, ct, g, 192:194], 1.0), w=["CR"])
        op("pool", lambda e: e.memset(Vs[:, :, 64:128], 1.0), w=["Vs1"])
        op("pool", lambda e: e.memset(KCT[:, :], 0.0), w=["KCT"])

        with ExitStack() as es0:
            STF = [sb("STF%d" % i, [128, DFF], F32, es0) for i in range(2)]
            STB = [sb("STB%d" % i, [128, DFF], BF16, es0) for i in range(2)]
            cvi = {"i": 0}

            def conv(src, dst, ncols, nk, gain=None, scale=1.0, dkey=None):
                i = cvi["i"] % 2
                eng = "dve" if (cvi["i"] % 3) != 2 else "pool"
                cvi["i"] += 1
                inner = ncols // nk
                sf = STF[i][:, 0:ncols].rearrange("p (k n) -> p k n", k=nk)
                sbb = STB[i][:, 0:ncols].rearrange("p (k n) -> p k n", k=nk)
                dma("sp", sf, src, w=[("STF", i)])
                if gain is not None:
                    op(eng, lambda e: e.tensor_tensor(out=sbb, in0=sf, in1=gain.unsqueeze(2).to_broadcast([128, nk, inner]),
                                                      op=ALU.mult), r=[("STF", i), "gn", "gq"], w=[("STB", i)])
                elif scale != 1.0:
                    op(eng, lambda e: e.tensor_scalar(out=sbb, in0=sf, scalar1=scale, scalar2=None, op0=ALU.mult),
                       r=[("STF", i)], w=[("STB", i)])
                else:
                    op(eng, lambda e: e.tensor_copy(out=sbb, in_=sf), r=[("STF", i)], w=[("STB", i)])
                dma("pool", dst, STB[i][:, 0:ncols], r=[("STB", i)], w=[dkey])

            def kview(wsrc, c0, c1):
                return wsrc.rearrange("(k p) n -> p k n", p=128)[:, :, c0:c1]

            def conv_ffn(wg, wu, wd, sg, su, sd, gi, tag):
                for j in range(NJ):
                    conv(kview(wg, j * 128, (j + 1) * 128), sg[j, :, :], 1024, 8, gain=gn_t[:, gi, :], dkey=(tag + "g", j))
                    conv(kview(wu, j * 128, (j + 1) * 128), su[j, :, :], 1024, 8, gain=gn_t[:, gi, :], dkey=(tag + "u", j))
                for d in range(8):
                    conv(wd.rearrange("(j p) n -> p j n", p=128)[:, :, d * 128:(d + 1) * 128], sd[d, :, :], DFF, NJ,
                         scale=0.5, dkey=(tag + "d", d))

            conv_ffn(w_f1g, w_f1u, w_f1d, s_f1g, s_f1u, s_f1d, 0, "f1")
            for c in range(WIN_CHUNKS):
                if c in (C_V, C_V + 1):
                    continue
                conv(kview(w_in, c * 128, (c + 1) * 128), s_in[c, :, :], 1024, 8,
                     gain=(gq_t[:, :] if c < 8 else gn_t[:, 1, :]), dkey=("in", c))
            conv(kview(w_in, C_V * 128, (C_V + 2) * 128), s_v[:, :], 2048, 8, gain=gn_t[:, 1, :], dkey="s_v")
            conv(kview(w_gn, 0, 24), s_gn[:, :], 192, 8, gain=gn_t[:, 1, :], dkey="s_gn")
            for d in range(8):
                conv(w_pa.rearrange("(k p) n -> p k n", p=128)[:, :, d * 128:(d + 1) * 128], s_pa[d, :, :], 512, 4, dkey=("pa", d))
                conv(w_pb.rearrange("(k p) n -> p k n", p=128)[:, :, d * 128:(d + 1) * 128], s_pb[d, :, :], 512, 4, dkey=("pb", d))
                conv(kview(w_o, d * 128, (d + 1) * 128), s_o[d, :, :], 1024, 8, dkey=("o", d))
                conv(kview(w_pg, d * 128, (d + 1) * 128), s_pg[d, :, :], 1024, 8, gain=gn_t[:, 3, :], dkey=("pg", d))
                conv(w_pp.rearrange("(k p) n -> p k n", p=128)[:, :, d * 128:(d + 1) * 128], s_pp[d, :, :], 256, 2, dkey=("pp", d))
            conv_ffn(w_f2g, w_f2u, w_f2d, s_f2g, s_f2u, s_f2d, 2, "f2")
            Sx.barrier()

        def rms_norm(out_bf, okey, apply=True):
            op("pool", lambda e: e.tensor_tensor(out=out_bf[:, 0:4, :], in0=HT[:, 0:4, :], in1=HT[:, 0:4, :], op=ALU.mult),
               r=HKEYS, w=[(okey, 0)])
            op("dve", lambda e: e.tensor_tensor(out=out_bf[:, 4:8, :], in0=HT[:, 4:8, :], in1=HT[:, 4:8, :], op=ALU.mult),
               r=HKEYS, w=[(okey, 1)])
            b = psd()
            for k in range(8):
                op("pe", lambda e, k=k: e.matmul(PS(b), lhsT=onesb[:, :], rhs=out_bf[:, k, :], start=(k == 0), stop=(k == 7)),
                   r=[(okey, k // 4), "onesb"], w=[("PS", b)])
            op("act", lambda e: e.activation(out=LNV[:, :], in_=PS(b), func=AF.Ln, bias=epsc[:, :], scale=1.0 / D),
               r=[("PS", b), "epsc"], w=["TB"])
            op("act", lambda e: e.activation(out=RS[:, :], in_=LNV[:, :], func=AF.Exp, scale=-0.5), r=["TB"], w=["RS"])
            if not apply:
                return
            rb = RS[:, :].unsqueeze(1).to_broadcast([128, 4, 512])
            op("dve", lambda e: e.tensor_tensor(out=out_bf[:, 0:4, :], in0=HT[:, 0:4, :], in1=rb, op=ALU.mult),
               r=HKEYS + ["RS"], w=[(okey, 0)])
            op("pool", lambda e: e.tensor_tensor(out=out_bf[:, 4:8, :], in0=HT[:, 4:8, :], in1=rb, op=ALU.mult),
               r=HKEYS + ["RS"], w=[(okey, 1)])

        UNK = [("UN", 0), ("UN", 1)]

        def fm_proj(src_dram, skey, xin, xkeys, nk, ncols, consume, ntok=512):
            wt, wk = wslot(src_dram, ncols, skey)
            b = psd()
            wv = wt[:, 0:ncols].rearrange("p (k n) -> p k n", k=nk)
            for k in range(nk):
                op("pe", lambda e, k=k: e.matmul(PS(b)[:, 0:ntok], lhsT=wv[:, k, :], rhs=xin[:, k, :], start=(k == 0), stop=(k == nk - 1)),
                   r=[wk] + xkeys, w=[("PS", b)])
            consume(b)

        def ffn(sg, su, sd, tag):
            rms_norm(UN, "UN")
            for j in range(NJ):
                wg, wgk = wslot(sg[j, :, :], 1024, (tag + "g", j))
                wu, wuk = wslot(su[j, :, :], 1024, (tag + "u", j))
                bg, bu = psd(), psd()
                wgv = wg[:, :].rearrange("p (k n) -> p k n", k=8)
                wuv = wu[:, :].rearrange("p (k n) -> p k n", k=8)
                for k in range(8):
                    op("pe", lambda e, k=k: e.matmul(PS(bg), lhsT=wgv[:, k, :], rhs=UN[:, k, :], start=(k == 0), stop=(k == 7)),
                       r=[wgk] + UNK, w=[("PS", bg)])
                for k in range(8):
                    op("pe", lambda e, k=k: e.matmul(PS(bu), lhsT=wuv[:, k, :], rhs=UN[:, k, :], start=(k == 0), stop=(k == 7)),
                       r=[wuk] + UNK, w=[("PS", bu)])
                sgt = SG[j % 2]
                op("act", lambda e: e.activation(out=sgt[:, :], in_=PS(bg), func=AF.Silu), r=[("PS", bg)], w=[("SG", j % 2)])
                op("dve", lambda e: e.tensor_tensor(out=BIG[:, j, :], in0=sgt[:, :], in1=PS(bu), op=ALU.mult),
                   r=[("SG", j % 2), ("PS", bu)], w=[("BIG", j)])
            for d in range(8):
                b = psd()
                for (j0, j1) in ((0, 8), (8, 16), (16, NJ)):
                    wd, wdk = wslot(sd[d, :, j0 * 128:j1 * 128], (j1 - j0) * 128, (tag + "d", d))
                    wdv = wd[:, 0:(j1 - j0) * 128].rearrange("p (j n) -> p j n", j=j1 - j0)
                    for j in range(j0, j1):
                        op("pe", lambda e, j=j, wdv=wdv, j0=j0: e.matmul(PS(b), lhsT=wdv[:, j - j0, :], rhs=BIG[:, j, :],
                                                                        start=(j == 0), stop=(j == NJ - 1)),
                           r=[wdk, ("BIG", j)], w=[("PS", b)])
                op("dve", lambda e, d=d: e.tensor_tensor(out=HT[:, d, :], in0=HT[:, d, :], in1=PS(b), op=ALU.add),
                   r=[(HK, d), ("PS", b)], w=[(HK, d)])

        SIN_SCALE = 2.0 * math.pi * 0.999999

        def rope_tables(pos_ap, ntok):
            dma("sp", PI[:, 0:ntok], pos_ap.partition_broadcast(128), w=["TI"])
            op("dve", lambda e: e.tensor_copy(out=TA[:, 0:ntok], in_=PI[:, 0:ntok]), r=["TI"], w=["TA"])
            op("dve", lambda e: e.tensor_scalar(out=TA[:, 0:ntok], in0=TA[:, 0:ntok], scalar1=sc_t[:, 0:1], scalar2=None,
                                                op0=ALU.mult), r=["TA", "sc"], w=["TA"])
            for (dst, shift, scale_ap, nm) in ((SN, 0.0, sc_t[:, 1:2], "SN"), (CS, 0.25, SIN_SCALE, "CS")):
                eng = "dve"
                op(eng, lambda e, shift=shift: e.tensor_scalar(out=TB[:, 0:ntok], in0=TA[:, 0:ntok], scalar1=shift, scalar2=None,
                                                               op0=ALU.add), r=["TA"], w=["TB"])
                op(eng, lambda e: e.tensor_copy(out=TI[:, 0:ntok], in_=TB[:, 0:ntok]), r=["TB"], w=["TI"])
                op(eng, lambda e, dst=dst: e.tensor_copy(out=dst[:, 0:ntok], in_=TI[:, 0:ntok]), r=["TI"], w=[nm])
                op(eng, lambda e, dst=dst: e.tensor_tensor(out=TB[:, 0:ntok], in0=TB[:, 0:ntok], in1=dst[:, 0:ntok], op=ALU.subtract),
                   r=["TB", nm], w=["TB"])
                op(eng, lambda e, dst=dst: e.tensor_scalar(out=dst[:, 0:ntok], in0=TB[:, 0:ntok], scalar1=0.5, scalar2=None,
                                                           op0=ALU.is_gt), r=["TB"], w=[nm])
                op(eng, lambda e, dst=dst: e.tensor_tensor(out=TB[:, 0:ntok], in0=TB[:, 0:ntok], in1=dst[:, 0:ntok], op=ALU.subtract),
                   r=["TB", nm], w=["TB"])
                op(eng, lambda e, dst=dst: e.tensor_scalar(out=dst[:, 0:ntok], in0=TB[:, 0:ntok], scalar1=-0.5, scalar2=None,
                                                           op0=ALU.is_lt), r=["TB"], w=[nm])
                op(eng, lambda e, dst=dst: e.tensor_tensor(out=TB[:, 0:ntok], in0=TB[:, 0:ntok], in1=dst[:, 0:ntok], op=ALU.add),
                   r=["TB", nm], w=["TB"])
                op("act", lambda e, dst=dst, scale_ap=scale_ap: e.activation(out=dst[:, 0:ntok], in_=TB[:, 0:ntok], func=AF.Sin,
                                                                             scale=scale_ap), r=["TB", "sc"], w=[nm])

        def rope_apply(bp, bs, out_ap, okeys, ntok):
            op("dve", lambda e: e.tensor_tensor(out=TA[:, 0:ntok], in0=PS(bp)[:, 0:ntok], in1=CS[:, 0:ntok], op=ALU.mult),
               r=[("PS", bp), "CS"], w=["TA"])
            op("dve", lambda e: e.tensor_tensor(out=TB[:, 0:ntok], in0=PS(bs)[:, 0:ntok], in1=SN[:, 0:ntok], op=ALU.mult),
               r=[("PS", bs), "SN"], w=["TB"])
            op("pool", lambda e: e.tensor_tensor(out=out_ap, in0=TA[:, 0:ntok], in1=TB[:, 0:ntok], op=ALU.add),
               r=["TA", "TB"], w=okeys)

        with ExitStack() as es1:
            KcT = sb("KcT", [128, S], BF16, es1)
            VcT = sb("VcT", [128, S], BF16, es1)
            KWst = sb("KWst", [128, 512], BF16, es1)
            VWst = sb("VWst", [128, 4, 192], BF16, es1)
            op("pool", lambda e: e.memset(VWst[:, :, 64:128], 1.0), w=["VWst"])
            for t in range(NT1 if stage >= 1 else 0):
                tok = slice(t * 512, (t + 1) * 512)
                dma("sp", HT[:, :, :], xT[:, :, tok].rearrange("c p t -> p c t"), w=HKEYS)
                ffn(s_f1g, s_f1u, s_f1d, "f1")
                if sub < 2:
                    continue
                dma("pool", h1s[:, :, tok].rearrange("c p t -> p c t"), HT[:, :, :], r=HKEYS, w=[("h1s", t)])
                op("pool", lambda e, t=t: e.tensor_copy(out=HALOALL[:, :, t * 4:(t + 1) * 4, :],
                                                        in_=HT[:, :, :].rearrange("p c (b t) -> p c b t", t=128)[:, :, :, 126:128]),
                   r=HKEYS, w=["HALOALL"])
                if sub < 3:
                    continue
                rms_norm(UN, "UN")
                rope_tables(posg[:, tok], 512)
                if sub < 4:
                    continue
                for (cp, cs_, dst_ap, keys) in ((C_KC, C_KCS, KcT[:, tok], [("KcT", t)]), (C_KS, C_KSS, KsT[:, tok], [("KsT", t)]),
                                                (C_KW, C_KWS, KWst[:, :], ["KWst"])):
                    banks = []
                    for c in (cp, cs_):
                        fm_proj(s_in[c, :, :], ("in", c), UN, UNK, 8, 1024, lambda b: banks.append(b))
                    rope_apply(banks[0], banks[1], dst_ap, keys, 512)
                dma("pool", s_kw[:, tok], KWst[:, :], r=["KWst"], w=[("s_kw", t)])
                if sub < 5:
                    continue
                fm_proj(s_in[C_VC, :, :], ("in", C_VC), UN, UNK, 8, 1024,
                        lambda b: op("act", lambda e: e.copy(out=VcT[:, tok], in_=PS(b)), r=[("PS", b)], w=[("VcT", t)]))
                if sub < 6:
                    continue
                wvt0, wvk0 = wslot(s_v[:, 0:1024], 1024, "s_v")
                wvt1, wvk1 = wslot(s_v[:, 1024:2048], 1024, "s_v")
                for bl in range(4):
                    gb_ = t * 4 + bl
                    b = psd()
                    for k in range(8):
                        wt_, wk_ = (wvt0, wvk0) if k < 4 else (wvt1, wvk1)
                        kk = k % 4
                        op("pe", lambda e, k=k, kk=kk, wt_=wt_, bl=bl, b=b: e.matmul(
                            PS(b)[:, 0:256], lhsT=UN[:, k, bl * 128:(bl + 1) * 128], rhs=wt_[:, kk * 256:(kk + 1) * 256],
                            start=(k == 0), stop=(k == 7)), r=[wk_] + UNK, w=[("PS", b)])
                    op("act", lambda e, gb_=gb_, b=b: e.copy(out=Vs[:, gb_, 0:64], in_=PS(b)[:, 0:64]), r=[("PS", b)], w=[("Vs", gb_)])
                    op("act", lambda e, gb_=gb_, b=b: e.copy(out=Vs[:, gb_, 128:192], in_=PS(b)[:, 64:128]), r=[("PS", b)], w=[("Vs", gb_)])
                    if sub < 7:
                        continue
                    op("act", lambda e, bl=bl, b=b: e.copy(out=VWst[:, bl, 0:64], in_=PS(b)[:, 128:192]), r=[("PS", b)], w=["VWst"])
                    op("act", lambda e, bl=bl, b=b: e.copy(out=VWst[:, bl, 128:192], in_=PS(b)[:, 192:256]), r=[("PS", b)], w=["VWst"])
                if sub < 7:
                    continue
                dma("pool", s_vw[:, t * 4:(t + 1) * 4, :], VWst[:, :, :], r=["VWst"], w=[("s_vw", t)])

            with ExitStack() as es15:
                W1b_ = sb("W1b_", [128, 32 * 256], BF16, es15)
                W2f = sb("W2f", [128, 2, 64], F32, es15)
                W2p = sb("W2p", [128, 2, 128], BF16, es15)
                B1 = sb("B1", [128, 2], F32, es15)
                B1e = sb("B1e", [128, 2], F32, es15)
                POSf = sb("POSf", [128, 32], F32, es15)
                POSb = sb("POSb", [128, 32], BF16, es15)
                HID = sb("HID", [128, 2, NCP], BF16, es15)
                HF, HG = TA, CS
                allK = [("KcT", t) for t in range(NT1)]
                allV = [("VcT", t) for t in range(NT1)]
                op("dve", lambda e: e.memset(HID[:, :, :], 0.0), w=["HID"])
                for kind in range(2 if stage >= 2 else 0):
                    srcT, skeys = (KcT, allK) if kind == 0 else (VcT, allV)
                    for q2 in range(2):
                        dma("sp", hflat[:, 0:4096], cw1[kind][:, q2 * 4096:(q2 + 1) * 4096], w=HKEYS)
                        op("dve", lambda e, q2=q2: e.tensor_copy(out=W1b_[:, q2 * 4096:(q2 + 1) * 4096], in_=hflat[:, 0:4096]),
                           r=HKEYS, w=["W1b"])
                    dma("sp", W2f[:, :, :], cw2[kind][:, :, :], w=["W2f"])
                    dma("sp", B1[:, :], cb1[kind][:, :], w=["B1"])
                    dma("sp", POSf[:, :], cpos[kind][:, :], w=["POSf"])
                    op("dve", lambda e: e.tensor_copy(out=W2p[:, :, 0:64], in_=W2f[:, :, :]), r=["W2f"], w=["W2p"])
                    op("dve", lambda e: e.tensor_copy(out=W2p[:, :, 64:128], in_=W2f[:, :, :]), r=["W2f"], w=["W2p"])
                    op("dve", lambda e: e.tensor_copy(out=POSb[:, :], in_=POSf[:, :]), r=["POSf"], w=["POSb"])
                    w1v = W1b_[:, :].rearrange("p (l n) -> p l n", l=32)
                    for hh in range(2):
                        b = psd()
                        for l in range(32):
                            op("pe", lambda e, l=l, hh=hh, b=b: e.matmul(PS(b)[:, 0:1], lhsT=w1v[0:64, l, hh * 128:(hh + 1) * 128],
                                                                       rhs=POSb[0:64, l:l + 1], start=(l == 0), stop=(l == 31)),
                               r=["W1b", "POSb"], w=[("PS", b)])
                        op("dve", lambda e, hh=hh, b=b: e.tensor_tensor(out=B1e[:, hh:hh + 1], in0=PS(b)[:, 0:1], in1=B1[:, hh:hh + 1],
                                                                      op=ALU.add), r=[("PS", b), "B1"], w=["B1e"])
                    for g in range(2):
                        gp = slice(64 * g, 64 * g + 64)
                        for hh in range(2):
                            b = psd()
                            for l in range(32):
                                rhs = srcT[gp, l:l + 16 * (NCB - 1) + 1:16]
                                op("pe", lambda e, l=l, hh=hh, b=b, rhs=rhs, gp=gp: e.matmul(
                                    PS(b)[:, 0:NCB], lhsT=w1v[gp, l, hh * 128:(hh + 1) * 128], rhs=rhs,
                                    start=(l == 0), stop=(l == 31)), r=["W1b"] + skeys, w=[("PS", b)])
                            op("act", lambda e, hh=hh, b=b: e.activation(out=HF[:, 0:NCB], in_=PS(b)[:, 0:NCB], func=AF.Identity,
                                                                        bias=B1e[:, hh:hh + 1], scale=1.0), r=[("PS", b), "B1e"], w=["TA"])
                            op("dve", lambda e: e.tensor_tensor(out=HG[:, 0:NCB], in0=HF[:, 0:NCB], in1=HF[:, 0:NCB], op=ALU.mult),
                               r=["TA"], w=["CS"])
                            op("dve", lambda e: e.tensor_scalar(out=HG[:, 0:NCB], in0=HG[:, 0:NCB], scalar1=0.044715, scalar2=1.0,
                                                                op0=ALU.mult, op1=ALU.add), r=["CS"], w=["CS"])
                            op("dve", lambda e: e.tensor_tensor(out=HG[:, 0:NCB], in0=HG[:, 0:NCB], in1=HF[:, 0:NCB], op=ALU.mult),
                               r=["CS", "TA"], w=["CS"])
                            op("act", lambda e: e.activation(out=HG[:, 0:NCB], in_=HG[:, 0:NCB], func=AF.Sigmoid, scale=1.5957691216),
                               r=["CS"], w=["CS"])
                            op("dve", lambda e, hh=hh: e.tensor_tensor(out=HID[:, hh, 0:NCB], in0=HG[:, 0:NCB], in1=HF[:, 0:NCB], op=ALU.mult),
                               r=["CS", "TA"], w=["HID"])
                        if kind == 0:
                            b = psd()
                            for hh in range(2):
                                op("pe", lambda e, hh=hh, b=b: e.matmul(PS(b)[:, 0:NCB], lhsT=W2p[:, hh, :], rhs=HID[:, hh, 0:NCB],
                                                                       start=(hh == 0), stop=(hh == 1)), r=["W2p", "HID"], w=[("PS", b)])
                            op("act", lambda e, b=b, gp=gp: e.copy(out=KCT[gp, 0:NCB], in_=PS(b)[gp, 0:NCB]), r=[("PS", b)], w=["KCT"])
                        else:
                            for ct in range(NCT):
                                b = psd()
                                for hh in range(2):
                                    op("pe", lambda e, hh=hh, b=b, ct=ct: e.matmul(PS(b)[:, 0:64], lhsT=HID[:, hh, ct * 128:(ct + 1) * 128],
                                                                                 rhs=W2p[:, hh, 0:64], start=(hh == 0), stop=(hh == 1)),
                                       r=["W2p", "HID"], w=[("PS", b)])
                                op("act", lambda e, b=b, ct=ct, g=g: e.copy(out=CR[:, ct, g, 128:192], in_=PS(b)[:, 0:64]),
                                   r=[("PS", b)], w=["CR"])
                Sx.barrier()
            Sx.barrier()

        NH = 2 * NOWN
        with ExitStack() as es2:
            HCU = sb("HCU", [128, 4, NH], F32, es2)
            def halo_pass(esh):
                HALO = sb("HALO", [128, 8, NOWN, 2], F32, esh)
                HUN = sb("HUN", [128, 8, NH], BF16, esh)
                HSQ = sb("HSQ", [128, 8, NH], BF16, esh)
                HTMP = sb("HTMP", [128, NH], F32, esh)
                op("dve", lambda e: e.memset(HALO[:, :, :, :], 0.0), w=["HALO"])
                if NOWN > 1:
                    op("dve", lambda e: e.tensor_scalar(out=HALO[:, :, 1:NOWN, :], in0=HALOALL[:, :, 1:NB - 1:2, :], scalar1=sc_t[:, 2:3],
                                                        scalar2=None, op0=ALU.mult), r=["HALOALL", "sc", "HALO"], w=["HALO"])
                op("dve", lambda e: e.scalar_tensor_tensor(out=HALO[:, :, :, :], in0=HALOALL[:, :, 0:NB:2, :], scalar=sc_t[:, 3:4],
                                                           in1=HALO[:, :, :, :], op0=ALU.mult, op1=ALU.add),
                   r=["HALOALL", "HALO", "sc"], w=["HALO"])
                hal = HALO[:, :, :, :].rearrange("p c n t -> p c (n t)")
                op("dve", lambda e: e.tensor_tensor(out=HSQ[:, :, :], in0=hal, in1=hal, op=ALU.mult), r=["HALO"], w=["HSQ"])
                b = psd()
                for k in range(8):
                    op("pe", lambda e, k=k, b=b: e.matmul(PS(b)[:, 0:NH], lhsT=onesb[:, :], rhs=HSQ[:, k, :], start=(k == 0), stop=(k == 7)),
                       r=["HSQ", "onesb"], w=[("PS", b)])
                op("act", lambda e, b=b: e.activation(out=HTMP[:, :], in_=PS(b)[:, 0:NH], func=AF.Ln, bias=epsc[:, :], scale=1.0 / D),
                   r=[("PS", b), "epsc"], w=["HTMP"])
                op("act", lambda e: e.activation(out=HTMP[:, :], in_=HTMP[:, :], func=AF.Exp, scale=-0.5), r=["HTMP"], w=["HTMP"])
                op("dve", lambda e: e.tensor_tensor(out=HUN[:, :, :], in0=hal, in1=HTMP[:, :].unsqueeze(1).to_broadcast([128, 8, NH]),
                                                    op=ALU.mult), r=["HALO", "HTMP"], w=["HUN"])
                for c4 in range(4):
                    banks = []
                    for cbase in (C_CC, C_CX):
                        fm_proj(s_in[cbase + c4, :, :], ("in", cbase + c4), HUN, ["HUN"], 8, 1024, lambda b: banks.append(b), ntok=NH)
                    op("act", lambda e, b0=banks[0]: e.copy(out=HTMP[:, :], in_=PS(b0)[:, 0:NH]), r=[("PS", banks[0])], w=["HTMP"])
                    op("dve", lambda e, c4=c4, b1=banks[1]: e.tensor_tensor(out=HCU[:, c4, :], in0=HTMP[:, :], in1=PS(b1)[:, 0:NH], op=ALU.mult),
                       r=["HTMP", ("PS", banks[1])], w=["HCU"])
                Sx.barrier()

            if stage >= 3:
                with ExitStack() as esh_:
                    halo_pass(esh_)

            CAND = sb("CAND", [128, 8, 128], F32, es2)
            QT = sb("QT", [128, 4, 512], BF16, es2)
            SIGN = sb("SIGN", [128, 4, 24], F32, es2)
            CBt = sb("CBt", [128, 512], BF16, es2)
            CU = sb("CU", [128, 4, 130], F32, es2)
            CY = sb("CY", [128, 512], F32, es2)
            ONT = sb("ONT", [128, 4, 512], BF16, es2)
            ONSA = sb("ONSA", [128, 512], F32, es2)
            PTb = [sb("PTb%d" % i, [128, 512], BF16, es2) for i in range(3)]
            BiasT = [sb("BiasT%d" % i, [128, 4, 128], BF16, es2) for i in range(2)]
            CBI = [sb("CBI%d" % i, [128, 4, 128], BF16, es2) for i in range(2)]
            OTs = sb("OTs", [128, 512], F32, es2)
            AFm = sb("AFm", [128, 128], F32, es2)
            FFm = sb("FFm", [128, 128], F32, es2)
            IMP = sb("IMP", [128, 128], F32, es2)
            SC1 = sb("SC1", [128, 128], F32, es2)
            SC2 = sb("SC2", [128, 128], F32, es2)
            BROW = [sb("BROW%d" % i, [128, 128], F32, es2) for i in range(2)]
            M8a = sb("M8a", [128, 8], F32, es2)
            M8b = sb("M8b", [128, 8], F32, es2)
            R4 = sb("R4", [128, 4], F32, es2)
            W4 = sb("W4", [128, 4], F32, es2)
            PTlb = sb("PTlb", [128, 2, 512], BF16, es2)
            KWt = [sb("KWt%d" % i, [128, 6 * 128], BF16, es2) for i in range(2)]
            VWt = [sb("VWt%d" % i, [128, 6, 192], BF16, es2) for i in range(2)]
            pt_i = {"i": 0, "cb": 0}
            h1v = h1s.rearrange("c p (n t) -> p c n t", t=128)
            h1keys = [("h1s", t) for t in range(NT1)]
            kwkeys = [("s_kw", t) for t in range(NT1)]
            vwkeys = [("s_vw", t) for t in range(NT1)]

            def attn_tiles(i, bl, g, kind, qrhs, qkeys, win):
                gp = slice(64 * g, 64 * g + 64)
                sgkey = ("SIGN", bl)
                if kind == 0:
                    kts = list(range(0, 2 * i + 2))
                    ob, gcol = 4, 1
                else:
                    kts = [k for k in range(2 * i - 4, 2 * i + 2) if k >= 0]
                    ob, gcol = 5, 2
                    wslot_i, kt0 = win
                LAG = 2
                pend = []

                def stage_a(n_, kt):
                    b = pst()
                    extra = []
                    if kind == 0:
                        extra.append((Gm[:, kt * 128:(kt + 1) * 128], BiasT[g][:, :, :], ["Gm", ("BiasT", g)]))
                        klhs = KsT[gp, kt * 128:(kt + 1) * 128]
                        kkeys = [("KsT", kt // 4)]
                        vl = Vs[:, kt, 64 * g:64 * g + 128]
                        vkeys = [("Vs", kt), "Vs1"]
                    else:
                        kl = kt - kt0
                        klhs = KWt[wslot_i][gp, kl * 128:(kl + 1) * 128]
                        kkeys = [("KWt", wslot_i)]
                        vl = VWt[wslot_i][:, kl, 64 * g:64 * g + 128]
                        vkeys = [("VWt", wslot_i)]
                    rel = kt - 2 * i
                    if rel == 0:
                        extra.append((identb[:, :], BAb[:, :, :], ["identb", "BA"]))
                    elif rel == 1:
                        extra.append((identb[:, :], BBb[:, :, :], ["identb", "BB"]))
                    elif kind == 1 and rel == -4:
                        extra.append((identb[:, :], W0b[:, :, :], ["identb", "W0"]))
                    elif kind == 1 and rel == -3:
                        extra.append((identb[:, :], W1b[:, :, :], ["identb", "W1"]))
                    ne = len(extra)
                    op("pe", lambda e, b=b, klhs=klhs, ne=ne: e.matmul(PS(b), lhsT=klhs, rhs=qrhs, start=True, stop=(ne == 0)),
                       r=kkeys + qkeys, w=[("PS", b)])
                    for xi, (l_, r_, ks_) in enumerate(extra):
                        op("pe", lambda e, b=b, l_=l_, r_=r_, xi=xi, ne=ne: e.matmul(PS(b), lhsT=l_, rhs=r_, start=False, stop=(xi == ne - 1)),
                           r=ks_, w=[("PS", b)])
                    pi_ = pt_i["i"]
                    pt_i["i"] = (pi_ + 1) % 3
                    op("act", lambda e, b=b, pi_=pi_: e.activation(out=PTb[pi_][:, :], in_=PS(b), func=AF.Exp), r=[("PS", b)], w=[("PTb", pi_)])
                    pend.append((n_, pi_, vl, vkeys))

                def stage_b():
                    n_, pi_, vl, vkeys = pend.pop(0)
                    nk_ = len(kts)
                    op("pe", lambda e, pi_=pi_, vl=vl, n_=n_, nk_=nk_: e.matmul(PS(ob), lhsT=vl, rhs=PTb[pi_][:, :],
                                                                              start=(n_ == 0), stop=(n_ == nk_ - 1)),
                       r=vkeys + [("PTb", pi_)], w=[("PS", ob)])

                for n_, kt in enumerate(kts):
                    stage_a(n_, kt)
                    if len(pend) > LAG:
                        stage_b()
                while pend:
                    stage_b()
                op("act", lambda e: e.copy(out=OTs[:, :], in_=PS(ob)), r=[("PS", ob)], w=["OTs"])
                for h in range(4):
                    op("pe", lambda e, h=h: e.transpose(PS(3)[:, h * 128:(h + 1) * 128], OTs[:, h * 128:(h + 1) * 128], ident[:, :]),
                       r=["OTs", "ident"], w=[("PS", 3)])
                p3 = PS(3).rearrange("p (h n) -> p h n", h=4)
                oc, dc = 64 * g, 64 * (1 - g)
                op("dve", lambda e: e.tensor_scalar(out=R4[:, :], in0=p3[:, :, dc], scalar1=1e-30, scalar2=None, op0=ALU.max),
                   r=[("PS", 3)], w=["R4"])
                op("dve", lambda e: e.reciprocal(out=R4[:, :], in_=R4[:, :]), r=["R4"], w=["R4"])
                op("dve", lambda e: e.tensor_tensor(out=W4[:, :], in0=R4[:, :], in1=SIGN[:, bl, g * 12 + gcol:g * 12 + 12:3], op=ALU.mult),
                   r=["R4", sgkey], w=["W4"])
                for h in range(4):
                    hc_ = slice((g * 4 + h) * 64, (g * 4 + h) * 64 + 64)
                    op("dve", lambda e, h=h, hc_=hc_: e.scalar_tensor_tensor(out=ONSA[:, hc_], in0=p3[:, h, oc:oc + 64], scalar=W4[:, h:h + 1],
                                                                            in1=ONSA[:, hc_], op0=ALU.mult, op1=ALU.add),
                       r=[("PS", 3), "W4", "ONSA"], w=["ONSA"])

            for t in range(NT2 if stage >= 4 else 0):
                otok = slice(t * 512, (t + 1) * 512)
                for bl in range(4):
                    i = t * 4 + bl
                    hb_ = HT[:, :, bl * 128:(bl + 1) * 128]
                    dma("sp", hb_, h1v[:, :, 2 * i, :], r=h1keys, w=HKEYS)
                    dma("sp", CAND[:, :, :], h1v[:, :, 2 * i + 1, :], r=h1keys, w=["CAND"])
                    op("dve", lambda e, hb_=hb_: e.tensor_scalar(out=hb_, in0=hb_, scalar1=sc_t[:, 2:3], scalar2=None, op0=ALU.mult),
                       r=HKEYS + ["sc"], w=HKEYS)
                    op("dve", lambda e, hb_=hb_: e.scalar_tensor_tensor(out=hb_, in0=CAND[:, :, :], scalar=sc_t[:, 3:4], in1=hb_,
                                                                        op0=ALU.mult, op1=ALU.add), r=["CAND", "sc"] + HKEYS, w=HKEYS)
                rms_norm(UN, "UN")
                rope_tables(poso[:, otok], 512)
                for c in range(4):
                    banks = []
                    for cc_ in (C_Q + c, C_QS + c):
                        fm_proj(s_in[cc_, :, :], ("in", cc_), UN, UNK, 8, 1024, lambda b: banks.append(b))
                    rope_apply(banks[0], banks[1], QT[:, c, :], [("QT", c)], 512)
                wgt, wgk = wslot(s_gn[:, :], 192, "s_gn")
                for bl in range(4):
                    b = psd()
                    for k in range(8):
                        op("pe", lambda e, k=k, bl=bl, b=b: e.matmul(PS(b)[:, 0:24], lhsT=UN[:, k, bl * 128:(bl + 1) * 128],
                                                                    rhs=wgt[:, k * 24:(k + 1) * 24], start=(k == 0), stop=(k == 7)),
                           r=[wgk] + UNK, w=[("PS", b)])
                    op("act", lambda e, bl=bl, b=b: e.activation(out=SIGN[:, bl, :], in_=PS(b)[:, 0:24], func=AF.Sigmoid),
                       r=[("PS", b)], w=[("SIGN", bl)])
                for c4 in range(4):
                    fm_proj(s_in[C_CB + c4, :, :], ("in", C_CB + c4), UN, UNK, 8, 1024,
                            lambda b: op("act", lambda e: e.copy(out=CBt[:, :], in_=PS(b)), r=[("PS", b)], w=["CBt"]))
                    banks = []
                    fm_proj(s_in[C_CC + c4, :, :], ("in", C_CC + c4), UN, UNK, 8, 1024, lambda b: banks.append(b))
                    fm_proj(s_in[C_CX + c4, :, :], ("in", C_CX + c4), UN, UNK, 8, 1024, lambda b: banks.append(b))
                    op("act", lambda e, b0=banks[0]: e.copy(out=CY[:, :], in_=PS(b0)), r=[("PS", banks[0])], w=["CY"])
                    op("dve", lambda e, b1=banks[1]: e.tensor_tensor(
                        out=CU[:, :, 2:130], in0=CY[:, :].rearrange("p (b t) -> p b t", b=4),
                        in1=PS(b1).rearrange("p (b t) -> p b t", b=4), op=ALU.mult), r=["CY", ("PS", banks[1])], w=["CU"])
                    op("pool", lambda e, c4=c4, t=t: e.tensor_copy(out=CU[:, :, 0:2],
                                                                   in_=HCU[:, c4, t * 8:(t + 1) * 8].rearrange("p (b t) -> p b t", t=2)),
                       r=["HCU"], w=["CU"])
                    cy3 = CY[:, :].rearrange("p (b t) -> p b t", b=4)
                    op("dve", lambda e, c4=c4, cy3=cy3: e.tensor_scalar(out=cy3, in0=CU[:, :, 0:128], scalar1=cvp[:, c4, 0:1],
                                                                        scalar2=cvp[:, c4, 3:4], op0=ALU.mult, op1=ALU.add),
                       r=["CU", "cvp"], w=["CY"])
                    op("dve", lambda e, c4=c4, cy3=cy3: e.scalar_tensor_tensor(out=cy3, in0=CU[:, :, 1:129], scalar=cvp[:, c4, 1:2], in1=cy3,
                                                                               op0=ALU.mult, op1=ALU.add), r=["CU", "cvp", "CY"], w=["CY"])
                    op("dve", lambda e, c4=c4, cy3=cy3: e.scalar_tensor_tensor(out=cy3, in0=CU[:, :, 2:130], scalar=cvp[:, c4, 2:3], in1=cy3,
                                                                               op0=ALU.mult, op1=ALU.add), r=["CU", "cvp", "CY"], w=["CY"])
                    op("pool", lambda e, c4=c4: e.tensor_tensor(out=BIG[:, 16 + c4, :], in0=CY[:, :], in1=CBt[:, :], op=ALU.mult),
                       r=["CY", "CBt"], w=[("BIG", 16 + c4)])
                for which, cbase in ((0, C_GA), (1, C_GB)):
                    for d in range(8):
                        fm_proj(s_in[cbase + d, :, :], ("in", cbase + d), UN, UNK, 8, 1024,
                                lambda b, which=which, d=d: op("act", lambda e: e.activation(out=BIG[:, which * 8 + d, :], in_=PS(b), func=AF.Sigmoid),
                                                               r=[("PS", b)], w=[("BIG", which * 8 + d)]))
                for bl in range(4):
                    i = t * 4 + bl
                    sgkey = ("SIGN", bl)
                    kt0 = max(0, 2 * i - 4)
                    nwk = 2 * i + 2 - kt0
                    wsl = i % 2
                    dma("sp", KWt[wsl][:, 0:nwk * 128], s_kw[:, kt0 * 128:(2 * i + 2) * 128], r=kwkeys, w=[("KWt", wsl)])
                    dma("sp", VWt[wsl][:, 0:nwk, :], s_vw[:, kt0:2 * i + 2, :], r=vwkeys, w=[("VWt", wsl)])
                    op("dve", lambda e, i=i: e.tensor_scalar(out=AFm[:, :], in0=Gidx[:, :], scalar1=blkq[:, i:i + 1], scalar2=None, op0=ALU.is_le),
                       r=["Gidx", "blkq"], w=["AFm"])
                    op("dve", lambda e, i=i: e.tensor_scalar(out=FFm[:, :], in0=Gidx[:, :], scalar1=blkq2[:, i:i + 1], scalar2=None, op0=ALU.is_gt),
                       r=["Gidx", "blkq2"], w=["FFm"])
                    op("dve", lambda e: e.tensor_tensor(out=FFm[:, :], in0=FFm[:, :], in1=AFm[:, :], op=ALU.mult), r=["FFm", "AFm"], w=["FFm"])
                    op("dve", lambda e: e.scalar_tensor_tensor(out=FFm[:, :], in0=FFm[:, :], scalar=1e4, in1=G0[:, :], op0=ALU.mult, op1=ALU.add),
                       r=["FFm", "G0"], w=["FFm"])
                    n_ct = min(NCT, (16 * i + 14) // 128 + 1)
                    cbias = {}
                    for ct in range(n_ct):
                        thr = 128 * (2 * i - 16 * ct)
                        if thr < 2063:
                            ci = pt_i["cb"]
                            pt_i["cb"] = (ci + 1) % 2
                            op("pool", lambda e, ci=ci, thr=thr: e.tensor_scalar(
                                out=CBI[ci][:, :, :], in0=Tp[:, :].unsqueeze(1).to_broadcast([128, 4, 128]), scalar1=float(thr), scalar2=NEG,
                                op0=ALU.is_gt, op1=ALU.mult), r=["Tp"], w=[("CBI", ci)])
                            cbias[ct] = ci
                    assert len(cbias) <= 2
                    for g in range(2):
                        gp = slice(64 * g, 64 * g + 64)
                        qrhs = QT[gp, :, bl * 128:(bl + 1) * 128]
                        qkeys = [("QT", c) for c in range(4)]
                        for zb in (6, 7):
                            op("pe", lambda e, zb=zb: e.matmul(PS(zb), lhsT=zerob[:, :], rhs=BAb[:, :, :].rearrange("p a b -> p (a b)"),
                                                              start=True, stop=False), r=["zerob", "BA"], w=[("PS", 6)])
                        cpend = []

                        def c_a(ct, g=g, gp=gp, qrhs=qrhs, qkeys=qkeys):
                            b = pst()
                            hb = ct in cbias
                            op("pe", lambda e, b=b, ct=ct, hb=hb: e.matmul(PS(b), lhsT=KCT[gp, ct * 128:(ct + 1) * 128], rhs=qrhs, start=True, stop=not hb),
                               r=["KCT"] + qkeys, w=[("PS", b)])
                            if hb:
                                ci = cbias[ct]
                                op("pe", lambda e, b=b, ci=ci: e.matmul(PS(b), lhsT=identb[:, :], rhs=CBI[ci][:, :, :], start=False, stop=True),
                                   r=["identb", ("CBI", ci)], w=[("PS", b)])
                            pi_ = pt_i["i"]
                            pt_i["i"] = (pi_ + 1) % 3
                            op("act", lambda e, b=b, pi_=pi_: e.activation(out=PTb[pi_][:, :], in_=PS(b), func=AF.Exp), r=[("PS", b)], w=[("PTb", pi_)])
                            cpend.append((ct, pi_))

                        def c_b(g=g):
                            ct, pi_ = cpend.pop(0)
                            for h in range(4):
                                op("pe", lambda e, h=h, ct=ct, pi_=pi_, g=g: e.matmul(
                                    PSA[:, 6 * 512 + h * 256:6 * 512 + h * 256 + 193], lhsT=PTb[pi_][:, h * 128:(h + 1) * 128],
                                    rhs=CR[:, ct, g, 0:193], start=False, stop=(ct == n_ct - 1)),
                                   r=[("PTb", pi_), "CR"], w=[("PS", 6)])

                        for ct in range(n_ct):
                            c_a(ct)
                            if len(cpend) > 2:
                                c_b()
                        while cpend:
                            c_b()
                        co = PSA[:, 6 * 512:8 * 512].rearrange("p (h n) -> p h n", h=4)
                        op("dve", lambda e: e.tensor_scalar(out=R4[:, :], in0=co[:, :, 192], scalar1=1e-30, scalar2=None, op0=ALU.max),
                           r=[("PS", 6)], w=["R4"])
                        op("dve", lambda e: e.reciprocal(out=R4[:, :], in_=R4[:, :]), r=["R4"], w=["R4"])
                        op("dve", lambda e: e.tensor_scalar(out=IMP[:, :], in0=co[:, 0, 0:128], scalar1=R4[:, 0:1], scalar2=None, op0=ALU.mult),
                           r=[("PS", 6), "R4"], w=["IMP"])
                        for h in range(1, 4):
                            op("dve", lambda e, h=h: e.scalar_tensor_tensor(out=IMP[:, :], in0=co[:, h, 0:128], scalar=R4[:, h:h + 1], in1=IMP[:, :],
                                                                            op0=ALU.mult, op1=ALU.add), r=[("PS", 6), "R4", "IMP"], w=["IMP"])
                        op("dve", lambda e, g=g: e.tensor_tensor(out=W4[:, :], in0=R4[:, :], in1=SIGN[:, bl, g * 12:g * 12 + 12:3], op=ALU.mult),
                           r=["R4", sgkey], w=["W4"])
                        for h in range(4):
                            hc_ = slice((g * 4 + h) * 64, (g * 4 + h) * 64 + 64)
                            op("dve", lambda e, h=h, hc_=hc_: e.tensor_scalar(out=ONSA[:, hc_], in0=co[:, h, 128:192], scalar1=W4[:, h:h + 1],
                                                                              scalar2=None, op0=ALU.mult), r=[("PS", 6), "W4"], w=["ONSA"])
                        op("dve", lambda e: e.scalar_tensor_tensor(out=SC1[:, :], in0=IMP[:, :], scalar=1.0, in1=AFm[:, :], op0=ALU.add, op1=ALU.mult),
                           r=["IMP", "AFm"], w=["SC1"])
                        op("dve", lambda e: e.tensor_tensor(out=SC1[:, :], in0=SC1[:, :], in1=FFm[:, :], op=ALU.add), r=["SC1", "FFm"], w=["SC1"])
                        op("dve", lambda e: e.max(out=M8a[:, :], in_=SC1[:, :]), r=["SC1"], w=["M8a"])
                        op("dve", lambda e: e.match_replace(out=SC2[:, :], in_to_replace=M8a[:, :], in_values=SC1[:, :], imm_value=-1.0),
                           r=["SC1", "M8a"], w=["SC2"])
                        op("dve", lambda e: e.max(out=M8b[:, :], in_=SC2[:, :]), r=["SC2"], w=["M8b"])
                        op("dve", lambda e, g=g: e.tensor_scalar(out=BROW[g][:, :], in0=SC1[:, :], scalar1=M8b[:, 7:8], scalar2=NEG, op0=ALU.is_lt, op1=ALU.mult),
                           r=["SC1", "M8b"], w=[("BROW", g)])
                        attn_tiles(i, bl, g, 1, qrhs, qkeys, (wsl, kt0))
                    for g in range(2):
                        gp = slice(64 * g, 64 * g + 64)
                        qrhs = QT[gp, :, bl * 128:(bl + 1) * 128]
                        qkeys = [("QT", c) for c in range(4)]
                        op("pe", lambda e, g=g: e.transpose(PS(3)[:, 0:128], BROW[g][:, :], ident[:, :]), r=[("BROW", g), "ident"], w=[("PS", 3)])
                        op("dve", lambda e, g=g: e.tensor_copy(out=BiasT[g][:, :, :], in_=PS(3)[:, 0:128].unsqueeze(1).to_broadcast([128, 4, 128])),
                           r=[("PS", 3)], w=[("BiasT", g)])
                        attn_tiles(i, bl, g, 0, qrhs, qkeys, None)
                    for c in range(4):
                        op("pe", lambda e, c=c: e.transpose(PS(3)[:, c * 128:(c + 1) * 128], ONSA[:, c * 128:(c + 1) * 128], ident[:, :]),
                           r=["ONSA", "ident"], w=[("PS", 3)])
                    op("act", lambda e, bl=bl: e.copy(out=ONT[:, :, bl * 128:(bl + 1) * 128], in_=PS(3).rearrange("p (c n) -> p c n", c=4)),
                       r=[("PS", 3)], w=[("ONT", bl)])
                ontk = [("ONT", bl) for bl in range(4)]
                for d in range(8):
                    fm_proj(s_pa[d, :, :], ("pa", d), ONT, ontk, 4, 512,
                            lambda b, d=d: op("dve", lambda e: e.tensor_tensor(out=TA[:, :], in0=BIG[:, d, :], in1=PS(b), op=ALU.mult),
                                              r=[("BIG", d), ("PS", b)], w=["TA"]))
                    fm_proj(s_pb[d, :, :], ("pb", d), BIG[:, 16:20, :], [("BIG", 16 + c) for c in range(4)], 4, 512,
                            lambda b, d=d: op("dve", lambda e: e.tensor_tensor(out=TB[:, :], in0=BIG[:, 8 + d, :], in1=PS(b), op=ALU.mult),
                                              r=[("BIG", 8 + d), ("PS", b)], w=["TB"]))
                    op("pool", lambda e, d=d: e.tensor_tensor(out=BIG[:, d, :], in0=TA[:, :], in1=TB[:, :], op=ALU.add),
                       r=["TA", "TB"], w=[("BIG", d)])
                mk = [("BIG", d) for d in range(8)]
                for d in range(8):
                    fm_proj(s_o[d, :, :], ("o", d), BIG[:, 0:8, :], mk, 8, 1024,
                            lambda b, d=d: op("dve", lambda e: e.tensor_tensor(out=HT[:, d, :], in0=HT[:, d, :], in1=PS(b), op=ALU.add),
                                              r=[(HK, d), ("PS", b)], w=[(HK, d)]))
                ffn(s_f2g, s_f2u, s_f2d, "f2")
                rms_norm(UN, "UN")
                dma("sp", TA[:, :], pT[0, :, otok], w=["TA"])
                dma("sp", CS[:, :], pT[1, :, otok], w=["CS"])
                op("pool", lambda e: e.tensor_copy(out=PTlb[:, 0, :], in_=TA[:, :]), r=["TA"], w=["PTlb"])
                op("pool", lambda e: e.tensor_copy(out=PTlb[:, 1, :], in_=CS[:, :]), r=["CS"], w=["PTlb"])
                for d in range(8):
                    fm_proj(s_pg[d, :, :], ("pg", d), UN, UNK, 8, 1024,
                            lambda b: op("act", lambda e: e.activation(out=CY[:, :], in_=PS(b), func=AF.Sigmoid), r=[("PS", b)], w=["CY"]))
                    fm_proj(s_pp[d, :, :], ("pp", d), PTlb, ["PTlb"], 2, 256,
                            lambda b: op("dve", lambda e: e.tensor_tensor(out=CY[:, :], in0=CY[:, :], in1=PS(b), op=ALU.mult),
                                         r=["CY", ("PS", b)], w=["CY"]))
                    op("dve", lambda e, d=d: e.tensor_tensor(out=HT[:, d, :], in0=HT[:, d, :], in1=CY[:, :], op=ALU.add),
                       r=[(HK, d), "CY"], w=[(HK, d)])
                rms_norm(UN, "UN", apply=False)
                for d in range(8):
                    eng = "dve" if d % 2 == 0 else "pool"
                    op(eng, lambda e, d=d: e.tensor_tensor(out=HT[:, d, :], in0=HT[:, d, :], in1=RS[:, :], op=ALU.mult),
                       r=[(HK, d), "RS"], w=[(HK, d)])
                    op(eng, lambda e, d=d: e.tensor_scalar(out=HT[:, d, :], in0=HT[:, d, :], scalar1=gn_t[:, 4, d:d + 1], scalar2=None, op0=ALU.mult),
                       r=[(HK, d), "gn"], w=[(HK, d)])
                dma("pool", outT[:, :, otok].rearrange("c p t -> p c t"), HT[:, :, :], r=HKEYS, w=[("out", t)])
            Sx.finish()
    print("[kernel] sbuf max bytes/partition", mem["max"], "instr", {k: v for k, v in Sx.cnt.items()}, "waits", Sx.nwait)
    return nc


def make_in_maps(inputs, S, B):
    f = lambda a: np.ascontiguousarray(np.asarray(a), dtype=np.float32)
    x = f(inputs["x"])
    p = f(inputs["p"])[0]
    pos = np.asarray(inputs["positions"]).astype(np.int32)
    w_in_p, w_gn = perm_win(f(inputs["w_in"])[0])
    gains = np.stack([pk(inputs[k][0]) for k in ("ffn1_norm", "mix_norm", "ffn2_norm", "ple_norm")] + [pk(inputs["final_norm"])], axis=1)
    cw = f(inputs["conv_w"])[0]
    cbv = f(inputs["conv_b"])[0]
    convp = np.stack([pk(cw[0]), pk(cw[1]), pk(cw[2]), pk(cbv)], axis=2)
    shared = {
        "w_f1g": f(inputs["ffn1_w_gate"])[0], "w_f1u": f(inputs["ffn1_w_up"])[0], "w_f1d": f(inputs["ffn1_w_down"])[0],
        "w_f2g": f(inputs["ffn2_w_gate"])[0], "w_f2u": f(inputs["ffn2_w_up"])[0], "w_f2d": f(inputs["ffn2_w_down"])[0],
        "w_in": w_in_p, "w_gn": w_gn,
        "w_pa": f(inputs["w_proj_nsa"])[0], "w_pb": f(inputs["w_proj_conv"])[0],
        "w_o": f(inputs["w_out"])[0], "w_pg": f(inputs["ple_w_gate"])[0], "w_pp": f(inputs["ple_w_proj"])[0],
        "gains": np.ascontiguousarray(gains), "convp": np.ascontiguousarray(convp),
    }
    for nm, pre in (("ck", "cmp_k"), ("cv", "cmp_v")):
        w1 = f(inputs[pre + "_w1"])[0].reshape(32, 64, 256).transpose(1, 0, 2).reshape(64, 32 * 256)
        shared[nm + "_w1"] = np.ascontiguousarray(np.concatenate([w1, w1], axis=0))
        shared[nm + "_w2"] = np.ascontiguousarray(f(inputs[pre + "_w2"])[0].reshape(2, 128, 64).transpose(1, 0, 2))
        shared[nm + "_b1"] = pk(f(inputs[pre + "_b1"])[0])
        pe = f(inputs[pre + "_pos"])[0].T
        shared[nm + "_pos"] = np.ascontiguousarray(np.concatenate([pe, pe], axis=0))
    maps = []
    owns = []
    for b in range(B):
        xTb = np.ascontiguousarray(x[b].T.reshape(8, 128, S))
        for hf in range(2):
            own = (np.arange(S // 256)[:, None] * 256 + hf * 128 + np.arange(128)[None, :]).reshape(-1)
            owns.append((b, own))
            m = dict(shared)
            m["xT"] = xTb
            m["posg"] = np.ascontiguousarray(pos[b][None, :])
            m["pT"] = np.ascontiguousarray(p[b][own].T.reshape(2, 128, own.size))
            m["poso"] = np.ascontiguousarray(pos[b][own][None, :])
            m.update(host_consts(S, hf))
            maps.append(m)
    return maps, owns


_CACHE = {}


def run(inputs, S, B):
    if S not in _CACHE:
        _CACHE[S] = build_program(S)
    nc = _CACHE[S]
    maps, owns = make_in_maps(inputs, S, B)
    res = run_bass_kernel_spmd(nc, maps, core_ids=list(range(2 * B)))
    out = np.zeros((B, S, D), np.float32)
    for c, (b, own) in enumerate(owns):
        o = np.asarray(res.results[c]["outT"]).reshape(D, own.size)
        out[b, own, :] = o.T
    return out


def kernel(**inputs):
    return run(inputs, 8192, 4)
```

```python
import math
from contextlib import ExitStack

import numpy as np
import concourse.bass as bass
import concourse.mybir as mybir
from concourse.bass_utils import run_bass_kernel_spmd

F32 = mybir.dt.float32
BF16 = mybir.dt.bfloat16
I32 = mybir.dt.int32
AF = mybir.ActivationFunctionType
ALU = mybir.AluOpType

D = 1024
DFF = 2816
NJ = DFF // 128
NEG = -30000.0
EPS = 1e-6
SEM_EPOCH = 60000


class Sched:
    def __init__(self, nc, ndma=40):
        self.nc = nc
        self.E = {"pe": nc.tensor, "act": nc.scalar, "dve": nc.vector, "pool": nc.gpsimd, "sp": nc.sync}
        self.sems = {k: [nc.alloc_semaphore(name="s_%s0" % k)] for k in self.E}
        self.cnt = {k: 0 for k in self.E}
        self.dsem = [nc.alloc_semaphore(name="d%d" % i) for i in range(ndma)]
        self.dtarget = [0] * ndma
        self.dnext = 0
        self.seen = {k: {} for k in self.E}
        self.lastw = {}
        self.readers = {}
        self.nwait = 0

    def _cur_sem(self, e):
        ep = self.cnt[e] // SEM_EPOCH
        while len(self.sems[e]) <= ep:
            self.sems[e].append(self.nc.alloc_semaphore(name="s_%s%d" % (e, len(self.sems[e]))))
        return ep

    def _wait(self, e, tok):
        kind, s, v = tok
        if kind == "e":
            if s == e and e == "pe":
                return
            ep = (v - 1) // SEM_EPOCH
            sid = (s, ep)
            semh = self.sems[s][ep]
            val = v - ep * SEM_EPOCH
            for (s2, ep2), _ in list(self.seen[e].items()):
                if s2 == s and ep2 > ep:
                    return
        else:
            sid = ("d", s)
            semh = self.dsem[s]
            val = v
        if self.seen[e].get(sid, 0) >= val:
            return
        self.seen[e][sid] = val
        self.E[e].wait_ge(semh, val)
        self.nwait += 1

    def deps(self, e, reads, writes):
        for k in reads:
            w = self.lastw.get(k)
            if w is not None:
                self._wait(e, w)
        for k in writes:
            w = self.lastw.get(k)
            if w is not None:
                self._wait(e, w)
            rd = self.readers.get(k)
            if rd:
                for sid, r in rd.items():
                    if r[0] == "e" and r[1] == e:
                        continue
                    self._wait(e, r)

    def commit(self, tok, reads, writes):
        for k in writes:
            self.lastw[k] = tok
            self.readers[k] = {}
        sid = (tok[0], tok[1])
        for k in reads:
            self.readers.setdefault(k, {})[sid] = tok

    def op(self, e, fn, r=(), w=()):
        self.deps(e, r, w)
        ins = fn(self.E[e])
        ep = self._cur_sem(e)
        self.cnt[e] += 1
        ins.then_inc(self.sems[e][ep], 1)
        self.commit(("e", e, self.cnt[e]), r, w)

    def dma(self, q, out, in_, r=(), w=(), **kw):
        s = self.dnext
        self.dnext = (self.dnext + 1) % len(self.dsem)
        if self.dtarget[s] > 0:
            self._wait(q, ("d", s, self.dtarget[s]))
        self.deps(q, r, w)
        ins = self.E[q].dma_start(out=out, in_=in_, **kw)
        self.dtarget[s] += 16
        ins.then_inc(self.dsem[s], 16)
        self.commit(("d", s, self.dtarget[s]), r, w)

    def barrier(self):
        toks = [("e", k, self.cnt[k]) for k in self.E if self.cnt[k] > 0]
        toks += [("d", s, t) for s, t in enumerate(self.dtarget) if t > 0]
        for e in self.E:
            for t in toks:
                self._wait(e, t)

    def finish(self):
        toks = [("e", k, self.cnt[k]) for k in self.E if self.cnt[k] > 0]
        toks += [("d", s, t) for s, t in enumerate(self.dtarget) if t > 0]
        for t in toks:
            self._wait("sp", t)


WIN_CHUNKS = 45


def perm_win(w_in):
    off = np.cumsum([0, 512, 128, 128, 128, 128, 128, 128, 24, 512, 512, 512, 1024, 1024])
    q, kc, vc, ks, vs, kw, vw, gn, cb, cc, cx, ga, gb = [w_in[:, off[i]:off[i + 1]] for i in range(13)]

    def swap(m):
        n = m.shape[1] // 64
        mm = m.reshape(m.shape[0], n, 2, 32)
        return mm[:, :, ::-1, :].reshape(m.shape[0], n * 64)

    def qperm(m):
        mm = m.reshape(m.shape[0], 8, 64)
        order = [0, 4, 1, 5, 2, 6, 3, 7]
        return mm[:, order, :].reshape(m.shape[0], 512)

    cols = [qperm(q), qperm(swap(q)), kc, swap(kc), ks, swap(ks), kw, swap(kw), vc,
            vs, vw, cb, cc, cx, ga, gb]
    big = np.concatenate(cols, axis=1)
    assert big.shape[1] == WIN_CHUNKS * 128
    return np.ascontiguousarray(big), np.ascontiguousarray(gn)


C_Q, C_QS, C_KC, C_KCS, C_KS, C_KSS, C_KW, C_KWS, C_VC, C_V, C_CB, C_CC, C_CX, C_GA, C_GB = (
    0, 4, 8, 9, 10, 11, 12, 13, 14, 15, 17, 21, 25, 29, 37)


def pk(v):
    v = np.asarray(v, np.float32)
    return np.ascontiguousarray(v.reshape(-1, 128).T)


def host_consts(S, hf):
    nsel = S // 64
    ncb = (S - 32) // 16 + 1
    nct = (ncb + 127) // 128
    nown = S // 256
    p = np.arange(128)[:, None].astype(np.float64)
    j = np.arange(128)[None, :].astype(np.float64)
    c = {}
    c["c_dp"] = (p - j - 128 * hf).astype(np.float32)
    c["c_tp"] = (16 * p + 31 - j - 128 * hf).astype(np.float32)
    c["c_gidx"] = np.broadcast_to(np.arange(128, dtype=np.float32)[None, :], (128, 128)).copy()
    g0 = np.zeros((128, 128), np.float32)
    g0[:, 0] = 1e4
    c["c_g0"] = g0
    gm = np.zeros((128, S), np.float32)
    xs = np.arange(S)
    gm[xs // 64, xs] = 1.0
    c["c_gm"] = gm
    cs = np.arange(nct * 128)[:, None] * 16
    ss = np.arange(128)[None, :] * 64
    ov = np.clip(np.minimum(cs + 32, ss + 64) - np.maximum(cs, ss), 0, None).astype(np.float32) / 32.0
    ov[ncb:, :] = 0.0
    ov[:, nsel:] = 0.0
    c["c_ov"] = np.ascontiguousarray(ov.reshape(nct, 128, 128).transpose(1, 0, 2))
    i = np.arange(nown)[None, :]
    blkq = 2 * (2 * i + hf) + (np.arange(128)[:, None] >= 64)
    c["c_blkq"] = blkq.astype(np.float32)
    half = 32
    inv_freq = (10000.0 ** (-np.arange(half, dtype=np.float32) / half)).astype(np.float32)
    pp = np.arange(128)
    sc = np.zeros((128, 4), np.float32)
    sc[:, 0] = inv_freq[pp % 32] / (2.0 * np.pi)
    sc[:, 1] = np.where((pp % 64) < 32, -1.0, 1.0) * 2.0 * np.pi
    sc[:, 2] = 1.0 - hf
    sc[:, 3] = float(hf)
    c["c_sc"] = sc
    return c


def build_program(S, stage=9, sub=9):
    NB = S // 128
    NOWN = NB // 2
    SO = NOWN * 128
    NT1 = S // 512
    NT2 = NOWN // 4
    NCB = (S - 32) // 16 + 1
    NCT = (NCB + 127) // 128
    NCP = NCT * 128

    nc = bass.Bass("TRN2", target_bir_lowering=False)

    def din(name, shape, dt=F32):
        return nc.dram_tensor(name, list(shape), dt, kind="ExternalInput").ap()

    def dscr(name, shape, dt):
        return nc.dram_tensor(name, list(shape), dt, kind="Internal").ap()

    xT = din("xT", [8, 128, S])
    posg = din("posg", [1, S], I32)
    pT = din("pT", [2, 128, SO])
    poso = din("poso", [1, SO], I32)
    outT = nc.dram_tensor("outT", [8, 128, SO], F32, kind="ExternalOutput").ap()
    w_f1g, w_f1u, w_f1d = din("w_f1g", [D, DFF]), din("w_f1u", [D, DFF]), din("w_f1d", [DFF, D])
    w_f2g, w_f2u, w_f2d = din("w_f2g", [D, DFF]), din("w_f2u", [D, DFF]), din("w_f2d", [DFF, D])
    w_in = din("w_in", [D, WIN_CHUNKS * 128])
    w_gn = din("w_gn", [D, 24])
    w_pa, w_pb = din("w_pa", [512, D]), din("w_pb", [512, D])
    w_o, w_pg, w_pp = din("w_o", [D, D]), din("w_pg", [D, D]), din("w_pp", [256, D])
    gains = din("gains", [128, 5, 8])
    convp = din("convp", [128, 4, 4])
    cw1 = [din("ck_w1", [128, 32 * 256]), din("cv_w1", [128, 32 * 256])]
    cw2 = [din("ck_w2", [128, 2, 64]), din("cv_w2", [128, 2, 64])]
    cb1 = [din("ck_b1", [128, 2]), din("cv_b1", [128, 2])]
    cpos = [din("ck_pos", [128, 32]), din("cv_pos", [128, 32])]
    c_dp, c_tp, c_gidx, c_g0 = din("c_dp", [128, 128]), din("c_tp", [128, 128]), din("c_gidx", [128, 128]), din("c_g0", [128, 128])
    c_gm = din("c_gm", [128, S])
    c_ov = din("c_ov", [128, NCT, 128])
    c_blkq = din("c_blkq", [128, NOWN])
    c_sc = din("c_sc", [128, 4])

    s_f1g, s_f1u = dscr("s_f1g", [NJ, 128, 1024], BF16), dscr("s_f1u", [NJ, 128, 1024], BF16)
    s_f2g, s_f2u = dscr("s_f2g", [NJ, 128, 1024], BF16), dscr("s_f2u", [NJ, 128, 1024], BF16)
    s_f1d, s_f2d = dscr("s_f1d", [8, 128, DFF], BF16), dscr("s_f2d", [8, 128, DFF], BF16)
    s_in = dscr("s_in", [WIN_CHUNKS, 128, 1024], BF16)
    s_v = dscr("s_v", [128, 8 * 256], BF16)
    s_gn = dscr("s_gn", [128, 8 * 24], BF16)
    s_pa, s_pb = dscr("s_pa", [8, 128, 512], BF16), dscr("s_pb", [8, 128, 512], BF16)
    s_o, s_pg = dscr("s_o", [8, 128, 1024], BF16), dscr("s_pg", [8, 128, 1024], BF16)
    s_pp = dscr("s_pp", [8, 128, 256], BF16)
    h1s = dscr("h1s", [8, 128, S], F32)

    es = ExitStack()
    with es:
        mem = {"cur": 0, "max": 0}

        class _Acct:
            def __init__(self, n):
                self.n = n

            def __enter__(self):
                mem["cur"] += self.n
                mem["max"] = max(mem["max"], mem["cur"])

            def __exit__(self, *a):
                mem["cur"] -= self.n

        def sb(name, shape, dt, stack=es):
            n = int(np.prod(shape[1:])) * (4 if dt in (F32, I32) else 2)
            stack.enter_context(_Acct(n))
            return stack.enter_context(nc.sbuf_tensor(name, list(shape), dt))

        Sx = Sched(nc)
        op, dma = Sx.op, Sx.dma

        PSA = es.enter_context(nc.psum_tensor("psA", [128, 8 * 512], F32))

        def PS(b):
            return PSA[:, b * 512:(b + 1) * 512]

        rot = {"d": 0, "st": 0}

        def psd():
            rot["d"] = (rot["d"] + 1) % 4
            return rot["d"]

        def pst():
            rot["st"] = (rot["st"] + 1) % 3
            return rot["st"]

        ident = sb("ident", [128, 128], F32)
        identb = sb("identb", [128, 128], BF16)
        onesb = sb("onesb", [128, 128], BF16)
        zerob = sb("zerob", [128, 128], BF16)
        epsc = sb("epsc", [128, 1], F32)
        gn_t = sb("gn_t", [128, 5, 8], F32)
        gq_t = sb("gq_t", [128, 8], F32)
        cvp = sb("cvp", [128, 4, 4], F32)
        sc_t = sb("sc_t", [128, 4], F32)
        Dp = sb("Dp", [128, 128], F32)
        Tp = sb("Tp", [128, 128], F32)
        Gidx = sb("Gidx", [128, 128], F32)
        G0 = sb("G0", [128, 128], F32)
        blkq = sb("blkq", [128, NOWN], F32)
        blkq2 = sb("blkq2", [128, NOWN], F32)
        Gm = sb("Gm", [128, S], BF16)
        BAb = sb("BAb", [128, 4, 128], BF16)
        BBb = sb("BBb", [128, 4, 128], BF16)
        W0b = sb("W0b", [128, 4, 128], BF16)
        W1b = sb("W1b", [128, 4, 128], BF16)
        KsT = sb("KsT", [128, S], BF16)
        Vs = sb("Vs", [128, NB, 192], BF16)
        HALOALL = sb("HALOALL", [128, 8, NB, 2], F32)
        KCT = sb("KCT", [128, NCP], BF16)
        CR = sb("CR", [128, NCT, 2, 194], BF16)
        H0_ = sb("H0", [128, 8, 512], F32)
        H = [H0_, H0_]
        UN = sb("UN", [128, 8, 512], BF16)
        BIG = sb("BIG", [128, NJ, 512], BF16)
        RS = sb("RS", [128, 512], F32)
        SG = [sb("SG0", [128, 512], F32), sb("SG1", [128, 512], F32)]
        WR = [sb("WR%d" % i, [128, 1024], BF16) for i in range(5)]
        CS = sb("CS", [128, 512], F32)
        SN = sb("SN", [128, 512], F32)
        TA = sb("TA", [128, 512], F32)
        TB = sb("TB", [128, 512], F32)
        TI = sb("TI", [128, 512], I32)
        PI = TI
        LNV = TB

        HT = H0_
        HK = "H"
        HKEYS = [(HK, d) for d in range(8)]
        s_kw = dscr("s_kw", [128, S], BF16)
        s_vw = dscr("s_vw", [128, NB, 192], BF16)

        wr_i = {"r": 0}

        def wslot(src_ap, ncols, skey):
            i = wr_i["r"]
            wr_i["r"] = (i + 1) % len(WR)
            dma("sp", WR[i][:, 0:ncols], src_ap, r=[skey], w=[("WR", i)])
            return WR[i], ("WR", i)

        def vaug(base_ap, col, ones_col):
            pstride = base_ap.ap[0][0]
            return bass.AP(base_ap.tensor, base_ap.offset + col, [[pstride, 128], [ones_col - col, 2], [1, 64]])

        op("pool", lambda e: e.iota(ident[:, :], pattern=[[1, 128]], base=0, channel_multiplier=-1,
                                    allow_small_or_imprecise_dtypes=True), w=["ident"])
        op("dve", lambda e: e.tensor_scalar(out=ident[:, :], in0=ident[:, :], scalar1=0.0, scalar2=None,
                                            op0=ALU.is_equal), r=["ident"], w=["ident"])
        op("dve", lambda e: e.tensor_copy(out=identb[:, :], in_=ident[:, :]), r=["ident"], w=["identb"])
        op("dve", lambda e: e.memset(onesb[:, :], 1.0), w=["onesb"])
        op("dve", lambda e: e.memset(zerob[:, :], 0.0), w=["zerob"])
        op("dve", lambda e: e.memset(epsc[:, :], EPS), w=["epsc"])
        dma("sp", gn_t[:, :, :], gains[:, :, :], w=["gn"])
        dma("sp", cvp[:, :, :], convp[:, :, :], w=["cvp"])
        dma("sp", sc_t[:, :], c_sc[:, :], w=["sc"])
        dma("sp", Dp[:, :], c_dp[:, :], w=["Dp"])
        dma("sp", Tp[:, :], c_tp[:, :], w=["Tp"])
        dma("sp", Gidx[:, :], c_gidx[:, :], w=["Gidx"])
        dma("sp", G0[:, :], c_g0[:, :], w=["G0"])
        dma("sp", blkq[:, :], c_blkq[:, :], w=["blkq"])
        op("dve", lambda e: e.tensor_scalar(out=blkq2[:, :], in0=blkq[:, :], scalar1=-2.0, scalar2=None, op0=ALU.add),
           r=["blkq"], w=["blkq2"])
        op("dve", lambda e: e.tensor_scalar(out=gq_t[:, :], in0=gn_t[:, 1, :], scalar1=0.125, scalar2=None, op0=ALU.mult),
           r=["gn"], w=["gq"])
        for (dst, o0, thr, nm) in ((BAb, ALU.is_gt, 0.0, "BA"), (BBb, ALU.is_gt, -128.0, "BB"),
                                   (W0b, ALU.is_le, 0.0, "W0"), (W1b, ALU.is_le, -128.0, "W1")):
            for h in range(4):
                op("dve", lambda e, dst=dst, o0=o0, thr=thr, h=h: e.tensor_scalar(
                    out=dst[:, h, :], in0=Dp[:, :], scalar1=thr, scalar2=NEG, op0=o0, op1=ALU.mult),
                   r=["Dp"], w=[nm])
        hflat = HT[:, :, :].rearrange("p a b -> p (a b)")
        for c0 in range(0, S, 4096):
            n = min(4096, S - c0)
            dma("sp", hflat[:, 0:n], c_gm[:, c0:c0 + n], w=HKEYS)
            op("dve", lambda e, c0=c0, n=n: e.tensor_copy(out=Gm[:, c0:c0 + n], in_=hflat[:, 0:n]), r=HKEYS, w=["Gm"])
        for ct in range(NCT):
            dma("sp", hflat[:, 0:128], c_ov[:, ct, :], w=HKEYS)
            for g in range(2):
                op("dve", lambda e, ct=ct, g=g: e.tensor_copy(out=CR[:, ct, g, 0:128], in_=hflat[:, 0:128]), r=HKEYS, w=["CR"])
                op("dve", lambda e, ct=ct, g=g: e.memset(CR[:, ct, g, 192:194], 1.0), w=["CR"])
        op("pool", lambda e: e.memset(Vs[:, :, 64:128], 1.0), w=["Vs1"])
        op("pool", lambda e: e.memset(KCT[:, :], 0.0), w=["KCT"])

        with ExitStack() as es0:
            STN = 4096
            STF = [sb("STF%d" % i, [128, STN], F32, es0) for i in range(2)]
            STB = [sb("STB%d" % i, [128, STN], BF16, es0) for i in range(2)]
            cvi = {"i": 0}

            def conv(src, dst, nch, nk, inner, gain=None, scale=1.0, dkeys=None):
                i = cvi["i"] % 2
                cvi["i"] += 1
                n = nk * nch * inner
                assert n <= STN
                sf = STF[i][:, 0:n].rearrange("p (k m) -> p k m", k=nk)
                sbb = STB[i][:, 0:n].rearrange("p (c k m) -> p c k m", c=nch, k=nk)
                dma("sp", sf, src, w=[("STF", i)])
                for jj in range(nch):
                    eng = "dve" if (jj % 3) != 2 else "pool"
                    o_ = sbb[:, jj, :, :]
                    i_ = sf[:, :, jj * inner:(jj + 1) * inner]
                    if gain is not None:
                        op(eng, lambda e, o_=o_, i_=i_: e.tensor_tensor(out=o_, in0=i_, in1=gain.unsqueeze(2).to_broadcast([128, nk, inner]),
                                                                        op=ALU.mult), r=[("STF", i), "gn", "gq"], w=[("STB", i)])
                    elif scale != 1.0:
                        op(eng, lambda e, o_=o_, i_=i_: e.tensor_scalar(out=o_, in0=i_, scalar1=scale, scalar2=None, op0=ALU.mult),
                           r=[("STF", i)], w=[("STB", i)])
                    else:
                        op(eng, lambda e, o_=o_, i_=i_: e.tensor_copy(out=o_, in_=i_), r=[("STF", i)], w=[("STB", i)])
                dma("pool", dst, STB[i][:, 0:n].rearrange("p (c m) -> p c m", c=nch), r=[("STB", i)], w=dkeys)

            def kview(wsrc, c0, c1, kp=8):
                return wsrc.rearrange("(k p) n -> p k n", p=128)[:, :, c0:c1]

            def slots(sdst, s0, s1):
                return sdst[s0:s1, :, :].rearrange("c p m -> p c m")

            def conv_cols(wsrc, sdst, nchunks, nk, gain, tag, grp=4, c_off=0, s_off=0):
                c = 0
                while c < nchunks:
                    g_ = min(grp, nchunks - c)
                    conv(kview(wsrc, (c_off + c) * 128, (c_off + c + g_) * 128), slots(sdst, s_off + c, s_off + c + g_), g_, nk, 128,
                         gain=gain, dkeys=[(tag, s_off + c + x) for x in range(g_)])
                    c += g_

            def conv_ffn(wg, wu, wd, sg, su, sd, gi, tag):
                conv_cols(wg, sg, NJ, 8, gn_t[:, gi, :], tag + "g")
                conv_cols(wu, su, NJ, 8, gn_t[:, gi, :], tag + "u")
                for d in range(8):
                    conv(wd.rearrange("(j p) n -> p j n", p=128)[:, :, d * 128:(d + 1) * 128], slots(sd, d, d + 1), 1, NJ, 128,
                         scale=0.5, dkeys=[(tag + "d", d)])

            conv_ffn(w_f1g, w_f1u, w_f1d, s_f1g, s_f1u, s_f1d, 0, "f1")
            conv_cols(w_in, s_in, 8, 8, gq_t[:, :], "in")
            conv_cols(w_in, s_in, C_V - 8, 8, gn_t[:, 1, :], "in", c_off=8, s_off=8)
            conv_cols(w_in, s_in, WIN_CHUNKS - (C_V + 2), 8, gn_t[:, 1, :], "in", c_off=C_V + 2, s_off=C_V + 2)
            conv(kview(w_in, C_V * 128, (C_V + 2) * 128), s_v[:, :].rearrange("p (c m) -> p c m", c=1), 1, 8, 256,
                 gain=gn_t[:, 1, :], dkeys=["s_v"])
            conv(kview(w_gn, 0, 24), s_gn[:, :].rearrange("p (c m) -> p c m", c=1), 1, 8, 24, gain=gn_t[:, 1, :], dkeys=["s_gn"])
            conv_cols(w_pa, s_pa, 8, 4, None, "pa", grp=8)
            conv_cols(w_pb, s_pb, 8, 4, None, "pb", grp=8)
            conv_cols(w_o, s_o, 8, 8, None, "o")
            conv_cols(w_pg, s_pg, 8, 8, gn_t[:, 3, :], "pg")
            conv_cols(w_pp, s_pp, 8, 2, None, "pp", grp=8)
            conv_ffn(w_f2g, w_f2u, w_f2d, s_f2g, s_f2u, s_f2d, 2, "f2")
            Sx.barrier()

        def rms_norm(out_bf, okey, apply=True):
            op("pool", lambda e: e.tensor_tensor(out=out_bf[:, 0:4, :], in0=HT[:, 0:4, :], in1=HT[:, 0:4, :], op=ALU.mult),
               r=HKEYS, w=[(okey, 0)])
            op("dve", lambda e: e.tensor_tensor(out=out_bf[:, 4:8, :], in0=HT[:, 4:8, :], in1=HT[:, 4:8, :], op=ALU.mult),
               r=HKEYS, w=[(okey, 1)])
            b = psd()
            for k in range(8):
                op("pe", lambda e, k=k: e.matmul(PS(b), lhsT=onesb[:, :], rhs=out_bf[:, k, :], start=(k == 0), stop=(k == 7)),
                   r=[(okey, k // 4), "onesb"], w=[("PS", b)])
            op("act", lambda e: e.activation(out=LNV[:, :], in_=PS(b), func=AF.Ln, bias=epsc[:, :], scale=1.0 / D),
               r=[("PS", b), "epsc"], w=["TB"])
            op("act", lambda e: e.activation(out=RS[:, :], in_=LNV[:, :], func=AF.Exp, scale=-0.5), r=["TB"], w=["RS"])
            if not apply:
                return
            rb = RS[:, :].unsqueeze(1).to_broadcast([128, 4, 512])
            op("dve", lambda e: e.tensor_tensor(out=out_bf[:, 0:4, :], in0=HT[:, 0:4, :], in1=rb, op=ALU.mult),
               r=HKEYS + ["RS"], w=[(okey, 0)])
            op("pool", lambda e: e.tensor_tensor(out=out_bf[:, 4:8, :], in0=HT[:, 4:8, :], in1=rb, op=ALU.mult),
               r=HKEYS + ["RS"], w=[(okey, 1)])

        UNK = [("UN", 0), ("UN", 1)]

        def fm_proj(src_dram, skey, xin, xkeys, nk, ncols, consume, ntok=512):
            wt, wk = wslot(src_dram, ncols, skey)
            b = psd()
            wv = wt[:, 0:ncols].rearrange("p (k n) -> p k n", k=nk)
            for k in range(nk):
                op("pe", lambda e, k=k: e.matmul(PS(b)[:, 0:ntok], lhsT=wv[:, k, :], rhs=xin[:, k, :], start=(k == 0), stop=(k == nk - 1)),
                   r=[wk] + xkeys, w=[("PS", b)])
            consume(b)

        def ffn(sg, su, sd, tag):
            rms_norm(UN, "UN")
            for j in range(NJ):
                wg, wgk = wslot(sg[j, :, :], 1024, (tag + "g", j))
                wu, wuk = wslot(su[j, :, :], 1024, (tag + "u", j))
                bg, bu = psd(), psd()
                wgv = wg[:, :].rearrange("p (k n) -> p k n", k=8)
                wuv = wu[:, :].rearrange("p (k n) -> p k n", k=8)
                for k in range(8):
                    op("pe", lambda e, k=k: e.matmul(PS(bg), lhsT=wgv[:, k, :], rhs=UN[:, k, :], start=(k == 0), stop=(k == 7)),
                       r=[wgk] + UNK, w=[("PS", bg)])
                for k in range(8):
                    op("pe", lambda e, k=k: e.matmul(PS(bu), lhsT=wuv[:, k, :], rhs=UN[:, k, :], start=(k == 0), stop=(k == 7)),
                       r=[wuk] + UNK, w=[("PS", bu)])
                sgt = SG[j % 2]
                op("act", lambda e: e.activation(out=sgt[:, :], in_=PS(bg), func=AF.Silu), r=[("PS", bg)], w=[("SG", j % 2)])
                op("dve", lambda e: e.tensor_tensor(out=BIG[:, j, :], in0=sgt[:, :], in1=PS(bu), op=ALU.mult),
                   r=[("SG", j % 2), ("PS", bu)], w=[("BIG", j)])
            for d in range(8):
                b = psd()
                for (j0, j1) in ((0, 8), (8, 16), (16, NJ)):
                    wd, wdk = wslot(sd[d, :, j0 * 128:j1 * 128], (j1 - j0) * 128, (tag + "d", d))
                    wdv = wd[:, 0:(j1 - j0) * 128].rearrange("p (j n) -> p j n", j=j1 - j0)
                    for j in range(j0, j1):
                        op("pe", lambda e, j=j, wdv=wdv, j0=j0: e.matmul(PS(b), lhsT=wdv[:, j - j0, :], rhs=BIG[:, j, :],
                                                                        start=(j == 0), stop=(j == NJ - 1)),
                           r=[wdk, ("BIG", j)], w=[("PS", b)])
                op("dve", lambda e, d=d: e.tensor_tensor(out=HT[:, d, :], in0=HT[:, d, :], in1=PS(b), op=ALU.add),
                   r=[(HK, d), ("PS", b)], w=[(HK, d)])

        SIN_SCALE = 2.0 * math.pi * 0.999999

        def rope_tables(pos_ap, ntok):
            dma("sp", PI[:, 0:ntok], pos_ap.partition_broadcast(128), w=["TI"])
            op("dve", lambda e: e.tensor_copy(out=TA[:, 0:ntok], in_=PI[:, 0:ntok]), r=["TI"], w=["TA"])
            op("dve", lambda e: e.tensor_scalar(out=TA[:, 0:ntok], in0=TA[:, 0:ntok], scalar1=sc_t[:, 0:1], scalar2=None,
                                                op0=ALU.mult), r=["TA", "sc"], w=["TA"])
            for (dst, shift, scale_ap, nm) in ((SN, 0.0, sc_t[:, 1:2], "SN"), (CS, 0.25, SIN_SCALE, "CS")):
                eng = "dve"
                op(eng, lambda e, shift=shift: e.tensor_scalar(out=TB[:, 0:ntok], in0=TA[:, 0:ntok], scalar1=shift, scalar2=None,
                                                               op0=ALU.add), r=["TA"], w=["TB"])
                op(eng, lambda e: e.tensor_copy(out=TI[:, 0:ntok], in_=TB[:, 0:ntok]), r=["TB"], w=["TI"])
                op(eng, lambda e, dst=dst: e.tensor_copy(out=dst[:, 0:ntok], in_=TI[:, 0:ntok]), r=["TI"], w=[nm])
                op(eng, lambda e, dst=dst: e.tensor_tensor(out=TB[:, 0:ntok], in0=TB[:, 0:ntok], in1=dst[:, 0:ntok], op=ALU.subtract),
                   r=["TB", nm], w=["TB"])
                op(eng, lambda e, dst=dst: e.tensor_scalar(out=dst[:, 0:ntok], in0=TB[:, 0:ntok], scalar1=0.5, scalar2=None,
                                                           op0=ALU.is_gt), r=["TB"], w=[nm])
                op(eng, lambda e, dst=dst: e.tensor_tensor(out=TB[:, 0:ntok], in0=TB[:, 0:ntok], in1=dst[:, 0:ntok], op=ALU.subtract),
                   r=["TB", nm], w=["TB"])
                op(eng, lambda e, dst=dst: e.tensor_scalar(out=dst[:, 0:ntok], in0=TB[:, 0:ntok], scalar1=-0.5, scalar2=None,
                                                           op0=ALU.is_lt), r=["TB"], w=[nm])
                op(eng, lambda e, dst=dst: e.tensor_tensor(out=TB[:, 0:ntok], in0=TB[:, 0:ntok], in1=dst[:, 0:ntok], op=ALU.add),
                   r=["TB", nm], w=["TB"])
                op("act", lambda e, dst=dst, scale_ap=scale_ap: e.activation(out=dst[:, 0:ntok], in_=TB[:, 0:ntok], func=AF.Sin,
                                                                             scale=scale_ap), r=["TB", "sc"], w=[nm])

        def rope_apply(bp, bs, out_ap, okeys, ntok):
            op("dve", lambda e: e.tensor_tensor(out=TA[:, 0:ntok], in0=PS(bp)[:, 0:ntok], in1=CS[:, 0:ntok], op=ALU.mult),
               r=[("PS", bp), "CS"], w=["TA"])
            op("dve", lambda e: e.tensor_tensor(out=TB[:, 0:ntok], in0=PS(bs)[:, 0:ntok], in1=SN[:, 0:ntok], op=ALU.mult),
               r=[("PS", bs), "SN"], w=["TB"])
            op("pool", lambda e: e.tensor_tensor(out=out_ap, in0=TA[:, 0:ntok], in1=TB[:, 0:ntok], op=ALU.add),
               r=["TA", "TB"], w=okeys)

        with ExitStack() as es1:
            KcT = sb("KcT", [128, S], BF16, es1)
            VcT = sb("VcT", [128, S], BF16, es1)
            KWst = sb("KWst", [128, 512], BF16, es1)
            VWst = sb("VWst", [128, 4, 192], BF16, es1)
            op("pool", lambda e: e.memset(VWst[:, :, 64:128], 1.0), w=["VWst"])
            for t in range(NT1 if stage >= 1 else 0):
                tok = slice(t * 512, (t + 1) * 512)
                dma("sp", HT[:, :, :], xT[:, :, tok].rearrange("c p t -> p c t"), w=HKEYS)
                ffn(s_f1g, s_f1u, s_f1d, "f1")
                if sub < 2:
                    continue
                dma("pool", h1s[:, :, tok].rearrange("c p t -> p c t"), HT[:, :, :], r=HKEYS, w=[("h1s", t)])
                op("pool", lambda e, t=t: e.tensor_copy(out=HALOALL[:, :, t * 4:(t + 1) * 4, :],
                                                        in_=HT[:, :, :].rearrange("p c (b t) -> p c b t", t=128)[:, :, :, 126:128]),
                   r=HKEYS, w=["HALOALL"])
                if sub < 3:
                    continue
                rms_norm(UN, "UN")
                rope_tables(posg[:, tok], 512)
                if sub < 4:
                    continue
                for (cp, cs_, dst_ap, keys) in ((C_KC, C_KCS, KcT[:, tok], [("KcT", t)]), (C_KS, C_KSS, KsT[:, tok], [("KsT", t)]),
                                                (C_KW, C_KWS, KWst[:, :], ["KWst"])):
                    banks = []
                    for c in (cp, cs_):
                        fm_proj(s_in[c, :, :], ("in", c), UN, UNK, 8, 1024, lambda b: banks.append(b))
                    rope_apply(banks[0], banks[1], dst_ap, keys, 512)
                dma("pool", s_kw[:, tok], KWst[:, :], r=["KWst"], w=[("s_kw", t)])
                if sub < 5:
                    continue
                fm_proj(s_in[C_VC, :, :], ("in", C_VC), UN, UNK, 8, 1024,
                        lambda b: op("act", lambda e: e.copy(out=VcT[:, tok], in_=PS(b)), r=[("PS", b)], w=[("VcT", t)]))
                if sub < 6:
                    continue
                wvt0, wvk0 = wslot(s_v[:, 0:1024], 1024, "s_v")
                wvt1, wvk1 = wslot(s_v[:, 1024:2048], 1024, "s_v")
                for bl in range(4):
                    gb_ = t * 4 + bl
                    b = psd()
                    for k in range(8):
                        wt_, wk_ = (wvt0, wvk0) if k < 4 else (wvt1, wvk1)
                        kk = k % 4
                        op("pe", lambda e, k=k, kk=kk, wt_=wt_, bl=bl, b=b: e.matmul(
                            PS(b)[:, 0:256], lhsT=UN[:, k, bl * 128:(bl + 1) * 128], rhs=wt_[:, kk * 256:(kk + 1) * 256],
                            start=(k == 0), stop=(k == 7)), r=[wk_] + UNK, w=[("PS", b)])
                    op("act", lambda e, gb_=gb_, b=b: e.copy(out=Vs[:, gb_, 0:64], in_=PS(b)[:, 0:64]), r=[("PS", b)], w=[("Vs", gb_)])
                    op("act", lambda e, gb_=gb_, b=b: e.copy(out=Vs[:, gb_, 128:192], in_=PS(b)[:, 64:128]), r=[("PS", b)], w=[("Vs", gb_)])
                    if sub < 7:
                        continue
                    op("act", lambda e, bl=bl, b=b: e.copy(out=VWst[:, bl, 0:64], in_=PS(b)[:, 128:192]), r=[("PS", b)], w=["VWst"])
                    op("act", lambda e, bl=bl, b=b: e.copy(out=VWst[:, bl, 128:192], in_=PS(b)[:, 192:256]), r=[("PS", b)], w=["VWst"])
                if sub < 7:
                    continue
                dma("pool", s_vw[:, t * 4:(t + 1) * 4, :], VWst[:, :, :], r=["VWst"], w=[("s_vw", t)])

            with ExitStack() as es15:
                W1b_ = sb("W1b_", [128, 32 * 256], BF16, es15)
                W2f = sb("W2f", [128, 2, 64], F32, es15)
                W2p = sb("W2p", [128, 2, 128], BF16, es15)
                B1 = sb("B1", [128, 2], F32, es15)
                B1e = sb("B1e", [128, 2], F32, es15)
                POSf = sb("POSf", [128, 32], F32, es15)
                POSb = sb("POSb", [128, 32], BF16, es15)
                HID = sb("HID", [128, 2, NCP], BF16, es15)
                HF, HG = TA, CS
                allK = [("KcT", t) for t in range(NT1)]
                allV = [("VcT", t) for t in range(NT1)]
                op("dve", lambda e: e.memset(HID[:, :, :], 0.0), w=["HID"])
                for kind in range(2 if stage >= 2 else 0):
                    srcT, skeys = (KcT, allK) if kind == 0 else (VcT, allV)
                    for q2 in range(2):
                        dma("sp", hflat[:, 0:4096], cw1[kind][:, q2 * 4096:(q2 + 1) * 4096], w=HKEYS)
                        op("dve", lambda e, q2=q2: e.tensor_copy(out=W1b_[:, q2 * 4096:(q2 + 1) * 4096], in_=hflat[:, 0:4096]),
                           r=HKEYS, w=["W1b"])
                    dma("sp", W2f[:, :, :], cw2[kind][:, :, :], w=["W2f"])
                    dma("sp", B1[:, :], cb1[kind][:, :], w=["B1"])
                    dma("sp", POSf[:, :], cpos[kind][:, :], w=["POSf"])
                    op("dve", lambda e: e.tensor_copy(out=W2p[:, :, 0:64], in_=W2f[:, :, :]), r=["W2f"], w=["W2p"])
                    op("dve", lambda e: e.tensor_copy(out=W2p[:, :, 64:128], in_=W2f[:, :, :]), r=["W2f"], w=["W2p"])
                    op("dve", lambda e: e.tensor_copy(out=POSb[:, :], in_=POSf[:, :]), r=["POSf"], w=["POSb"])
                    w1v = W1b_[:, :].rearrange("p (l n) -> p l n", l=32)
                    for hh in range(2):
                        b = psd()
                        for l in range(32):
                            op("pe", lambda e, l=l, hh=hh, b=b: e.matmul(PS(b)[:, 0:1], lhsT=w1v[0:64, l, hh * 128:(hh + 1) * 128],
                                                                       rhs=POSb[0:64, l:l + 1], start=(l == 0), stop=(l == 31)),
                               r=["W1b", "POSb"], w=[("PS", b)])
                        op("dve", lambda e, hh=hh, b=b: e.tensor_tensor(out=B1e[:, hh:hh + 1], in0=PS(b)[:, 0:1], in1=B1[:, hh:hh + 1],
                                                                      op=ALU.add), r=[("PS", b), "B1"], w=["B1e"])
                    for g in range(2):
                        gp = slice(64 * g, 64 * g + 64)
                        for hh in range(2):
                            b = psd()
                            for l in range(32):
                                rhs = srcT[gp, l:l + 16 * (NCB - 1) + 1:16]
                                op("pe", lambda e, l=l, hh=hh, b=b, rhs=rhs, gp=gp: e.matmul(
                                    PS(b)[:, 0:NCB], lhsT=w1v[gp, l, hh * 128:(hh + 1) * 128], rhs=rhs,
                                    start=(l == 0), stop=(l == 31)), r=["W1b"] + skeys, w=[("PS", b)])
                            op("act", lambda e, hh=hh, b=b: e.activation(out=HF[:, 0:NCB], in_=PS(b)[:, 0:NCB], func=AF.Identity,
                                                                        bias=B1e[:, hh:hh + 1], scale=1.0), r=[("PS", b), "B1e"], w=["TA"])
                            op("dve", lambda e: e.tensor_tensor(out=HG[:, 0:NCB], in0=HF[:, 0:NCB], in1=HF[:, 0:NCB], op=ALU.mult),
                               r=["TA"], w=["CS"])
                            op("dve", lambda e: e.tensor_scalar(out=HG[:, 0:NCB], in0=HG[:, 0:NCB], scalar1=0.044715, scalar2=1.0,
                                                                op0=ALU.mult, op1=ALU.add), r=["CS"], w=["CS"])
                            op("dve", lambda e: e.tensor_tensor(out=HG[:, 0:NCB], in0=HG[:, 0:NCB], in1=HF[:, 0:NCB], op=ALU.mult),
                               r=["CS", "TA"], w=["CS"])
                            op("act", lambda e: e.activation(out=HG[:, 0:NCB], in_=HG[:, 0:NCB], func=AF.Sigmoid, scale=1.5957691216),
                               r=["CS"], w=["CS"])
                            op("dve", lambda e, hh=hh: e.tensor_tensor(out=HID[:, hh, 0:NCB], in0=HG[:, 0:NCB], in1=HF[:, 0:NCB], op=ALU.mult),
                               r=["CS", "TA"], w=["HID"])
                        if kind == 0:
                            b = psd()
                            for hh in range(2):
                                op("pe", lambda e, hh=hh, b=b: e.matmul(PS(b)[:, 0:NCB], lhsT=W2p[:, hh, :], rhs=HID[:, hh, 0:NCB],
                                                                       start=(hh == 0), stop=(hh == 1)), r=["W2p", "HID"], w=[("PS", b)])
                            op("act", lambda e, b=b, gp=gp: e.copy(out=KCT[gp, 0:NCB], in_=PS(b)[gp, 0:NCB]), r=[("PS", b)], w=["KCT"])
                        else:
                            for ct in range(NCT):
                                b = psd()
                                for hh in range(2):
                                    op("pe", lambda e, hh=hh, b=b, ct=ct: e.matmul(PS(b)[:, 0:64], lhsT=HID[:, hh, ct * 128:(ct + 1) * 128],
                                                                                 rhs=W2p[:, hh, 0:64], start=(hh == 0), stop=(hh == 1)),
                                       r=["W2p", "HID"], w=[("PS", b)])
                                op("act", lambda e, b=b, ct=ct, g=g: e.copy(out=CR[:, ct, g, 128:192], in_=PS(b)[:, 0:64]),
                                   r=[("PS", b)], w=["CR"])
                Sx.barrier()
            Sx.barrier()

        NH = 2 * NOWN
        with ExitStack() as es2:
            HCU = sb("HCU", [128, 4, NH], F32, es2)
            def halo_pass(esh):
                HALO = sb("HALO", [128, 8, NOWN, 2], F32, esh)
                HUN = sb("HUN", [128, 8, NH], BF16, esh)
                HSQ = sb("HSQ", [128, 8, NH], BF16, esh)
                HTMP = sb("HTMP", [128, NH], F32, esh)
                op("dve", lambda e: e.memset(HALO[:, :, :, :], 0.0), w=["HALO"])
                if NOWN > 1:
                    op("dve", lambda e: e.tensor_scalar(out=HALO[:, :, 1:NOWN, :], in0=HALOALL[:, :, 1:NB - 1:2, :], scalar1=sc_t[:, 2:3],
                                                        scalar2=None, op0=ALU.mult), r=["HALOALL", "sc", "HALO"], w=["HALO"])
                op("dve", lambda e: e.scalar_tensor_tensor(out=HALO[:, :, :, :], in0=HALOALL[:, :, 0:NB:2, :], scalar=sc_t[:, 3:4],
                                                           in1=HALO[:, :, :, :], op0=ALU.mult, op1=ALU.add),
                   r=["HALOALL", "HALO", "sc"], w=["HALO"])
                hal = HALO[:, :, :, :].rearrange("p c n t -> p c (n t)")
                op("dve", lambda e: e.tensor_tensor(out=HSQ[:, :, :], in0=hal, in1=hal, op=ALU.mult), r=["HALO"], w=["HSQ"])
                b = psd()
                for k in range(8):
                    op("pe", lambda e, k=k, b=b: e.matmul(PS(b)[:, 0:NH], lhsT=onesb[:, :], rhs=HSQ[:, k, :], start=(k == 0), stop=(k == 7)),
                       r=["HSQ", "onesb"], w=[("PS", b)])
                op("act", lambda e, b=b: e.activation(out=HTMP[:, :], in_=PS(b)[:, 0:NH], func=AF.Ln, bias=epsc[:, :], scale=1.0 / D),
                   r=[("PS", b), "epsc"], w=["HTMP"])
                op("act", lambda e: e.activation(out=HTMP[:, :], in_=HTMP[:, :], func=AF.Exp, scale=-0.5), r=["HTMP"], w=["HTMP"])
                op("dve", lambda e: e.tensor_tensor(out=HUN[:, :, :], in0=hal, in1=HTMP[:, :].unsqueeze(1).to_broadcast([128, 8, NH]),
                                                    op=ALU.mult), r=["HALO", "HTMP"], w=["HUN"])
                for c4 in range(4):
                    banks = []
                    for cbase in (C_CC, C_CX):
                        fm_proj(s_in[cbase + c4, :, :], ("in", cbase + c4), HUN, ["HUN"], 8, 1024, lambda b: banks.append(b), ntok=NH)
                    op("act", lambda e, b0=banks[0]: e.copy(out=HTMP[:, :], in_=PS(b0)[:, 0:NH]), r=[("PS", banks[0])], w=["HTMP"])
                    op("dve", lambda e, c4=c4, b1=banks[1]: e.tensor_tensor(out=HCU[:, c4, :], in0=HTMP[:, :], in1=PS(b1)[:, 0:NH], op=ALU.mult),
                       r=["HTMP", ("PS", banks[1])], w=["HCU"])
                Sx.barrier()

            if stage >= 3:
                with ExitStack() as esh_:
                    halo_pass(esh_)

            CAND = sb("CAND", [128, 8, 128], F32, es2)
            QT = sb("QT", [128, 4, 512], BF16, es2)
            SIGN = sb("SIGN", [128, 4, 24], F32, es2)
            CBt = sb("CBt", [128, 512], BF16, es2)
            CU = sb("CU", [128, 4, 130], F32, es2)
            CY = sb("CY", [128, 512], F32, es2)
            ONT = sb("ONT", [128, 4, 512], BF16, es2)
            ONSA = sb("ONSA", [128, 512], F32, es2)
            PTb = [sb("PTb%d" % i, [128, 512], BF16, es2) for i in range(3)]
            BiasT = [sb("BiasT%d" % i, [128, 4, 128], BF16, es2) for i in range(2)]
            CBI = [sb("CBI%d" % i, [128, 4, 128], BF16, es2) for i in range(2)]
            OTs = sb("OTs", [128, 512], F32, es2)
            AFm = sb("AFm", [128, 128], F32, es2)
            FFm = sb("FFm", [128, 128], F32, es2)
            IMP = sb("IMP", [128, 128], F32, es2)
            SC1 = sb("SC1", [128, 128], F32, es2)
            SC2 = sb("SC2", [128, 128], F32, es2)
            BROW = [sb("BROW%d" % i, [128, 128], F32, es2) for i in range(2)]
            M8a = sb("M8a", [128, 8], F32, es2)
            M8b = sb("M8b", [128, 8], F32, es2)
            R4 = sb("R4", [128, 4], F32, es2)
            W4 = sb("W4", [128, 4], F32, es2)
            PTlb = sb("PTlb", [128, 2, 512], BF16, es2)
            KWt = [sb("KWt%d" % i, [128, 6 * 128], BF16, es2) for i in range(2)]
            VWt = [sb("VWt%d" % i, [128, 6, 192], BF16, es2) for i in range(2)]
            pt_i = {"i": 0, "cb": 0}
            h1v = h1s.rearrange("c p (n t) -> p c n t", t=128)
            h1keys = [("h1s", t) for t in range(NT1)]
            kwkeys = [("s_kw", t) for t in range(NT1)]
            vwkeys = [("s_vw", t) for t in range(NT1)]

            def attn_tiles(i, bl, g, kind, qrhs, qkeys, win):
                gp = slice(64 * g, 64 * g + 64)
                sgkey = ("SIGN", bl)
                if kind == 0:
                    kts = list(range(0, 2 * i + 2))
                    ob, gcol = 4, 1
                else:
                    kts = [k for k in range(2 * i - 4, 2 * i + 2) if k >= 0]
                    ob, gcol = 5, 2
                    wslot_i, kt0 = win
                LAG = 2
                pend = []

                def stage_a(n_, kt):
                    b = pst()
                    extra = []
                    if kind == 0:
                        extra.append((Gm[:, kt * 128:(kt + 1) * 128], BiasT[g][:, :, :], ["Gm", ("BiasT", g)]))
                        klhs = KsT[gp, kt * 128:(kt + 1) * 128]
                        kkeys = [("KsT", kt // 4)]
                        vl = Vs[:, kt, 64 * g:64 * g + 128]
                        vkeys = [("Vs", kt), "Vs1"]
                    else:
                        kl = kt - kt0
                        klhs = KWt[wslot_i][gp, kl * 128:(kl + 1) * 128]
                        kkeys = [("KWt", wslot_i)]
                        vl = VWt[wslot_i][:, kl, 64 * g:64 * g + 128]
                        vkeys = [("VWt", wslot_i)]
                    rel = kt - 2 * i
                    if rel == 0:
                        extra.append((identb[:, :], BAb[:, :, :], ["identb", "BA"]))
                    elif rel == 1:
                        extra.append((identb[:, :], BBb[:, :, :], ["identb", "BB"]))
                    elif kind == 1 and rel == -4:
                        extra.append((identb[:, :], W0b[:, :, :], ["identb", "W0"]))
                    elif kind == 1 and rel == -3:
                        extra.append((identb[:, :], W1b[:, :, :], ["identb", "W1"]))
                    ne = len(extra)
                    op("pe", lambda e, b=b, klhs=klhs, ne=ne: e.matmul(PS(b), lhsT=klhs, rhs=qrhs, start=True, stop=(ne == 0)),
                       r=kkeys + qkeys, w=[("PS", b)])
                    for xi, (l_, r_, ks_) in enumerate(extra):
                        op("pe", lambda e, b=b, l_=l_, r_=r_, xi=xi, ne=ne: e.matmul(PS(b), lhsT=l_, rhs=r_, start=False, stop=(xi == ne - 1)),
                           r=ks_, w=[("PS", b)])
                    pi_ = pt_i["i"]
                    pt_i["i"] = (pi_ + 1) % 3
                    op("act", lambda e, b=b, pi_=pi_: e.activation(out=PTb[pi_][:, :], in_=PS(b), func=AF.Exp), r=[("PS", b)], w=[("PTb", pi_)])
                    pend.append((n_, pi_, vl, vkeys))

                def stage_b():
                    n_, pi_, vl, vkeys = pend.pop(0)
                    nk_ = len(kts)
                    op("pe", lambda e, pi_=pi_, vl=vl, n_=n_, nk_=nk_: e.matmul(PS(ob), lhsT=vl, rhs=PTb[pi_][:, :],
                                                                              start=(n_ == 0), stop=(n_ == nk_ - 1)),
                       r=vkeys + [("PTb", pi_)], w=[("PS", ob)])

                first_b = [True]

                def do_b():
                    if first_b[0]:
                        first_b[0] = False
                        flush_fin()
                    stage_b()

                for n_, kt in enumerate(kts):
                    stage_a(n_, kt)
                    if len(pend) > LAG:
                        do_b()
                while pend:
                    do_b()
                fin_pending.append(lambda: finalize(g, bl, ob, gcol))

            fin_pending = []

            def flush_fin():
                while fin_pending:
                    fin_pending.pop(0)()

            def finalize(g, bl, ob, gcol):
                sgkey = ("SIGN", bl)
                op("act", lambda e: e.copy(out=OTs[:, :], in_=PS(ob)), r=[("PS", ob)], w=["OTs"])
                for h in range(4):
                    op("pe", lambda e, h=h: e.transpose(PS(3)[:, h * 128:(h + 1) * 128], OTs[:, h * 128:(h + 1) * 128], ident[:, :]),
                       r=["OTs", "ident"], w=[("PS", 3)])
                p3 = PS(3).rearrange("p (h n) -> p h n", h=4)
                oc, dc = 64 * g, 64 * (1 - g)
                op("dve", lambda e: e.tensor_scalar(out=R4[:, :], in0=p3[:, :, dc], scalar1=1e-30, scalar2=None, op0=ALU.max),
                   r=[("PS", 3)], w=["R4"])
                op("dve", lambda e: e.reciprocal(out=R4[:, :], in_=R4[:, :]), r=["R4"], w=["R4"])
                op("dve", lambda e: e.tensor_tensor(out=W4[:, :], in0=R4[:, :], in1=SIGN[:, bl, g * 12 + gcol:g * 12 + 12:3], op=ALU.mult),
                   r=["R4", sgkey], w=["W4"])
                for h in range(4):
                    hc_ = slice((g * 4 + h) * 64, (g * 4 + h) * 64 + 64)
                    op("dve", lambda e, h=h, hc_=hc_: e.scalar_tensor_tensor(out=ONSA[:, hc_], in0=p3[:, h, oc:oc + 64], scalar=W4[:, h:h + 1],
                                                                            in1=ONSA[:, hc_], op0=ALU.mult, op1=ALU.add),
                       r=[("PS", 3), "W4", "ONSA"], w=["ONSA"])

            for t in range(NT2 if stage >= 4 else 0):
                otok = slice(t * 512, (t + 1) * 512)
                for bl in range(4):
                    i = t * 4 + bl
                    hb_ = HT[:, :, bl * 128:(bl + 1) * 128]
                    dma("sp", hb_, h1v[:, :, 2 * i, :], r=h1keys, w=HKEYS)
                    dma("sp", CAND[:, :, :], h1v[:, :, 2 * i + 1, :], r=h1keys, w=["CAND"])
                    op("dve", lambda e, hb_=hb_: e.tensor_scalar(out=hb_, in0=hb_, scalar1=sc_t[:, 2:3], scalar2=None, op0=ALU.mult),
                       r=HKEYS + ["sc"], w=HKEYS)
                    op("dve", lambda e, hb_=hb_: e.scalar_tensor_tensor(out=hb_, in0=CAND[:, :, :], scalar=sc_t[:, 3:4], in1=hb_,
                                                                        op0=ALU.mult, op1=ALU.add), r=["CAND", "sc"] + HKEYS, w=HKEYS)
                rms_norm(UN, "UN")
                rope_tables(poso[:, otok], 512)
                for c in range(4):
                    banks = []
                    for cc_ in (C_Q + c, C_QS + c):
                        fm_proj(s_in[cc_, :, :], ("in", cc_), UN, UNK, 8, 1024, lambda b: banks.append(b))
                    rope_apply(banks[0], banks[1], QT[:, c, :], [("QT", c)], 512)
                wgt, wgk = wslot(s_gn[:, :], 192, "s_gn")
                for bl in range(4):
                    b = psd()
                    for k in range(8):
                        op("pe", lambda e, k=k, bl=bl, b=b: e.matmul(PS(b)[:, 0:24], lhsT=UN[:, k, bl * 128:(bl + 1) * 128],
                                                                    rhs=wgt[:, k * 24:(k + 1) * 24], start=(k == 0), stop=(k == 7)),
                           r=[wgk] + UNK, w=[("PS", b)])
                    op("act", lambda e, bl=bl, b=b: e.activation(out=SIGN[:, bl, :], in_=PS(b)[:, 0:24], func=AF.Sigmoid),
                       r=[("PS", b)], w=[("SIGN", bl)])
                for c4 in range(4):
                    fm_proj(s_in[C_CB + c4, :, :], ("in", C_CB + c4), UN, UNK, 8, 1024,
                            lambda b: op("act", lambda e: e.copy(out=CBt[:, :], in_=PS(b)), r=[("PS", b)], w=["CBt"]))
                    banks = []
                    fm_proj(s_in[C_CC + c4, :, :], ("in", C_CC + c4), UN, UNK, 8, 1024, lambda b: banks.append(b))
                    fm_proj(s_in[C_CX + c4, :, :], ("in", C_CX + c4), UN, UNK, 8, 1024, lambda b: banks.append(b))
                    op("act", lambda e, b0=banks[0]: e.copy(out=CY[:, :], in_=PS(b0)), r=[("PS", banks[0])], w=["CY"])
                    op("dve", lambda e, b1=banks[1]: e.tensor_tensor(
                        out=CU[:, :, 2:130], in0=CY[:, :].rearrange("p (b t) -> p b t", b=4),
                        in1=PS(b1).rearrange("p (b t) -> p b t", b=4), op=ALU.mult), r=["CY", ("PS", banks[1])], w=["CU"])
                    op("pool", lambda e, c4=c4, t=t: e.tensor_copy(out=CU[:, :, 0:2],
                                                                   in_=HCU[:, c4, t * 8:(t + 1) * 8].rearrange("p (b t) -> p b t", t=2)),
                       r=["HCU"], w=["CU"])
                    cy3 = CY[:, :].rearrange("p (b t) -> p b t", b=4)
                    op("dve", lambda e, c4=c4, cy3=cy3: e.tensor_scalar(out=cy3, in0=CU[:, :, 0:128], scalar1=cvp[:, c4, 0:1],
                                                                        scalar2=cvp[:, c4, 3:4], op0=ALU.mult, op1=ALU.add),
                       r=["CU", "cvp"], w=["CY"])
                    op("dve", lambda e, c4=c4, cy3=cy3: e.scalar_tensor_tensor(out=cy3, in0=CU[:, :, 1:129], scalar=cvp[:, c4, 1:2], in1=cy3,
                                                                               op0=ALU.mult, op1=ALU.add), r=["CU", "cvp", "CY"], w=["CY"])
                    op("dve", lambda e, c4=c4, cy3=cy3: e.scalar_tensor_tensor(out=cy3, in0=CU[:, :, 2:130], scalar=cvp[:, c4, 2:3], in1=cy3,
                                                                               op0=ALU.mult, op1=ALU.add), r=["CU", "cvp", "CY"], w=["CY"])
                    op("pool", lambda e, c4=c4: e.tensor_tensor(out=BIG[:, 16 + c4, :], in0=CY[:, :], in1=CBt[:, :], op=ALU.mult),
                       r=["CY", "CBt"], w=[("BIG", 16 + c4)])
                for which, cbase in ((0, C_GA), (1, C_GB)):
                    for d in range(8):
                        fm_proj(s_in[cbase + d, :, :], ("in", cbase + d), UN, UNK, 8, 1024,
                                lambda b, which=which, d=d: op("act", lambda e: e.activation(out=BIG[:, which * 8 + d, :], in_=PS(b), func=AF.Sigmoid),
                                                               r=[("PS", b)], w=[("BIG", which * 8 + d)]))
                for bl in range(4):
                    i = t * 4 + bl
                    sgkey = ("SIGN", bl)
                    kt0 = max(0, 2 * i - 4)
                    nwk = 2 * i + 2 - kt0
                    wsl = i % 2
                    dma("sp", KWt[wsl][:, 0:nwk * 128], s_kw[:, kt0 * 128:(2 * i + 2) * 128], r=kwkeys, w=[("KWt", wsl)])
                    dma("sp", VWt[wsl][:, 0:nwk, :], s_vw[:, kt0:2 * i + 2, :], r=vwkeys, w=[("VWt", wsl)])
                    op("dve", lambda e, i=i: e.tensor_scalar(out=AFm[:, :], in0=Gidx[:, :], scalar1=blkq[:, i:i + 1], scalar2=None, op0=ALU.is_le),
                       r=["Gidx", "blkq"], w=["AFm"])
                    op("dve", lambda e, i=i: e.tensor_scalar(out=FFm[:, :], in0=Gidx[:, :], scalar1=blkq2[:, i:i + 1], scalar2=None, op0=ALU.is_gt),
                       r=["Gidx", "blkq2"], w=["FFm"])
                    op("dve", lambda e: e.tensor_tensor(out=FFm[:, :], in0=FFm[:, :], in1=AFm[:, :], op=ALU.mult), r=["FFm", "AFm"], w=["FFm"])
                    op("dve", lambda e: e.scalar_tensor_tensor(out=FFm[:, :], in0=FFm[:, :], scalar=1e4, in1=G0[:, :], op0=ALU.mult, op1=ALU.add),
                       r=["FFm", "G0"], w=["FFm"])
                    n_ct = min(NCT, (16 * i + 14) // 128 + 1)
                    cbias = {}
                    for ct in range(n_ct):
                        thr = 128 * (2 * i - 16 * ct)
                        if thr < 2063:
                            ci = pt_i["cb"]
                            pt_i["cb"] = (ci + 1) % 2
                            op("pool", lambda e, ci=ci, thr=thr: e.tensor_scalar(
                                out=CBI[ci][:, :, :], in0=Tp[:, :].unsqueeze(1).to_broadcast([128, 4, 128]), scalar1=float(thr), scalar2=NEG,
                                op0=ALU.is_gt, op1=ALU.mult), r=["Tp"], w=[("CBI", ci)])
                            cbias[ct] = ci
                    assert len(cbias) <= 2
                    for g in range(2):
                        gp = slice(64 * g, 64 * g + 64)
                        qrhs = QT[gp, :, bl * 128:(bl + 1) * 128]
                        qkeys = [("QT", c) for c in range(4)]
                        for zb in (6, 7):
                            op("pe", lambda e, zb=zb: e.matmul(PS(zb), lhsT=zerob[:, :], rhs=BAb[:, :, :].rearrange("p a b -> p (a b)"),
                                                              start=True, stop=False), r=["zerob", "BA"], w=[("PS", 6)])
                        cpend = []

                        def c_a(ct, g=g, gp=gp, qrhs=qrhs, qkeys=qkeys):
                            b = pst()
                            hb = ct in cbias
                            op("pe", lambda e, b=b, ct=ct, hb=hb: e.matmul(PS(b), lhsT=KCT[gp, ct * 128:(ct + 1) * 128], rhs=qrhs, start=True, stop=not hb),
                               r=["KCT"] + qkeys, w=[("PS", b)])
                            if hb:
                                ci = cbias[ct]
                                op("pe", lambda e, b=b, ci=ci: e.matmul(PS(b), lhsT=identb[:, :], rhs=CBI[ci][:, :, :], start=False, stop=True),
                                   r=["identb", ("CBI", ci)], w=[("PS", b)])
                            pi_ = pt_i["i"]
                            pt_i["i"] = (pi_ + 1) % 3
                            op("act", lambda e, b=b, pi_=pi_: e.activation(out=PTb[pi_][:, :], in_=PS(b), func=AF.Exp), r=[("PS", b)], w=[("PTb", pi_)])
                            cpend.append((ct, pi_))

                        def c_b(g=g):
                            ct, pi_ = cpend.pop(0)
                            for h in range(4):
                                op("pe", lambda e, h=h, ct=ct, pi_=pi_, g=g: e.matmul(
                                    PSA[:, 6 * 512 + h * 256:6 * 512 + h * 256 + 193], lhsT=PTb[pi_][:, h * 128:(h + 1) * 128],
                                    rhs=CR[:, ct, g, 0:193], start=False, stop=(ct == n_ct - 1)),
                                   r=[("PTb", pi_), "CR"], w=[("PS", 6)])

                        for ct in range(n_ct):
                            c_a(ct)
                            if len(cpend) > 2:
                                flush_fin()
                                c_b()
                        flush_fin()
                        while cpend:
                            c_b()
                        co = PSA[:, 6 * 512:8 * 512].rearrange("p (h n) -> p h n", h=4)
                        op("dve", lambda e: e.tensor_scalar(out=R4[:, :], in0=co[:, :, 192], scalar1=1e-30, scalar2=None, op0=ALU.max),
                           r=[("PS", 6)], w=["R4"])
                        op("dve", lambda e: e.reciprocal(out=R4[:, :], in_=R4[:, :]), r=["R4"], w=["R4"])
                        op("dve", lambda e: e.tensor_scalar(out=IMP[:, :], in0=co[:, 0, 0:128], scalar1=R4[:, 0:1], scalar2=None, op0=ALU.mult),
                           r=[("PS", 6), "R4"], w=["IMP"])
                        for h in range(1, 4):
                            op("dve", lambda e, h=h: e.scalar_tensor_tensor(out=IMP[:, :], in0=co[:, h, 0:128], scalar=R4[:, h:h + 1], in1=IMP[:, :],
                                                                            op0=ALU.mult, op1=ALU.add), r=[("PS", 6), "R4", "IMP"], w=["IMP"])
                        op("dve", lambda e, g=g: e.tensor_tensor(out=W4[:, :], in0=R4[:, :], in1=SIGN[:, bl, g * 12:g * 12 + 12:3], op=ALU.mult),
                           r=["R4", sgkey], w=["W4"])
                        for h in range(4):
                            hc_ = slice((g * 4 + h) * 64, (g * 4 + h) * 64 + 64)
                            op("dve", lambda e, h=h, hc_=hc_: e.tensor_scalar(out=ONSA[:, hc_], in0=co[:, h, 128:192], scalar1=W4[:, h:h + 1],
                                                                              scalar2=None, op0=ALU.mult), r=[("PS", 6), "W4"], w=["ONSA"])
                        op("dve", lambda e: e.scalar_tensor_tensor(out=SC1[:, :], in0=IMP[:, :], scalar=1.0, in1=AFm[:, :], op0=ALU.add, op1=ALU.mult),
                           r=["IMP", "AFm"], w=["SC1"])
                        op("dve", lambda e: e.tensor_tensor(out=SC1[:, :], in0=SC1[:, :], in1=FFm[:, :], op=ALU.add), r=["SC1", "FFm"], w=["SC1"])
                        op("dve", lambda e: e.max(out=M8a[:, :], in_=SC1[:, :]), r=["SC1"], w=["M8a"])
                        op("dve", lambda e: e.match_replace(out=SC2[:, :], in_to_replace=M8a[:, :], in_values=SC1[:, :], imm_value=-1.0),
                           r=["SC1", "M8a"], w=["SC2"])
                        op("dve", lambda e: e.max(out=M8b[:, :], in_=SC2[:, :]), r=["SC2"], w=["M8b"])
                        op("dve", lambda e, g=g: e.tensor_scalar(out=BROW[g][:, :], in0=SC1[:, :], scalar1=M8b[:, 7:8], scalar2=NEG, op0=ALU.is_lt, op1=ALU.mult),
                           r=["SC1", "M8b"], w=[("BROW", g)])
                        attn_tiles(i, bl, g, 1, qrhs, qkeys, (wsl, kt0))
                    for g in range(2):
                        gp = slice(64 * g, 64 * g + 64)
                        qrhs = QT[gp, :, bl * 128:(bl + 1) * 128]
                        qkeys = [("QT", c) for c in range(4)]
                        op("pe", lambda e, g=g: e.transpose(PS(3)[:, 0:128], BROW[g][:, :], ident[:, :]), r=[("BROW", g), "ident"], w=[("PS", 3)])
                        op("dve", lambda e, g=g: e.tensor_copy(out=BiasT[g][:, :, :], in_=PS(3)[:, 0:128].unsqueeze(1).to_broadcast([128, 4, 128])),
                           r=[("PS", 3)], w=[("BiasT", g)])
                        attn_tiles(i, bl, g, 0, qrhs, qkeys, None)
                    flush_fin()
                    for c in range(4):
                        op("pe", lambda e, c=c: e.transpose(PS(3)[:, c * 128:(c + 1) * 128], ONSA[:, c * 128:(c + 1) * 128], ident[:, :]),
                           r=["ONSA", "ident"], w=[("PS", 3)])
                    op("act", lambda e, bl=bl: e.copy(out=ONT[:, :, bl * 128:(bl + 1) * 128], in_=PS(3).rearrange("p (c n) -> p c n", c=4)),
                       r=[("PS", 3)], w=[("ONT", bl)])
                ontk = [("ONT", bl) for bl in range(4)]
                for d in range(8):
                    fm_proj(s_pa[d, :, :], ("pa", d), ONT, ontk, 4, 512,
                            lambda b, d=d: op("dve", lambda e: e.tensor_tensor(out=TA[:, :], in0=BIG[:, d, :], in1=PS(b), op=ALU.mult),
                                              r=[("BIG", d), ("PS", b)], w=["TA"]))
                    fm_proj(s_pb[d, :, :], ("pb", d), BIG[:, 16:20, :], [("BIG", 16 + c) for c in range(4)], 4, 512,
                            lambda b, d=d: op("dve", lambda e: e.tensor_tensor(out=TB[:, :], in0=BIG[:, 8 + d, :], in1=PS(b), op=ALU.mult),
                                              r=[("BIG", 8 + d), ("PS", b)], w=["TB"]))
                    op("pool", lambda e, d=d: e.tensor_tensor(out=BIG[:, d, :], in0=TA[:, :], in1=TB[:, :], op=ALU.add),
                       r=["TA", "TB"], w=[("BIG", d)])
                mk = [("BIG", d) for d in range(8)]
                for d in range(8):
                    fm_proj(s_o[d, :, :], ("o", d), BIG[:, 0:8, :], mk, 8, 1024,
                            lambda b, d=d: op("dve", lambda e: e.tensor_tensor(out=HT[:, d, :], in0=HT[:, d, :], in1=PS(b), op=ALU.add),
                                              r=[(HK, d), ("PS", b)], w=[(HK, d)]))
                ffn(s_f2g, s_f2u, s_f2d, "f2")
                rms_norm(UN, "UN")
                dma("sp", TA[:, :], pT[0, :, otok], w=["TA"])
                dma("sp", CS[:, :], pT[1, :, otok], w=["CS"])
                op("pool", lambda e: e.tensor_copy(out=PTlb[:, 0, :], in_=TA[:, :]), r=["TA"], w=["PTlb"])
                op("pool", lambda e: e.tensor_copy(out=PTlb[:, 1, :], in_=CS[:, :]), r=["CS"], w=["PTlb"])
                for d in range(8):
                    fm_proj(s_pg[d, :, :], ("pg", d), UN, UNK, 8, 1024,
                            lambda b: op("act", lambda e: e.activation(out=CY[:, :], in_=PS(b), func=AF.Sigmoid), r=[("PS", b)], w=["CY"]))
                    fm_proj(s_pp[d, :, :], ("pp", d), PTlb, ["PTlb"], 2, 256,
                            lambda b: op("dve", lambda e: e.tensor_tensor(out=CY[:, :], in0=CY[:, :], in1=PS(b), op=ALU.mult),
                                         r=["CY", ("PS", b)], w=["CY"]))
                    op("dve", lambda e, d=d: e.tensor_tensor(out=HT[:, d, :], in0=HT[:, d, :], in1=CY[:, :], op=ALU.add),
                       r=[(HK, d), "CY"], w=[(HK, d)])
                rms_norm(UN, "UN", apply=False)
                for d in range(8):
                    eng = "dve" if d % 2 == 0 else "pool"
                    op(eng, lambda e, d=d: e.tensor_tensor(out=HT[:, d, :], in0=HT[:, d, :], in1=RS[:, :], op=ALU.mult),
                       r=[(HK, d), "RS"], w=[(HK, d)])
                    op(eng, lambda e, d=d: e.tensor_scalar(out=HT[:, d, :], in0=HT[:, d, :], scalar1=gn_t[:, 4, d:d + 1], scalar2=None, op0=ALU.mult),
                       r=[(HK, d), "gn"], w=[(HK, d)])
                dma("pool", outT[:, :, otok].rearrange("c p t -> p c t"), HT[:, :, :], r=HKEYS, w=[("out", t)])
            Sx.finish()
    print("[kernel] sbuf max bytes/partition", mem["max"], "instr", {k: v for k, v in Sx.cnt.items()}, "waits", Sx.nwait)
    return nc


def make_in_maps(inputs, S, B):
    f = lambda a: np.ascontiguousarray(np.asarray(a), dtype=np.float32)
    x = f(inputs["x"])
    p = f(inputs["p"])[0]
    pos = np.asarray(inputs["positions"]).astype(np.int32)
    w_in_p, w_gn = perm_win(f(inputs["w_in"])[0])
    gains = np.stack([pk(inputs[k][0]) for k in ("ffn1_norm", "mix_norm", "ffn2_norm", "ple_norm")] + [pk(inputs["final_norm"])], axis=1)
    cw = f(inputs["conv_w"])[0]
    cbv = f(inputs["conv_b"])[0]
    convp = np.stack([pk(cw[0]), pk(cw[1]), pk(cw[2]), pk(cbv)], axis=2)
    shared = {
        "w_f1g": f(inputs["ffn1_w_gate"])[0], "w_f1u": f(inputs["ffn1_w_up"])[0], "w_f1d": f(inputs["ffn1_w_down"])[0],
        "w_f2g": f(inputs["ffn2_w_gate"])[0], "w_f2u": f(inputs["ffn2_w_up"])[0], "w_f2d": f(inputs["ffn2_w_down"])[0],
        "w_in": w_in_p, "w_gn": w_gn,
        "w_pa": f(inputs["w_proj_nsa"])[0], "w_pb": f(inputs["w_proj_conv"])[0],
        "w_o": f(inputs["w_out"])[0], "w_pg": f(inputs["ple_w_gate"])[0], "w_pp": f(inputs["ple_w_proj"])[0],
        "gains": np.ascontiguousarray(gains), "convp": np.ascontiguousarray(convp),
    }
    for nm, pre in (("ck", "cmp_k"), ("cv", "cmp_v")):
        w1 = f(inputs[pre + "_w1"])[0].reshape(32, 64, 256).transpose(1, 0, 2).reshape(64, 32 * 256)
        shared[nm + "_w1"] = np.ascontiguousarray(np.concatenate([w1, w1], axis=0))
        shared[nm + "_w2"] = np.ascontiguousarray(f(inputs[pre + "_w2"])[0].reshape(2, 128, 64).transpose(1, 0, 2))
        shared[nm + "_b1"] = pk(f(inputs[pre + "_b1"])[0])
        pe = f(inputs[pre + "_pos"])[0].T
        shared[nm + "_pos"] = np.ascontiguousarray(np.concatenate([pe, pe], axis=0))
    maps = []
    owns = []
    for b in range(B):
        xTb = np.ascontiguousarray(x[b].T.reshape(8, 128, S))
        for hf in range(2):
            own = (np.arange(S // 256)[:, None] * 256 + hf * 128 + np.arange(128)[None, :]).reshape(-1)
            owns.append((b, own))
            m = dict(shared)
            m["xT"] = xTb
            m["posg"] = np.ascontiguousarray(pos[b][None, :])
            m["pT"] = np.ascontiguousarray(p[b][own].T.reshape(2, 128, own.size))
            m["poso"] = np.ascontiguousarray(pos[b][own][None, :])
            m.update(host_consts(S, hf))
            maps.append(m)
    return maps, owns


_CACHE = {}


def run(inputs, S, B):
    if S not in _CACHE:
        _CACHE[S] = build_program(S)
    nc = _CACHE[S]
    maps, owns = make_in_maps(inputs, S, B)
    res = run_bass_kernel_spmd(nc, maps, core_ids=list(range(2 * B)))
    out = np.zeros((B, S, D), np.float32)
    for c, (b, own) in enumerate(owns):
        o = np.asarray(res.results[c]["outT"]).reshape(D, own.size)
        out[b, own, :] = o.T
    return out


def kernel(**inputs):
    return run(inputs, 8192, 4)
```
